# Optimizing a Trainium2 kernel written in Bass

```python
import jax, jax.numpy as jnp
from jax import lax
import numpy as np

D_MODEL = 1024
BATCH = 16
SEQ = 256
DEPTH = 1
DEC_BATCH = 4
DEC_SEQ = 2048
PAST_LEN = 256

GRID_W = 64
RWKV_HEAD_DIM = 64
RWKV_WIDTH = D_MODEL // 2
RWKV_HEADS = RWKV_WIDTH // RWKV_HEAD_DIM
GLA_HEADS = 4
GLA_V_WIDTH = D_MODEL // 2
GLA_VAL_DIM = GLA_V_WIDTH // GLA_HEADS
GLA_QK_WIDTH = GLA_V_WIDTH // 2
GLA_KEY_DIM = GLA_QK_WIDTH // GLA_HEADS
GLA_CHUNK = 32
GLA_GATE_RANK = 16
GLA_GATE_NORMALIZER = 16.0
DECAY_LORA = 64
AAA_LORA = 64
GATE_LORA = 128
D_FF = 4 * D_MODEL
IN_WIDTH = 3 * RWKV_WIDTH + 2 * GLA_QK_WIDTH + 2 * GLA_V_WIDTH
MIX_WIDTH = RWKV_WIDTH + GLA_V_WIDTH
N_MOD = 6
RMS_EPS = 1e-6
LNX_EPS = 64e-5
GLA_NORM_EPS = 1e-5

kernel_name = 'hymba_rwkv7_gla_diffusion_step'


def _rmsnorm(x, g, eps=RMS_EPS):
    xf = x.astype(jnp.float32)
    y = xf * lax.rsqrt(jnp.mean(xf * xf, axis=-1, keepdims=True) + eps)
    return (y * g.astype(jnp.float32)).astype(x.dtype)


def _shift_seq(x):
    xp = jnp.pad(x, ((0, 0), (1, 1), (0, 0)))
    return 0.5 * (xp[:, :-2] + xp[:, 2:])


def _shift_grid(x):
    b, l, ch = x.shape
    rows = l // GRID_W
    g = jnp.pad(x.reshape(b, rows, GRID_W, ch), ((0, 0), (1, 1), (1, 1), (0, 0)))
    nb = g[:, :-2, 1:-1] + g[:, 2:, 1:-1] + g[:, 1:-1, :-2] + g[:, 1:-1, 2:]
    return (0.25 * nb).reshape(b, l, ch)


def _flip(t):
    return jnp.flip(t, axis=1)


def _rwkv7_scan(r, w, k, v, kk, a, s0):
    def step(s, inp):
        r_t, w_t, k_t, v_t, kk_t, a_t = inp
        sa = jnp.einsum('bhij,bhj->bhi', s, -kk_t)
        s = (s * w_t[:, :, None, :] + sa[..., None] * (kk_t * a_t)[:, :, None, :]
             + v_t[..., None] * k_t[:, :, None, :])
        return s, jnp.einsum('bhij,bhj->bhi', s, r_t)
    xs = tuple(jnp.swapaxes(t, 0, 1) for t in (r, w, k, v, kk, a))
    s_fin, ys = lax.scan(step, s0, xs)
    return jnp.swapaxes(ys, 0, 1), s_fin


def _gla_chunked(q, k, v, log_a, s0):
    b, l, h, _ = q.shape
    n = l // GLA_CHUNK

    def chunks(t):
        return t.reshape(b, n, GLA_CHUNK, h, t.shape[-1]).transpose(0, 1, 3, 2, 4)

    q, k, v, log_a = chunks(q), chunks(k), chunks(v), chunks(log_a)
    cum = jnp.cumsum(log_a, axis=3)
    causal = jnp.tril(jnp.ones((GLA_CHUNK, GLA_CHUNK), dtype=bool))[:, :, None]
    diff = cum[..., :, None, :] - cum[..., None, :, :]
    decay = jnp.where(causal, jnp.exp(jnp.where(causal, diff, 0.0)), 0.0)
    att = jnp.sum(q[..., :, None, :] * k[..., None, :, :] * decay, axis=-1)
    o_intra = jnp.einsum('bnhij,bnhjv->bnhiv', att, v)
    last = cum[..., -1:, :]
    q_in = q * jnp.exp(cum)
    k_in = k * jnp.exp(last - cum)
    g_last = jnp.exp(last[..., 0, :])

    def step(s, inp):
        qc, kc, vc, gc = inp
        o = jnp.einsum('bhck,bhkv->bhcv', qc, s)
        s = gc[..., None] * s + jnp.einsum('bhck,bhcv->bhkv', kc, vc)
        return s, o

    xs = tuple(jnp.moveaxis(t, 1, 0) for t in (q_in, k_in, v, g_last))
    s_fin, o_inter = lax.scan(step, s0, xs)
    o = o_intra + jnp.moveaxis(o_inter, 0, 1)
    return o.transpose(0, 1, 3, 2, 4).reshape(b, l, h, v.shape[-1]), s_fin


def _mixer(h, shift, s_rf, s_rb, s_gf, s_gb, p):
    f32 = jnp.float32
    b, l, _ = h.shape
    R = RWKV_WIDTH
    proj = h @ p['w_in']
    rkv, gq, gkey, gv, gg = jnp.split(
        proj, [3 * R, 3 * R + GLA_QK_WIDTH, 3 * R + 2 * GLA_QK_WIDTH,
               3 * R + 2 * GLA_QK_WIDTH + GLA_V_WIDTH], axis=-1)

    rkv = rkv + p['mu_rkv'] * (shift(rkv) - rkv)
    r, k, v = jnp.split(rkv, 3, axis=-1)
    dh = shift(h) - h
    xw = h + p['mu_wag'][0] * dh
    xa = h + p['mu_wag'][1] * dh
    xg = h + p['mu_wag'][2] * dh

    def heads(t):
        return t.astype(f32).reshape(b, l, RWKV_HEADS, RWKV_HEAD_DIM)

    rh, kh, vh = heads(r), heads(k), heads(v)
    kk = kh * p['k_k'].astype(f32).reshape(RWKV_HEADS, RWKV_HEAD_DIM)
    kk = kk * lax.rsqrt(jnp.maximum(jnp.sum(kk * kk, axis=-1, keepdims=True), 1e-12))
    k_a = p['k_a'].astype(f32).reshape(RWKV_HEADS, RWKV_HEAD_DIM)

    def dir_inputs(d):
        z = (p['w0'][d] + jnp.tanh(xw @ p['w1'][d]) @ p['w2'][d]).astype(f32)
        wh = heads(jnp.exp(-jnp.exp(-jax.nn.softplus(-z) - 0.5)))
        ah = heads(jax.nn.sigmoid(p['a0'][d] + (xa @ p['a1'][d]) @ p['a2'][d]))
        kd = kh * (1.0 + (ah - 1.0) * k_a)
        return wh, ah, kd

    w_f, a_f, kd_f = dir_inputs(0)
    w_b, a_b, kd_b = dir_inputs(1)
    y_f, srf = _rwkv7_scan(rh, w_f, kd_f, vh, kk, a_f, s_rf.astype(f32))
    y_b, srb = _rwkv7_scan(_flip(rh), _flip(w_b), _flip(kd_b), _flip(vh), _flip(kk), _flip(a_b),
                           s_rb.astype(f32))
    y = y_f + _flip(y_b)
    mu = jnp.mean(y, axis=-1, keepdims=True)
    var = jnp.mean(jnp.square(y - mu), axis=-1, keepdims=True)
    yn = ((y - mu) * lax.rsqrt(var + LNX_EPS)).reshape(b, l, R)
    yn = yn * p['lnx_g'].astype(f32) + p['lnx_b'].astype(f32)
    bonus = (jnp.sum(rh * (kd_f + kd_b) * p['r_k'].astype(f32), axis=-1, keepdims=True) * vh).reshape(b, l, R)
    gate = (jax.nn.sigmoid(xg @ p['g1']) @ p['g2']).astype(f32)
    rwkv_out = (yn + bonus) * gate

    def gheads(t, dim):
        return t.astype(f32).reshape(b, l, GLA_HEADS, dim)

    q = gheads(gq, GLA_KEY_DIM) * (GLA_KEY_DIM ** -0.5)
    kg = gheads(gkey, GLA_KEY_DIM)
    vg = gheads(gv, GLA_VAL_DIM)

    def log_gate(d):
        logits = ((h @ p['gk1'][d]) @ p['gk2'][d] + p['gk_b'][d]).astype(f32)
        return gheads(jax.nn.log_sigmoid(logits), GLA_KEY_DIM) / GLA_GATE_NORMALIZER

    o_f, sgf = _gla_chunked(q, kg, vg, log_gate(0), s_gf.astype(f32))
    o_b, sgb = _gla_chunked(_flip(q), _flip(kg), _flip(vg), _flip(log_gate(1)), s_gb.astype(f32))
    o = o_f + _flip(o_b)
    o = (o * lax.rsqrt(jnp.mean(o * o, axis=-1, keepdims=True) + GLA_NORM_EPS)
         * p['gla_norm_g'].astype(f32) * jax.nn.silu(gheads(gg, GLA_VAL_DIM)))
    gla_out = o.reshape(b, l, GLA_V_WIDTH)

    out = jnp.concatenate([rwkv_out, gla_out], axis=-1).astype(h.dtype) @ p['w_out']
    return out, (srf, srb, sgf, sgb)


def _block(x, mod, shift, states, p):
    sh1, sc1, gt1, sh2, sc2, gt2 = jnp.split(mod, N_MOD, axis=-1)
    h = _rmsnorm(x, p['norm1_g']) * (1.0 + sc1) + sh1
    o, new_states = _mixer(h, shift, states[0], states[1], states[2], states[3], p)
    x = x + gt1 * o
    h = _rmsnorm(x, p['norm2_g']) * (1.0 + sc2) + sh2
    f = jnp.square(jax.nn.relu(h @ p['mlp_w1'])) @ p['mlp_w2']
    x = x + gt2 * f
    return x, new_states


def setup_inputs(seed: int = 0) -> dict:
    key = jax.random.key(seed)
    ks = iter(jax.random.split(key, 48))

    def nrm(shape, scale):
        return scale * jax.random.normal(next(ks), shape, jnp.float32)

    def unif(shape, lo, hi):
        return jax.random.uniform(next(ks), shape, jnp.float32, lo, hi)

    D = D_MODEL
    R = RWKV_WIDTH
    L = DEPTH
    return {
        'x_prompt': nrm((BATCH, SEQ, D), 1.0),
        'x_sample': nrm((DEC_BATCH, DEC_SEQ, D), 1.0),
        'c': nrm((DEC_BATCH, D), 1.0),
        'state_rwkv_fwd': nrm((DEC_BATCH, L, RWKV_HEADS, RWKV_HEAD_DIM, RWKV_HEAD_DIM), 0.5),
        'state_rwkv_bwd': nrm((DEC_BATCH, L, RWKV_HEADS, RWKV_HEAD_DIM, RWKV_HEAD_DIM), 0.5),
        'state_gla_fwd': nrm((DEC_BATCH, L, GLA_HEADS, GLA_KEY_DIM, GLA_VAL_DIM), 0.5),
        'state_gla_bwd': nrm((DEC_BATCH, L, GLA_HEADS, GLA_KEY_DIM, GLA_VAL_DIM), 0.5),
        'c_ctx': nrm((D,), 1.0),
        'ada_w': nrm((L, D, N_MOD * D), 0.3 * D ** -0.5),
        'ada_b': nrm((L, N_MOD * D), 0.02),
        'norm1_g': 1.0 + nrm((L, D), 0.01),
        'norm2_g': 1.0 + nrm((L, D), 0.01),
        'w_in': nrm((L, D, IN_WIDTH), D ** -0.5),
        'rwkv_mu_rkv': unif((L, 3 * R), 0.2, 0.8),
        'rwkv_mu_wag': unif((L, 3, D), 0.2, 0.8),
        'rwkv_w0': nrm((L, 2, R), 0.5) - 0.5,
        'rwkv_w1': nrm((L, 2, D, DECAY_LORA), D ** -0.5),
        'rwkv_w2': nrm((L, 2, DECAY_LORA, R), 0.3 * DECAY_LORA ** -0.5),
        'rwkv_a0': nrm((L, 2, R), 0.1),
        'rwkv_a1': nrm((L, 2, D, AAA_LORA), D ** -0.5),
        'rwkv_a2': nrm((L, 2, AAA_LORA, R), 0.3 * AAA_LORA ** -0.5),
        'rwkv_g1': nrm((L, D, GATE_LORA), D ** -0.5),
        'rwkv_g2': nrm((L, GATE_LORA, R), GATE_LORA ** -0.5),
        'rwkv_k_k': 0.85 + nrm((L, R), 0.05),
        'rwkv_k_a': 1.0 + nrm((L, R), 0.05),
        'rwkv_r_k': nrm((L, RWKV_HEADS, RWKV_HEAD_DIM), 0.1),
        'rwkv_lnx_g': 1.0 + nrm((L, R), 0.01),
        'rwkv_lnx_b': nrm((L, R), 0.01),
        'gla_gk1': nrm((L, 2, D, GLA_GATE_RANK), D ** -0.5),
        'gla_gk2': nrm((L, 2, GLA_GATE_RANK, GLA_QK_WIDTH), GLA_GATE_RANK ** -0.5),
        'gla_gk_b': nrm((L, 2, GLA_QK_WIDTH), 0.5) + 1.0,
        'gla_norm_g': 1.0 + nrm((L, GLA_VAL_DIM), 0.01),
        'w_out': nrm((L, MIX_WIDTH, D), MIX_WIDTH ** -0.5),
        'mlp_w1': nrm((L, D, D_FF), D ** -0.5),
        'mlp_w2': nrm((L, D_FF, D), D_FF ** -0.5),
        'final_norm_g': 1.0 + nrm((D,), 0.01),
    }


def reference(x_prompt, x_sample, c, state_rwkv_fwd, state_rwkv_bwd, state_gla_fwd, state_gla_bwd,
              c_ctx, ada_w, ada_b, norm1_g, norm2_g, w_in, rwkv_mu_rkv, rwkv_mu_wag,
              rwkv_w0, rwkv_w1, rwkv_w2, rwkv_a0, rwkv_a1, rwkv_a2, rwkv_g1, rwkv_g2,
              rwkv_k_k, rwkv_k_a, rwkv_r_k, rwkv_lnx_g, rwkv_lnx_b,
              gla_gk1, gla_gk2, gla_gk_b, gla_norm_g, w_out, mlp_w1, mlp_w2, final_norm_g):
    f32 = jnp.float32
    nb = x_prompt.shape[0]
    zr = jnp.zeros((nb, RWKV_HEADS, RWKV_HEAD_DIM, RWKV_HEAD_DIM), f32)
    zg = jnp.zeros((nb, GLA_HEADS, GLA_KEY_DIM, GLA_VAL_DIM), f32)
    ctx_cond = jax.nn.silu(c_ctx)
    lat_cond = jax.nn.silu(c)
    xp, xs = x_prompt, x_sample
    new_rf, new_rb, new_gf, new_gb = [], [], [], []
    for layer in range(DEPTH):
        p = {
            'norm1_g': norm1_g[layer], 'norm2_g': norm2_g[layer], 'w_in': w_in[layer],
            'mu_rkv': rwkv_mu_rkv[layer], 'mu_wag': rwkv_mu_wag[layer],
            'w0': rwkv_w0[layer], 'w1': rwkv_w1[layer], 'w2': rwkv_w2[layer],
            'a0': rwkv_a0[layer], 'a1': rwkv_a1[layer], 'a2': rwkv_a2[layer],
            'g1': rwkv_g1[layer], 'g2': rwkv_g2[layer],
            'k_k': rwkv_k_k[layer], 'k_a': rwkv_k_a[layer], 'r_k': rwkv_r_k[layer],
            'lnx_g': rwkv_lnx_g[layer], 'lnx_b': rwkv_lnx_b[layer],
            'gk1': gla_gk1[layer], 'gk2': gla_gk2[layer], 'gk_b': gla_gk_b[layer],
            'gla_norm_g': gla_norm_g[layer], 'w_out': w_out[layer],
            'mlp_w1': mlp_w1[layer], 'mlp_w2': mlp_w2[layer],
        }
        mod_ctx = (ctx_cond @ ada_w[layer] + ada_b[layer])[None, None, :]
        mod_lat = (lat_cond @ ada_w[layer] + ada_b[layer])[:, None, :]
        xp, (srf, srb, sgf, sgb) = _block(xp, mod_ctx, _shift_seq, (zr, zr, zg, zg), p)
        new_rf.append(srf)
        new_rb.append(srb)
        new_gf.append(sgf)
        new_gb.append(sgb)
        xs, _ = _block(xs, mod_lat, _shift_grid,
                       (state_rwkv_fwd[:, layer], state_rwkv_bwd[:, layer],
                        state_gla_fwd[:, layer], state_gla_bwd[:, layer]), p)
    y_prompt = _rmsnorm(xp, final_norm_g)
    y_sample = _rmsnorm(xs, final_norm_g)
    new_state_rwkv_fwd = jnp.stack(new_rf, axis=1).astype(x_prompt.dtype)
    new_state_rwkv_bwd = jnp.stack(new_rb, axis=1).astype(x_prompt.dtype)
    new_state_gla_fwd = jnp.stack(new_gf, axis=1).astype(x_prompt.dtype)
    new_state_gla_bwd = jnp.stack(new_gb, axis=1).astype(x_prompt.dtype)
    return (y_prompt, y_sample, new_state_rwkv_fwd, new_state_rwkv_bwd, new_state_gla_fwd, new_state_gla_bwd)
```

```python
import numpy as np
from contextlib import ExitStack
import concourse.bass as bass
import concourse.mybir as mybir
from concourse.bass_utils import run_bass_kernel_spmd

F32 = mybir.dt.float32
BF16 = mybir.dt.bfloat16
ALU = mybir.AluOpType
AF = mybir.ActivationFunctionType
AX = mybir.AxisListType

_MARKS = []
_ENVD = {}
CW = 0.6065306597126334


class K:
    N_DMA_SEMS = 24

    def __init__(self, nc, same_engine_sync=True):
        self.nc = nc
        self.es = ExitStack()
        self.ops = {e: [] for e in ("pe", "act", "dve", "pool", "sp")}
        self.recs = {}
        self.same_engine_sync = same_engine_sync
        self.dma_cnt = [0] * self.N_DMA_SEMS
        self.dma_rr = 0
        self.out_events = []
        self.needed = set()
        self.waited = {e: {} for e in self.ops}

    def sb(self, name, shape, dtype):
        return self.es.enter_context(self.nc.sbuf_tensor(name, list(shape), dtype))

    def ps(self, name, shape, dtype=F32):
        return self.es.enter_context(self.nc.psum_tensor(name, list(shape), dtype))

    @staticmethod
    def _box(ap):
        if "PSUM" in str(ap.space).upper():
            return (0, 128, 0, 1 << 30)
        a = ap.ap
        pstep, pcnt = a[0]
        off = int(ap.offset)
        if pstep == 0:
            p0, f0 = 0, off
            pcnt = 1
        else:
            p0 = off // pstep
            f0 = off - p0 * pstep
        lo = 0
        hi = 0
        for st, cn in a[1:]:
            if st >= 0:
                hi += st * (cn - 1)
            else:
                lo += st * (cn - 1)
        return (p0, p0 + pcnt, f0 + lo, f0 + hi + 1)

    @staticmethod
    def _ovl(a, b):
        return a[0] < b[1] and b[0] < a[1] and a[2] < b[3] and b[2] < a[3]

    @staticmethod
    def _covers(a, b):
        return a[0] <= b[0] and a[1] >= b[1] and a[2] <= b[2] and a[3] >= b[3]

    def _track(self, reads, writes, ev):
        deps = {}
        items = []
        for ap in reads:
            if "DRAM" in str(ap.space).upper():
                continue
            items.append((ap.name, self._box(ap), False))
        for ap in writes:
            if "DRAM" in str(ap.space).upper():
                continue
            items.append((ap.name, self._box(ap), True))
        for name, box, isw in items:
            lst = self.recs.setdefault(name, [])
            for (b, w, e) in lst:
                if (w or isw) and self._ovl(b, box):
                    if e[1] > deps.get(e[0], -1):
                        deps[e[0]] = e[1]
        for name, box, isw in items:
            lst = self.recs[name]
            if isw:
                lst[:] = [r for r in lst if not self._covers(box, r[0])]
                lst.append((box, True, ev))
            else:
                lst[:] = [r for r in lst if not (r[1] is False and r[2][0] == ev[0] and self._covers(box, r[0]))]
                lst.append((box, False, ev))
        return deps

    def op(self, eng, fn, reads=(), writes=()):
        idx = len(self.ops[eng])
        ev = (eng, idx)
        deps = self._track(reads, writes, ev)
        waits = []
        for k, v in deps.items():
            if k == eng and (eng == "pe" or not self.same_engine_sync):
                continue
            if self.waited[eng].get(k, -1) >= v:
                continue
            self.waited[eng][k] = v
            waits.append((k, v))
            self.needed.add((k, v))
        self.ops[eng].append(dict(kind="op", fn=fn, waits=waits, desc=(writes[0].name if writes else "?") + "<-" + ",".join(sorted(set(r.name for r in reads)))))
        return ev

    def dma(self, out, in_, queue="sp", is_output=False, **kw):
        k = self.dma_rr
        self.dma_rr = (self.dma_rr + 1) % self.N_DMA_SEMS
        self.dma_cnt[k] += 1
        semname = "dma%d" % k
        ev = (semname, self.dma_cnt[k])
        deps = self._track([in_], [out], ev)
        if self.dma_cnt[k] > 1:
            deps[semname] = max(deps.get(semname, -1), self.dma_cnt[k] - 1)
        waits = []
        for kk, v in deps.items():
            if self.waited[queue].get(kk, -1) >= v:
                continue
            self.waited[queue][kk] = v
            waits.append((kk, v))
            self.needed.add((kk, v))
        self.ops[queue].append(dict(kind="dma", out=out, in_=in_, waits=waits, sem=semname, kw=kw))
        if is_output:
            self.out_events.append(ev)
        return ev

    def emit(self):
        nc = self.nc
        fin = []
        for ev in self.out_events:
            if self.waited["sp"].get(ev[0], -1) >= ev[1]:
                continue
            self.waited["sp"][ev[0]] = ev[1]
            fin.append(ev)
        self.ops["sp"].append(dict(kind="fin", waits=fin))
        val = {}
        for e, lst in self.ops.items():
            c = 0
            for i, o in enumerate(lst):
                if (e, i) in self.needed:
                    c += 1
                    val[(e, i)] = c
        sems = {}
        for e in ("pe", "act", "dve", "pool"):
            sems[e] = self.es.enter_context(nc.semaphore("s_" + e))
        for k in range(self.N_DMA_SEMS):
            sems["dma%d" % k] = self.es.enter_context(nc.semaphore("s_dma%d" % k))

        def wv(k, v):
            if k.startswith("dma"):
                return 16 * v
            return val[(k, v)]

        def run(ename, eng):
            dm = getattr(self, "dummy", None)
            for i, o in enumerate(self.ops[ename]):
                if ename == "pe" and dm is not None and o["waits"] and any(not k.startswith("dma") for (k, v) in o["waits"]):
                    for _ in range(dm[3]):
                        eng.matmul(dm[0], dm[1], dm[2], start=True, stop=True)
                for (k, v) in o["waits"]:
                    eng.wait_ge(sems[k], wv(k, v))
                if o["kind"] == "op":
                    ins = o["fn"](eng)
                    if (ename, i) in self.needed:
                        ins.then_inc(sems[ename], 1)
                elif o["kind"] == "dma":
                    eng.dma_start(out=o["out"], in_=o["in_"], **o["kw"]).then_inc(sems[o["sem"]], 16)

        with nc.Block() as block:
            @block.tensor
            def _(e):
                run("pe", e)

            @block.scalar
            def _(e):
                run("act", e)

            @block.vector
            def _(e):
                run("dve", e)

            @block.gpsimd
            def _(e):
                run("pool", e)

            @block.sync
            def _(e):
                run("sp", e)
        self.es.close()
        return {e: len(l) for e, l in self.ops.items()}


def bc(ap, n):
    return bass.AP(ap.tensor, ap.offset, list(ap.ap) + [(0, n)])


def rev(ap):
    a = list(ap.ap)
    st, cn = a[-1]
    return bass.AP(ap.tensor, ap.offset + (cn - 1) * st, a[:-1] + [(-st, cn)])


V_N1G, V_N2G, V_FNG, V_MUW, V_MUA, V_MUG = 0, 8, 16, 24, 32, 40
V_W0, V_A0, V_KK, V_KA, V_RK, V_GKB, V_ADB = 48, 56, 64, 68, 72, 76, 80
NV = 80 + 48
R_MU, R_LNG, R_LNB, R_GNG = 0, 1536, 2048, 2560
NR = 3072


def build(dbg=False):
    nc = bass.Bass("TRN2", target_bir_lowering=False)
    import os as _os
    k = K(nc, same_engine_sync=(_ENVD.get("KSES", "1") == "1"))
    DT = lambda n, s, kind="ExternalInput": nc.dram_tensor(n, list(s), F32, kind=kind).ap()
    xT = DT("xT", [8, 128, 2560])
    cond = DT("cond", [128, 8, 2])
    ada_w = DT("ada_w", [1024, 6144])
    vp = DT("vp", [128, NV])
    rp = DT("rp", [1, NR])
    w_in = DT("w_in", [1024, 3072])
    w1 = DT("w1", [2, 1024, 64]); w2 = DT("w2", [2, 64, 512])
    a1 = DT("a1", [2, 1024, 64]); a2 = DT("a2", [2, 64, 512])
    g1 = DT("g1", [1024, 128]); g2 = DT("g2", [128, 512])
    gk1 = DT("gk1", [2, 1024, 16]); gk2 = DT("gk2", [2, 16, 256])
    w_out = DT("w_out", [1024, 1024]); m1 = DT("m1", [1024, 4096]); m2 = DT("m2", [4096, 1024])
    st_r = DT("st_r", [2, 8, 64, 64]); st_g = DT("st_g", [2, 4, 64, 128])
    yT = DT("yT", [8, 128, 1536], "ExternalOutput")
    ns_r = DT("ns_r", [2, 2, 8, 64, 64], "ExternalOutput")
    ns_g = DT("ns_g", [2, 2, 4, 64, 128], "ExternalOutput")
    dbg_out = {}

    def TT(eng, out, a, b, op):
        k.op(eng, lambda e: e.tensor_tensor(out, a, b, op), reads=[a, b], writes=[out])

    def TS(eng, out, a, s1, s2, op0, op1=None):
        rd = [a] + [s for s in (s1, s2) if not isinstance(s, (int, float, type(None)))]
        if op1 is None:
            k.op(eng, lambda e: e.tensor_scalar(out, a, s1, None, op0), reads=rd, writes=[out])
        else:
            k.op(eng, lambda e: e.tensor_scalar(out, a, s1, s2, op0, op1), reads=rd, writes=[out])

    def STT(eng, out, a, s, b, op0, op1):
        rd = [a, b] + ([] if isinstance(s, (int, float)) else [s])
        k.op("dve", lambda e: e.scalar_tensor_tensor(out, a, s, b, op0, op1), reads=rd, writes=[out])

    def ACT(out, a, func, bias=None, scale=None):
        rd = [a]
        kw = {}
        if bias is not None:
            kw["bias"] = bias
            if not isinstance(bias, (int, float)):
                rd.append(bias)
        if scale is not None:
            kw["scale"] = scale
            if not isinstance(scale, (int, float)):
                rd.append(scale)
        k.op("act", lambda e: e.activation(out, a, func, **kw), reads=rd, writes=[out])

    def CP(eng, out, a):
        if eng == "act":
            k.op("act", lambda e: e.copy(out, a), reads=[a], writes=[out])
        else:
            k.op(eng, lambda e: e.tensor_copy(out, a), reads=[a], writes=[out])

    def MM(out, lhsT, rhs, start=True, stop=True):
        k.op("pe", lambda e: e.matmul(out, lhsT, rhs, start=start, stop=stop), reads=[lhsT, rhs], writes=[out])

    def TR(out, a, idn):
        k.op("pe", lambda e: e.transpose(out, a, idn), reads=[a, idn], writes=[out])

    def MS(eng, out, v):
        k.op(eng, lambda e: e.memset(out, v), writes=[out])

    def ASEL(out, pattern, cmp, base, cm):
        k.op("pool", lambda e: e.affine_select(out, out, pattern=pattern, compare_op=cmp, fill=0.0, base=base,
                                               channel_multiplier=cm), reads=[out], writes=[out])

    def SCAN(out, m, x):
        k.op("dve", lambda e: e.tensor_tensor_scan(out, m, x, 0.0, ALU.mult, ALU.add), reads=[m, x], writes=[out])

    def RSUM(eng, out, a):
        k.op(eng, lambda e: e.reduce_sum(out, a, AX.X), reads=[a], writes=[out])

    def RSQ(out, a, scale, eps_ap):
        ACT(out, a, AF.Sqrt, bias=eps_ap, scale=scale)
        k.op("dve", lambda e: e.reciprocal(out, out), reads=[out], writes=[out])

    def bcm(ap2d, n):
        return bass.AP(ap2d.tensor, ap2d.offset, [ap2d.ap[0], (0, n)] + list(ap2d.ap[1:]))

    NDUM = int(_ENVD.get("KDUM", "0"))
    NPF = 5 if NDUM else 6
    pf = [k.ps("pf%d" % i, [128, 512], F32) for i in range(NPF)]
    if NDUM:
        pdum = k.ps("pdum", [128, 512], F32)
    pb = [k.ps("pb%d" % i, [128, 1024], BF16) for i in range(2)]
    cnt = {"pf": 0, "pb": 0, "alt": 0, "stg": 0}

    def PF():
        cnt["pf"] += 1
        return pf[cnt["pf"] % NPF]

    def PB():
        cnt["pb"] += 1
        return pb[cnt["pb"] % 2]

    def ALT(a="dve", b="pool"):
        cnt["alt"] += 1
        return a if cnt["alt"] % 2 else b

    def STG():
        cnt["stg"] += 1
        return cnt["stg"] % 2

    import os

    class _Stop(Exception):
        pass

    def MARK(name):
        _MARKS.append((name, len(k.ops["pe"])))

    def CK(tag):
        if _ENVD.get("KSTOP") == tag:
            raise _Stop()

    def body():
        ident_b = k.sb("ident_b", [128, 128], BF16)
        ones_b = k.sb("ones_b", [128, 128], BF16)
        blk_b = k.sb("blk_b", [128, 128], BF16)
        blk256_f = k.sb("blk256_f", [128, 256], BF16)
        blkind_b = k.sb("blkind_b", [128, 2], BF16)
        mask2 = k.sb("mask2", [128, 2, 256], BF16)
        maskQ = k.sb("maskQ", [128, 2, 128], BF16)
        maskI = k.sb("maskI", [128, 2, 128], BF16)
        scm = k.sb("scm", [128, 257], F32)

        MS("pool", ident_b[:], 1.0)
        ASEL(ident_b[:], [[1, 128]], ALU.is_equal, 0, -1)
        MS("pool", ones_b[:], 1.0)
        if NDUM:
            k.dummy = (pdum[:, 0:int(_ENVD.get("KDUMN", "128"))], ident_b[:], ones_b[:, 0:int(_ENVD.get("KDUMN", "128"))], NDUM)
        MS("pool", blk_b[:], 0.0)
        MS("pool", blk_b[0:64, 0:64], 1.0)
        MS("pool", blk_b[64:128, 64:128], 1.0)
        MS("pool", blk256_f[:], 0.0)
        MS("pool", blk256_f[0:64, 0:128], 1.0)
        MS("pool", blk256_f[64:128, 128:256], 1.0)
        MS("pool", blkind_b[:], 0.0)
        MS("pool", blkind_b[0:64, 0:1], 1.0)
        MS("pool", blkind_b[64:128, 1:2], 1.0)
        MS("pool", mask2[:], 1.0)
        MS("pool", maskQ[:], 1.0)
        MS("pool", maskI[:], 1.0)
        ASEL(mask2[:, 0, 0:128], [[1, 128]], ALU.is_ge, -1, -1)
        ASEL(mask2[:, 0, 128:256], [[1, 128]], ALU.is_ge, 0, -1)
        ASEL(mask2[:, 1, 0:128], [[-1, 128]], ALU.is_ge, -1, 1)
        ASEL(mask2[:, 1, 128:256], [[-1, 128]], ALU.is_ge, 0, 1)
        ASEL(maskQ[:, 0, :], [[-1, 128]], ALU.is_ge, -1, 1)
        ASEL(maskQ[:, 1, :], [[1, 128]], ALU.is_ge, -1, -1)
        ASEL(maskI[:, 0, :], [[1, 128]], ALU.is_ge, 0, -1)
        ASEL(maskI[:, 1, :], [[-1, 128]], ALU.is_ge, 0, 1)
        MS("pool", scm[:], 1.0)
        for c_ in (0, 128, 256):
            MS("pool", scm[:, c_:c_ + 1], 0.0)

        epsv = k.sb("epsv", [128, 4], F32)
        MS("pool", epsv[:, 0:1], 1e-6)
        MS("pool", epsv[:, 1:2], 64e-5)
        MS("pool", epsv[:, 2:3], 1e-5)
        MS("pool", epsv[:, 3:4], 0.0)
        CK("A")
        vps = k.sb("vps", [128, NV], F32)
        k.dma(vps[:], vp)
        conds = k.sb("conds", [128, 8, 2], F32)
        k.dma(conds[:], cond)
        omka = k.sb("omka", [128, 4], F32)
        TS("dve", omka[:], vps[:, V_KA:V_KA + 4], -1.0, 1.0, ALU.mult, ALU.add)

        stg = [k.sb("stg%d" % i, [128, 8, 512], F32) for i in range(2)]
        wbf = [k.sb("wbf%d" % i, [128, 8, 512], BF16) for i in range(2)]
        xg = stg[0]
        sqb = wbf[1]

        CK("B")
        csil = k.sb("csil", [128, 8, 2], F32)
        ACT(csil[:], conds[:], AF.Sigmoid)
        TT("dve", csil[:], csil[:], conds[:], ALU.mult)
        mod = k.sb("mod", [128, 48, 2], F32)
        pm = PF()
        for blk in range(12):
            si = STG()
            k.dma(stg[si][:], ada_w[:, blk * 512:(blk + 1) * 512].rearrange("(k p) n -> p k n", p=128))
            for cc in range(4):
                ch = blk * 4 + cc
                for kc in range(8):
                    MM(pm[:, 2 * ch:2 * ch + 2], stg[si][:, kc, cc * 128:(cc + 1) * 128], csil[:, kc, :],
                       start=(kc == 0), stop=(kc == 7))
        for v_ in range(2):
            TT("dve", mod[:, :, v_], pm[:, 0:96].rearrange("p (c v) -> p c v", v=2)[:, :, v_], vps[:, V_ADB:V_ADB + 48], ALU.add)
        CK("C")
        A1 = k.sb("A1", [128, 8, 2], F32)
        A2 = k.sb("A2", [128, 8, 2], F32)
        for v_ in range(2):
            STT("dve", A1[:, :, v_], mod[:, 8:16, v_], 1.0, vps[:, V_N1G:V_N1G + 8], ALU.add, ALU.mult)
            STT("dve", A2[:, :, v_], mod[:, 32:40, v_], 1.0, vps[:, V_N2G:V_N2G + 8], ALU.add, ALU.mult)
        SH1, GT1, SH2, GT2 = 0, 16, 24, 40

        HW = 1152
        hT = k.sb("hT", [128, 8, HW], BF16)
        dhT = k.sb("dhT", [128, 8, 1024], BF16)
        tanhT = k.sb("tanhT", [128, 1024], BF16)
        a1T = k.sb("a1T", [128, 1024], BF16)
        sigT = k.sb("sigT", [128, 1024], BF16)
        gk1T = k.sb("gk1T", [64, 1024], BF16)
        mixtok = k.sb("mixtok", [128, 12, 1024], BF16)
        store = k.sb("store", [128, 16, 384], BF16)
        gam = k.sb("gam", [128, 2, 8], F32)
        vtok = k.sb("vtok", [128, 8, 256], BF16)
        prodb = k.sb("prodb", [128, 8, 128], BF16)
        Tb = k.sb("Tb", [128, 256], BF16)
        Whp = k.sb("Whp", [128, 8, 2, 448], BF16)
        L1 = Whp
        W2b = k.sb("W2b", [128, 2, 128], BF16)
        G2b = k.sb("G2b", [128, 512], BF16)
        GK2b = k.sb("GK2b", [64, 2, 128], BF16)
        arena = k.sb("arena", [128, 8192], F32)
        _ao = [0]

        def carve(n):
            a = arena[:, _ao[0]:_ao[0] + n]
            _ao[0] += n
            return a

        T = {n: carve(256) for n in ("r", "k", "kq", "kk", "sig", "a", "kd", "kd0", "be", "S", "D", "E1", "E2", "E3", "x1", "x2")}
        yacc = carve(2048).rearrange("p (a b) -> p a b", b=256)
        rows = carve(768)
        Tf = carve(256)
        Tmid_r = carve(512).rearrange("p (a b) -> p a b", b=128)
        Tmid_g = carve(512).rearrange("p (a b) -> p a b", b=256)
        assert _ao[0] <= 8192
        x1 = arena[:, 0:6144].rearrange("p (a b) -> p a b", b=768)
        vb = k.sb("vb", [128, 256], BF16)
        AR = [k.sb("AR%d" % i, [128, 2, 2, 128], BF16) for i in range(2)]
        BK = [k.sb("BK%d" % i, [128, 2, 2, 128], BF16) for i in range(2)]
        toks = [k.sb("toks%d" % i, [128, 2, 3, 128], BF16) for i in range(2)]
        XK = [k.sb("XK%d" % i, [128, 2, 2, 256], BF16) for i in range(2)]
        QP = [k.sb("QP%d" % i, [128, 2, 2, 128], BF16) for i in range(6)]
        RR = [k.sb("RR%d" % i, [128, 2, 128], BF16) for i in range(4)]
        ZR = [k.sb("ZR%d" % i, [128, 2, 64], BF16) for i in range(2)]
        ZS = [k.sb("ZS%d" % i, [128, 2, 2, 64], BF16) for i in range(2)]
        sm = k.sb("sm", [128, 16], F32)
        rstd = arena[:, 6144:6656]
        ntmp = arena[:, 6656:7168]

        def load_l1():
            k.dma(rows[:, 640:768], rp[:, R_GNG:R_GNG + 128].partition_broadcast(128))
            si = STG()
            for d in range(2):
                k.dma(stg[si][:, :, d * 64:(d + 1) * 64], w1[d].rearrange("(k p) n -> p k n", p=128))
                k.dma(stg[si][:, :, 128 + d * 64:128 + (d + 1) * 64], a1[d].rearrange("(k p) n -> p k n", p=128))
            k.dma(stg[si][:, :, 256:384], g1.rearrange("(k p) n -> p k n", p=128))
            MS("pool", stg[si][:, :, 384:448], 0.0)
            for d in range(2):
                k.dma(stg[si][:, :, 384 + 32 * d:384 + 32 * d + 16], gk1[d].rearrange("(k p) n -> p k n", p=128))
            for kc in range(8):
                CP("pool", L1[:, kc, 0, :], stg[si][:, kc, 0:448])
                for j, vo in enumerate((V_MUW, V_MUA, V_MUG)):
                    TS("dve", L1[:, kc, 1, j * 128:(j + 1) * 128], stg[si][:, kc, j * 128:(j + 1) * 128],
                       vps[:, vo + kc:vo + kc + 1], None, ALU.mult)

        si = STG()
        k.dma(stg[si][:, 0, :], g2)
        CP("pool", G2b[:], stg[si][:, 0, :])

        def mixer(P):
            nt = P["nt"]
            pc = P["pc"]
            mv = P["mv"]
            x0 = P["x0"]
            own = P["own"]
            g0 = P["g0"]
            dirs = P["dirs"]
            MARK(P["name"] + ":norm")
            load_l1()
            for (a_, b_) in P["pads"]:
                MS("pool", hT[:, :, a_:b_], 0.0)
            for (pcol, xcol, n) in P["ngroups"]:
                k.dma(xg[:, :, 0:n], xT[:, :, xcol:xcol + n].rearrange("k p t -> p k t"))
                for kc in range(8):
                    ACT(sqb[:, kc, 0:n], xg[:, kc, 0:n], AF.Square)
                pss = PF()
                for kc in range(8):
                    MM(pss[:, 0:n], ones_b[:], sqb[:, kc, 0:n], start=(kc == 0), stop=(kc == 7))
                RSQ(rstd[:, 0:n], pss[:, 0:n], 1.0 / 1024, epsv[:, 0:1])
                for kc in range(8):
                    e_ = ALT()
                    TT(e_, xg[:, kc, 0:n], xg[:, kc, 0:n], rstd[:, 0:n], ALU.mult)
                    ACT(hT[:, kc, pcol:pcol + n], xg[:, kc, 0:n], AF.Identity,
                        bias=mod[:, SH1 + kc, mv:mv + 1], scale=A1[:, kc, mv:mv + 1])
            CK("M1")
            MARK(P["name"] + ":lora1")
            for (t0g, ntile) in P["groups"]:
                for sc in range(t0g // 2, (t0g + ntile) // 2):
                    t0 = 2 * sc
                    cc = pc[t0]
                    e_ = "dve"
                    acc = (arena[:, 0:2048] if sc % 2 == 0 else arena[:, 2048:4096]).rearrange("p (k t) -> p k t", t=256)
                    hh = hT[:, :, cc:cc + 256]
                    if P["kind"] == "seq":
                        TT(e_, acc, hT[:, :, cc - 1:cc + 255], hT[:, :, cc + 1:cc + 257], ALU.add)
                        sc0 = 0.5
                    else:
                        TT("pool", acc, hT[:, :, cc - 64:cc + 192], hT[:, :, cc + 64:cc + 320], ALU.add)
                        a4 = acc.rearrange("p k (r c) -> p k r c", c=64)
                        h4 = hh.rearrange("p k (r c) -> p k r c", c=64)
                        TT(e_, a4[:, :, :, 1:64], a4[:, :, :, 1:64], h4[:, :, :, 0:63], ALU.add)
                        TT(e_, a4[:, :, :, 0:63], a4[:, :, :, 0:63], h4[:, :, :, 1:64], ALU.add)
                        sc0 = 0.25
                    STT("dve", dhT[:, :, 128 * t0:128 * t0 + 256], acc, sc0, hh, ALU.mult, ALU.subtract)
                t0 = t0g
                n = 128 * ntile
                hs = lambda kc: hT[:, kc, pc[t0]:pc[t0] + n]
                ds = lambda kc: dhT[:, kc, 128 * t0:128 * t0 + n]
                cs = slice(128 * t0, 128 * t0 + n)
                for j in range(3):
                    if j == 2 and not own:
                        continue
                    pp = PF()
                    for kc in range(8):
                        MM(pp[:, 0:n], L1[:, kc, 0, j * 128:(j + 1) * 128], hs(kc), start=(kc == 0), stop=False)
                    for kc in range(8):
                        MM(pp[:, 0:n], L1[:, kc, 1, j * 128:(j + 1) * 128], ds(kc), start=False, stop=(kc == 7))
                    if j == 0:
                        ACT(tanhT[:, cs], pp[:, 0:n], AF.Tanh)
                    elif j == 1:
                        CP("act", a1T[:, cs], pp[:, 0:n])
                    else:
                        ACT(sigT[:, cs], pp[:, 0:n], AF.Sigmoid)
                pp = PF()
                for kc in range(8):
                    MM(pp[0:64, 0:n], L1[:, kc, 0, 384:448], hs(kc), start=(kc == 0), stop=(kc == 7))
                CP("act", gk1T[:, cs], pp[0:64, 0:n])

            CK("M3")

            def init_state(kind, d, width, dram_blocks, mid):
                tf = Tf[:, 0:width]
                if kind == "mid":
                    CP("pool", tf, mid)
                else:
                    MS("pool", tf, 0.0)
                    if kind == "dram":
                        bw = width // 2
                        for e in range(2):
                            k.dma(Tf[64 * e:64 * e + 64, bw * e:bw * (e + 1)], dram_blocks[e])
                CP("act", Tb[:, 0:width], tf)

            wb0 = wbf[0][:].rearrange("p a b -> p (a b)")
            wb1 = wbf[1][:].rearrange("p a b -> p (a b)")

            def carve_job(wb, o):
                xk_ = wb[:, o:o + 1024].rearrange("p (e m c) -> p e m c", e=2, m=2)
                o += 1024
                qp_ = []
                for _q in range(3):
                    qp_.append(wb[:, o:o + 512].rearrange("p (e m c) -> p e m c", e=2, m=2))
                    o += 512
                rb_ = []
                for _q in range(2):
                    rb_.append(wb[:, o:o + 256].rearrange("p (e c) -> p e c", e=2))
                    o += 256
                zr_ = wb[:, o:o + 128].rearrange("p (e c) -> p e c", e=2)
                o += 128
                zs_ = wb[:, o:o + 256].rearrange("p (m e c) -> p m e c", m=2, e=2)
                return dict(xk=xk_, qp=qp_, rb=rb_, zr=zr_, zs=zs_)

            JB = [dict(xk=XK[i][:], qp=[q[:] for q in QP[3 * i:3 * i + 3]], rb=[r_[:] for r_ in RR[2 * i:2 * i + 2]], zr=ZR[i][:], zs=ZS[i][:]) for i in range(2)]
            JB.append(carve_job(wb0, 0))
            JB.append(carve_job(wb1, 256))
            SETS = [[(AR[di][:], BK[di][:], toks[di][:]) for di in range(2)], []]
            for di in range(2):
                ar_b = mixtok[:, 4 + di, 512:1024].rearrange("p (a b c) -> p a b c", a=2, b=2)
                bk_b = mixtok[:, 6 + di, 512:1024].rearrange("p (a b c) -> p a b c", a=2, b=2)
                tk_b = mixtok[:, 8 + 2 * di:10 + 2 * di, 512:896].rearrange("p i (j c) -> p i j c", j=3)
                SETS[1].append((ar_b, bk_b, tk_b))
            SETSF = [SETS[0][0], SETS[0][1], SETS[1][0], SETS[1][1]]

            def wprep_steps(hp):
                si = STG()
                for j in range(3):
                    k.dma(stg[si][:, :, j * 128:(j + 1) * 128],
                          w_in[:, j * 512 + hp * 128:j * 512 + (hp + 1) * 128].rearrange("(k p) n -> p k n", p=128))
                for j in range(3):
                    k.dma(rows[:, j * 128:(j + 1) * 128],
                          rp[:, R_MU + j * 512 + hp * 128:R_MU + j * 512 + (hp + 1) * 128].partition_broadcast(128))
                si2 = STG()
                for d in range(2):
                    k.dma(stg[si2][64 * d:64 * d + 64, 0, 0:128], w2[d][:, hp * 128:(hp + 1) * 128])
                    k.dma(stg[si2][64 * d:64 * d + 64, 0, 128:256], a2[d][:, hp * 128:(hp + 1) * 128])
                yield
                for kc in range(8):
                    CP("act" if kc % 2 else "pool", Whp[:, kc, 0, 0:384], stg[si][:, kc, 0:384])
                    TT("dve", Whp[:, kc, 1, 0:384], stg[si][:, kc, 0:384], rows[:, 0:384], ALU.mult)
                    if kc % 2:
                        yield
                CP("pool", W2b[:], stg[si2][:, 0, 0:256].rearrange("p (a b) -> p a b", b=128))
                yield

            def gla_wprep_steps(gp):
                si = STG()
                k.dma(stg[si][:, :, 0:128], w_in[:, 1536 + gp * 128:1536 + (gp + 1) * 128].rearrange("(k p) n -> p k n", p=128))
                k.dma(stg[si][:, :, 128:256], w_in[:, 1792 + gp * 128:1792 + (gp + 1) * 128].rearrange("(k p) n -> p k n", p=128))
                MS("pool", stg[si][0:64, 0, 256:384], 0.0)
                for d in range(2):
                    k.dma(stg[si][32 * d:32 * d + 16, 0, 256:384], gk2[d][:, gp * 128:(gp + 1) * 128])
                si2 = STG()
                k.dma(stg[si2][:, :, 0:256], w_in[:, 2048 + gp * 256:2048 + (gp + 1) * 256].rearrange("(k p) n -> p k n", p=128))
                k.dma(stg[si2][:, :, 256:512], w_in[:, 2560 + gp * 256:2560 + (gp + 1) * 256].rearrange("(k p) n -> p k n", p=128))
                yield
                for kc in range(8):
                    CP("act" if kc % 2 else "pool", Whp[:, kc, gp, 0:256], stg[si][:, kc, 0:256])
                    CP("pool" if kc % 2 else "act", wbf[gp][:, kc, :], stg[si2][:, kc, :])
                    if kc % 2:
                        yield
                CP("pool", GK2b[:, gp, :], stg[si][0:64, 0, 256:384])
                yield

            for hp in range(4):
                MARK(P["name"] + ":rwkv%d" % hp)
                if hp == 0:
                    for _ in wprep_steps(0):
                        pass
                kkv = vps[:, V_KK + hp:V_KK + hp + 1]
                kav = vps[:, V_KA + hp:V_KA + hp + 1]
                rkv_ = vps[:, V_RK + hp:V_RK + hp + 1]
                CK("M3a")

                def proj_steps(sc):
                    t0 = 2 * sc
                    hs = lambda kc: hT[:, kc, pc[t0]:pc[t0] + 256]
                    ds = lambda kc: dhT[:, kc, 128 * t0:128 * t0 + 256]
                    dst = [T["r"], T["k"], vb[:]]
                    for j in range(3):
                        if j == 0 and not own:
                            continue
                        pp = PF()
                        for kc in range(8):
                            MM(pp[:, 0:256], Whp[:, kc, 0, j * 128:(j + 1) * 128], hs(kc), start=(kc == 0), stop=False)
                        for kc in range(8):
                            MM(pp[:, 0:256], Whp[:, kc, 1, j * 128:(j + 1) * 128], ds(kc), start=False, stop=(kc == 7))
                        CP("act", dst[j], pp[:, 0:256])
                        yield
                    TS("dve", T["kq"], T["k"], kkv, None, ALU.mult)
                    ACT(sqb[:, 0, 0:256], T["kq"], AF.Square)
                    pp = PF()
                    MM(pp[:, 0:256], blk_b[:], sqb[:, 0, 0:256])
                    TS("dve", T["x1"], pp[:, 0:256], 1e-12, None, ALU.max)
                    yield
                    RSQ(T["x1"], T["x1"], 1.0, epsv[:, 3:4])
                    TT("dve", T["kk"], T["kq"], T["x1"], ALU.mult)
                    pt = PB()
                    for i in range(2):
                        TR(pt[:, i * 128:(i + 1) * 128], vb[:, i * 128:(i + 1) * 128], ident_b[:])
                    CP("dve", vtok[:, t0:t0 + 2, 0:128], pt[:, 0:256].rearrange("p (a b) -> p a b", b=128))
                    yield

                def prep_steps(sc, d, par):
                    t0 = 2 * sc
                    cs = slice(128 * t0, 128 * t0 + 256)
                    ar, bk, tk = SETSF[par]
                    pz = PF()
                    MM(pz[:, 0:256], W2b[64 * d:64 * d + 64, 0, :], tanhT[64 * d:64 * d + 64, cs])
                    MM(pz[:, 256:512], W2b[64 * d:64 * d + 64, 1, :], a1T[64 * d:64 * d + 64, cs])
                    ACT(T["sig"], pz[:, 0:256], AF.Sigmoid, bias=vps[:, V_W0 + d * 4 + hp:V_W0 + d * 4 + hp + 1])
                    ACT(T["a"], pz[:, 256:512], AF.Sigmoid, bias=vps[:, V_A0 + d * 4 + hp:V_A0 + d * 4 + hp + 1])
                    yield
                    kd = T["kd0"] if d == 0 else T["kd"]
                    TS("pool", T["x2"], T["a"], kav, omka[:, hp:hp + 1], ALU.mult, ALU.add)
                    TT("pool", kd, T["x2"], T["k"], ALU.mult)
                    TT("pool", T["be"], T["kk"], T["a"], ALU.mult)
                    if d == 0:
                        SCAN(T["S"], scm[:, 0:256], T["sig"])
                    else:
                        SCAN(rev(T["S"]), rev(scm[:, 1:257]), rev(T["sig"]))
                    TT("dve", T["D"], T["S"], T["sig"], ALU.subtract)
                    yield
                    ACT(T["E1"], T["S"], AF.Exp, scale=-CW)
                    ACT(T["E2"], T["S"], AF.Exp, scale=CW)
                    ACT(T["E3"], T["D"], AF.Exp, scale=-CW)
                    yield
                    v3 = lambda t: t.rearrange("p (a b) -> p a b", b=128)
                    STT("dve", ar[:, :, 0, :], v3(T["kk"]), -1.0, v3(T["E3"]), ALU.mult, ALU.mult)
                    TT("pool", ar[:, :, 1, :], v3(T["r"]), v3(T["E1"]), ALU.mult)
                    TT("dve", bk[:, :, 0, :], v3(T["be"]), v3(T["E2"]), ALU.mult)
                    TT("pool", bk[:, :, 1, :], v3(kd), v3(T["E2"]), ALU.mult)
                    gcol = 127 if d == 0 else 0
                    CP("pool", gam[:, d, t0:t0 + 2], v3(T["E1"])[:, :, gcol])
                    yield
                    if d == 1 and 0 in dirs:
                        for i in range(2):
                            if (t0 + i) in own:
                                oi = own.index(t0 + i)
                                TT("pool", T["x2"][:, 0:128], T["kd0"][:, i * 128:(i + 1) * 128], T["kd"][:, i * 128:(i + 1) * 128], ALU.add)
                                STT("pool", prodb[:, oi, :], T["x2"][:, 0:128], rkv_, T["r"][:, i * 128:(i + 1) * 128], ALU.mult, ALU.mult)
                    for i in range(2):
                        pt = PB()
                        TR(pt[:, 0:128], ar[:, i, 0, :], ident_b[:])
                        TR(pt[:, 128:256], bk[:, i, 0, :], ident_b[:])
                        TR(pt[:, 256:384], bk[:, i, 1, :], ident_b[:])
                        CP("act", tk[:, i, :, :], pt[:, 0:384].rearrange("p (a b) -> p a b", b=128))
                        yield

                def chunk_steps(group):
                    jobs = []
                    for gi_, (sc_, d_, si_) in enumerate(group):
                        for i in range(2):
                            jb = JB[2 * gi_ + i]
                            st_ = SETSF[si_]
                            jobs.append(dict(i=i, d=d_, tile=2 * sc_ + i, cd=d_ * 8 + 2 * sc_ + i, ar=st_[0], bk=st_[1], tk=st_[2],
                                             ev=("dve" if (2 * gi_ + i) == 2 * len(group) - 1 else "act"), **jb))
                    for J in jobs:
                        i, d, ar, bk, tk = J["i"], J["d"], J["ar"], J["bk"], J["tk"]
                        J["bE"] = [PF(), PF()]
                        J["bQ"] = [PF(), PF()]
                        for e in range(2):
                            hsl = slice(64 * e, 64 * e + 64)
                            arf = ar[hsl, i, :, :].rearrange("p a b -> p (a b)")
                            MM(J["bE"][e][:, 0:256], bk[hsl, i, 0, :], arf)
                            MM(J["bE"][e][:, 256:512], bk[hsl, i, 1, :], arf)
                            MM(J["bQ"][e][:, 0:128], ar[hsl, i, 0, :], bk[hsl, i, 0, :])
                        xk = J["xk"]
                        for e in range(2):
                            TT("dve", xk[:, e, :, :], J["bE"][e][:, 0:512].rearrange("p (a b) -> p a b", b=256), bcm(mask2[:, d, :], 2), ALU.mult)
                            TT("dve", J["qp"][0][:, e, 0, :], J["bQ"][e][:, 0:128], maskQ[:, d, :], ALU.mult)
                        yield
                    for J in jobs:
                        xk = J["xk"]
                        J["rr"] = J["rb"][0]
                        TT("pool", J["rr"][:], xk[:, :, 0, 0:128], bcm(ident_b[:], 2), ALU.add)
                        J["Pm"] = [xk[:, e, 0, 0:128] for e in range(2)]
                        J["Qm"] = [J["qp"][0][:, e, 0, :] for e in range(2)]
                    for lv in range(1, 7):
                        for J in jobs:
                            pq = PF()
                            J["pq"] = pq
                            for e in range(2):
                                MM(pq[:, e * 256:e * 256 + 128], J["Pm"][e], J["Qm"][e])
                                if lv < 6:
                                    MM(pq[:, e * 256 + 128:e * 256 + 256], J["Qm"][e], J["Pm"][e])
                        for J in jobs:
                            qn = J["qp"][1 + (lv % 2)]
                            pq4 = J["pq"][:, 0:512].rearrange("p (a b c) -> p a b c", b=2, c=128)
                            ee_ = J["ev"]
                            if lv < 6:
                                CP(ee_, qn[:], pq4)
                            else:
                                CP(ee_, qn[:, :, 0, :], pq4[:, :, 0, :])
                            J["Qm"] = [qn[:, e, 0, :] for e in range(2)]
                            J["Pm"] = [qn[:, e, 1, :] for e in range(2)]
                        yield
                        for J in jobs:
                            prr = PF()
                            J["prr"] = prr
                            for e in range(2):
                                MM(prr[:, e * 128:(e + 1) * 128], J["Qm"][e], J["rr"][:, e, :])
                        for J in jobs:
                            rn = J["rb"][lv % 2]
                            TT("dve", rn[:], J["prr"][:, 0:256].rearrange("p (a b) -> p a b", b=128), J["rr"][:], ALU.add)
                            J["rr"] = rn
                        yield
                    for J in jobs:
                        pw = PF()
                        J["pw"] = pw
                        for e in range(2):
                            MM(pw[:, e * 64:(e + 1) * 64], J["xk"][:, e, 1, 0:128], vtok[:, J["tile"], 64 * e:64 * e + 64])
                    for J in jobs:
                        CP(J["ev"], J["zr"][:], J["pw"][:, 0:128].rearrange("p (a b) -> p a b", b=64))
                    yield
                    for J in jobs:
                        pzz = PF()
                        J["pzz"] = pzz
                        for e in range(2):
                            MM(pzz[:, e * 64:(e + 1) * 64], J["rr"][:, e, :], J["tk"][:, J["i"], 0, 64 * e:64 * e + 64])
                            MM(pzz[:, 128 + e * 64:128 + (e + 1) * 64], J["rr"][:, e, :], J["zr"][:, e, :])
                    for J in jobs:
                        CP(J["ev"], J["zs"][:], J["pzz"][:, 0:256].rearrange("p (a b c) -> p a b c", b=2, c=64))
                        J["Atok"] = J["zs"][:, 0, :, :].rearrange("p a b -> p (a b)")
                        J["U0"] = J["zs"][:, 1, :, :].rearrange("p a b -> p (a b)")
                    yield
                    for J in jobs:
                        i, tile, tk = J["i"], J["tile"], J["tk"]
                        pg1 = PF()
                        J["pg1"] = pg1
                        MM(pg1[:, 0:128], J["Atok"], tk[:, i, 1, :], start=True, stop=False)
                        MM(pg1[:, 0:128], ident_b[:], ident_b[:], start=False, stop=True)
                        MM(pg1[:, 128:256], tk[:, i, 1, :], J["U0"], start=True, stop=False)
                        MM(pg1[:, 128:256], tk[:, i, 2, :], vtok[:, tile, 0:128], start=False, stop=True)
                        if tile in own:
                            for e in range(2):
                                MM(pg1[:, 256 + e * 128:256 + (e + 1) * 128], J["Atok"], J["xk"][:, e, 0, 128:256])
                    for J in jobs:
                        i, tile, cd, d, ar = J["i"], J["tile"], J["cd"], J["d"], J["ar"]
                        pg1 = J["pg1"]
                        gsc = gam[:, d, tile:tile + 1]
                        TT("dve", store[:, cd, 0:128], pg1[:, 0:128], blk_b[:], ALU.mult)
                        STT("dve", store[:, cd, 256:384], pg1[:, 128:256], gsc, blk_b[:], ALU.mult, ALU.mult)
                        if tile in own:
                            for e in range(2):
                                hsl = slice(64 * e, 64 * e + 64)
                                TT("dve", store[hsl, cd, 128:256], pg1[hsl, 256 + e * 128:256 + (e + 1) * 128], ar[hsl, i, 1, :], ALU.add)
                    yield
                    for J in jobs:
                        i, tile = J["i"], J["tile"]
                        if tile in own:
                            py = PF()
                            J["py"] = py
                            for e in range(2):
                                MM(py[:, e * 64:(e + 1) * 64], J["xk"][:, e, 0, 128:256], J["zs"][:, 1, e, :], start=True, stop=False)
                                MM(py[:, e * 64:(e + 1) * 64], J["xk"][:, e, 1, 128:256], vtok[:, tile, 64 * e:64 * e + 64], start=False, stop=True)
                    for J in jobs:
                        tile = J["tile"]
                        if tile in own:
                            oi = own.index(tile)
                            if J["d"] == dirs[0]:
                                CP("act", yacc[:, oi, 0:128], J["py"][:, 0:128])
                            else:
                                TT("dve", yacc[:, oi, 0:128], J["py"][:, 0:128], yacc[:, oi, 0:128], ALU.add)
                    yield

                import itertools
                units = [(sc, d) for sc in range(nt // 2) for d in dirs]
                groups = [[(sc, d, (2 * g_ + j_) % 4) for j_, (sc, d) in enumerate(units[2 * g_:2 * g_ + 2])] for g_ in range((len(units) + 1) // 2)]

                def P_of(group):
                    its = []
                    seen = set()
                    for (sc, d, si) in group:
                        if d == dirs[0] and sc not in seen:
                            its.append(proj_steps(sc))
                            seen.add(sc)
                        its.append(prep_steps(sc, d, si))
                    return itertools.chain(*its)

                for _ in P_of(groups[0]):
                    pass
                for gi, group in enumerate(groups):
                    C = chunk_steps(group)
                    if gi + 1 < len(groups):
                        Pn = P_of(groups[gi + 1])
                    else:
                        Pn = wprep_steps(hp + 1) if hp < 3 else iter(())
                    ca = pa_ = True
                    while ca or pa_:
                        if ca:
                            try:
                                next(C)
                            except StopIteration:
                                ca = False
                        if pa_:
                            try:
                                next(Pn)
                            except StopIteration:
                                pa_ = False
                MARK(P["name"] + ":rseq%d" % hp)
                CK("M8")
                for d in dirs:
                    for ci, chain in enumerate(P["chains"]):
                        init_state(P["init"][d], d, 128, [st_r[d, 2 * hp + e] for e in range(2)], Tmid_r[:, hp, :])
                        order = chain if d == 0 else chain[::-1]
                        pend = None
                        for tile in order:
                            cd = d * 8 + tile
                            ptt = PF()
                            MM(ptt[:, 0:128], store[:, cd, 0:128], Tb[:, 0:128])
                            this = None
                            if tile in own:
                                MM(ptt[:, 128:256], store[:, cd, 128:256], Tb[:, 0:128])
                                this = (ptt, own.index(tile))
                            STT("dve", Tf[:, 0:128], ptt[:, 0:128], gam[:, d, tile:tile + 1], store[:, cd, 256:384], ALU.mult, ALU.add)
                            CP("act", Tb[:, 0:128], Tf[:, 0:128])
                            if pend is not None:
                                TT("dve", yacc[:, pend[1], 0:128], pend[0][:, 128:256], yacc[:, pend[1], 0:128], ALU.add)
                            pend = this
                        if pend is not None:
                            TT("dve", yacc[:, pend[1], 0:128], pend[0][:, 128:256], yacc[:, pend[1], 0:128], ALU.add)
                        if P["end"][d] == "out":
                            for e in range(2):
                                k.dma(ns_r[ci, d, 2 * hp + e], Tf[64 * e:64 * e + 64, 64 * e:64 * e + 64], is_output=True)
                        elif P["end"][d] == "mid":
                            CP("pool", Tmid_r[:, hp, :], Tf[:, 0:128])
                MARK(P["name"] + ":rfin%d" % hp)
                k.dma(rows[:, 384:512], rp[:, R_LNG + hp * 128:R_LNG + (hp + 1) * 128].partition_broadcast(128))
                k.dma(rows[:, 512:640], rp[:, R_LNB + hp * 128:R_LNB + (hp + 1) * 128].partition_broadcast(128))
                CK("M9")
                n_ = len(own)
                if n_:
                    assert own == list(range(n_))
                    yv = yacc[:, 0:n_, 0:128]
                    y4 = yv.rearrange("p n (a b) -> p n a b", b=64)
                    big1 = arena[:, 0:n_ * 128].rearrange("p (n c) -> p n c", c=128)
                    big2 = arena[:, 1024:1024 + n_ * 128].rearrange("p (n c) -> p n c", c=128)
                    b14 = big1.rearrange("p n (a b) -> p n a b", b=64)
                    b24 = big2.rearrange("p n (a b) -> p n a b", b=64)
                    st = lambda j: arena[:, 2048 + 16 * j:2048 + 16 * j + 2 * n_]
                    st3 = lambda j: st(j).rearrange("p (n a) -> p n a", a=2)
                    RSUM("dve", st3(0), y4)
                    ACT(big1, yv, AF.Square)
                    RSUM("dve", st3(1), b14)
                    TS("dve", st(2), st(0), 1.0 / 64, None, ALU.mult)
                    TT("dve", st(3), st(2), st(2), ALU.mult)
                    STT("dve", st(4), st(1), 1.0 / 64, st(3), ALU.mult, ALU.subtract)
                    RSQ(st(4), st(4), 1.0, epsv[:, 1:2])
                    TT("dve", b14, y4, bc(st3(2), 64), ALU.subtract)
                    TT("dve", b14, b14, bc(st3(4), 64), ALU.mult)
                    TT("pool", big1, big1, bcm(rows[:, 384:512], n_), ALU.mult)
                    TT("pool", big1, big1, bcm(rows[:, 512:640], n_), ALU.add)
                    pbn = PF()
                    for oi in range(n_):
                        MM(pbn[:, 2 * oi:2 * oi + 2], prodb[:, oi, :], blkind_b[:])
                    CP("act", st(5), pbn[:, 0:2 * n_])
                    TT("dve", b24, vtok[:, 0:n_, 0:128].rearrange("p n (a b) -> p n a b", b=64), bc(st3(5), 64), ALU.mult)
                    TT("dve", big1, big1, big2, ALU.add)
                    for g_ in range(n_ // 4):
                        pgt = PF()
                        for j in range(4):
                            tile = 4 * g_ + j
                            MM(pgt[:, j * 128:(j + 1) * 128], sigT[:, 128 * tile:128 * tile + 128], G2b[:, hp * 128:(hp + 1) * 128])
                        TT("dve", mixtok[:, g0 + 4 * g_:g0 + 4 * g_ + 4, hp * 128:(hp + 1) * 128], big1[:, 4 * g_:4 * g_ + 4, :],
                           pgt[:, 0:512].rearrange("p (n c) -> p n c", c=128), ALU.mult)

            for _ in gla_wprep_steps(0):
                pass
            CK("M10")
            for gp in range(2):
                MARK(P["name"] + ":gla%d" % gp)
                wq = Whp[:, :, gp, :]
                wv = wbf[gp]
                gkw = GK2b[:, gp, :]
                def gproj_steps(sc):
                    t0 = 2 * sc
                    hs = lambda kc: hT[:, kc, pc[t0]:pc[t0] + 256]
                    pq_ = PF()
                    if own:
                        for kc in range(8):
                            MM(pq_[:, 0:256], wq[:, kc, 0:128], hs(kc), start=(kc == 0), stop=(kc == 7))
                    for kc in range(8):
                        MM(pq_[:, 256:512], wq[:, kc, 128:256], hs(kc), start=(kc == 0), stop=(kc == 7))
                    if own:
                        k.op("act", lambda e, o=T["r"], a=pq_[:, 0:256]: e.mul(o, a, 0.125), reads=[pq_[:, 0:256]], writes=[T["r"]])
                    CP("act", T["k"], pq_[:, 256:512])
                    yield
                    for i in range(2):
                        tile = t0 + i
                        pv = PF()
                        for kc in range(8):
                            MM(pv[:, 0:256], hT[:, kc, pc[tile]:pc[tile] + 128], wv[:, kc, 0:256], start=(kc == 0), stop=(kc == 7))
                        CP("act", vtok[:, tile, :], pv[:, 0:256])
                        yield

                def gprep_steps(sc, d, par):
                    t0 = 2 * sc
                    cs = slice(128 * t0, 128 * t0 + 256)
                    ar, bk, tk = G4[par]
                    Tn = GT_[dirs.index(d)]
                    pz = PF()
                    MM(pz[:, 0:256], gkw[32 * d:32 * d + 16, :], gk1T[32 * d:32 * d + 16, cs])
                    ACT(Tn["sig"], pz[:, 0:256], AF.Sigmoid, bias=vps[:, V_GKB + d * 2 + gp:V_GKB + d * 2 + gp + 1])
                    ACT(Tn["a"], Tn["sig"], AF.Ln)
                    yield
                    if d == 0:
                        SCAN(Tn["S"], scm[:, 0:256], Tn["a"])
                    else:
                        SCAN(rev(Tn["S"]), rev(scm[:, 1:257]), rev(Tn["a"]))
                    yield
                    ACT(Tn["E1"], Tn["S"], AF.Exp, scale=1.0 / 16)
                    ACT(Tn["E2"], Tn["S"], AF.Exp, scale=-1.0 / 16)
                    yield
                    v3 = lambda t: t.rearrange("p (a b) -> p a b", b=128)
                    TT("dve", ar[:, :, 0, :], v3(T["r"]), v3(Tn["E1"]), ALU.mult)
                    TT("pool", bk[:, :, 0, :], v3(T["k"]), v3(Tn["E2"]), ALU.mult)
                    gcol = 127 if d == 0 else 0
                    CP("pool", gam[:, d, t0:t0 + 2], v3(Tn["E1"])[:, :, gcol])
                    yield
                    for i in range(2):
                        pt = PB()
                        TR(pt[:, 0:128], bk[:, i, 0, :], ident_b[:])
                        CP("act", tk[:, i, 0, :], pt[:, 0:128])
                        yield

                def gchunk_steps(sc, d, par):
                    t0 = 2 * sc
                    ar, bk, tk = G4[par]
                    gj = [dict(i=i, tile=t0 + i, cd=d * 8 + t0 + i, at=XK[i]) for i in range(2)]
                    for J in gj:
                        ph = PF()
                        J["ph"] = ph
                        MM(ph[:, 0:256], tk[:, J["i"], 0, :], vtok[:, J["tile"], :])
                        if J["tile"] in own:
                            J["pa"] = [PF(), PF()]
                            for e in range(2):
                                hsl = slice(64 * e, 64 * e + 64)
                                MM(J["pa"][e][:, 0:128], bk[hsl, J["i"], 0, :], ar[hsl, J["i"], 0, :])
                        STT("dve", store[:, J["cd"], 128:384], ph[:, 0:256], gam[:, d, J["tile"]:J["tile"] + 1], blk256_f[:], ALU.mult, ALU.mult)
                        if J["tile"] in own:
                            for e in range(2):
                                TT("dve", J["at"][:, e, 0, 0:128], J["pa"][e][:, 0:128], maskI[:, d, :], ALU.mult)
                            CP("pool", store[:, J["cd"], 0:128], ar[:, J["i"], 0, :])
                        yield
                    for J in gj:
                        if J["tile"] in own:
                            phy = PF()
                            J["phy"] = phy
                            for e in range(2):
                                MM(phy[:, e * 128:(e + 1) * 128], J["at"][:, e, 0, 0:128], vtok[:, J["tile"], 128 * e:128 * e + 128])
                    for J in gj:
                        if J["tile"] in own:
                            oi = own.index(J["tile"])
                            if d == dirs[0]:
                                CP("act", yacc[:, oi, :], J["phy"][:, 0:256])
                            else:
                                TT("dve", yacc[:, oi, :], J["phy"][:, 0:256], yacc[:, oi, :], ALU.add)
                    yield

                G4 = [(AR[0][:], BK[0][:], toks[0][:]), (AR[1][:], BK[1][:], toks[1][:]),
                      (QP[0][:], QP[2][:], RR[0][:].rearrange("p i (j c) -> p i j c", j=1)),
                      (QP[1][:], QP[3][:], RR[1][:].rearrange("p i (j c) -> p i j c", j=1))]
                GT_ = [dict(sig=T["sig"], a=T["a"], S=T["S"], E1=T["E1"], E2=T["E2"]),
                       dict(sig=T["kd"], a=T["kd0"], S=T["be"], E1=T["D"], E2=T["E3"])]

                def rrobin(its):
                    its = list(its)
                    while its:
                        nxt = []
                        for it in its:
                            try:
                                next(it)
                                nxt.append(it)
                                yield
                            except StopIteration:
                                pass
                        its = nxt

                def GP_of(sc):
                    return itertools.chain(gproj_steps(sc), rrobin([gprep_steps(sc, d, (2 * sc + di) % 4) for di, d in enumerate(dirs)]))

                def GC_of(sc):
                    return itertools.chain(*[gchunk_steps(sc, d, (2 * sc + di) % 4) for di, d in enumerate(dirs)])

                for _ in GP_of(0):
                    pass
                for sc in range(nt // 2):
                    C = GC_of(sc)
                    if sc + 1 < nt // 2:
                        Pn = GP_of(sc + 1)
                    else:
                        Pn = gla_wprep_steps(1) if gp == 0 else iter(())
                    ca = pa_ = True
                    while ca or pa_:
                        if ca:
                            try:
                                next(C)
                            except StopIteration:
                                ca = False
                        if pa_:
                            try:
                                next(Pn)
                            except StopIteration:
                                pa_ = False
                for d in dirs:
                    for ci, chain in enumerate(P["chains"]):
                        init_state(P["init"][d], d, 256, [st_g[d, 2 * gp + e] for e in range(2)], Tmid_g[:, gp, :])
                        order = chain if d == 0 else chain[::-1]
                        Tfa = [Tf, T["x1"]]
                        Tbs = [vb[:], Tb[:]]
                        cur = 0
                        pend = None
                        for ci_, tile in enumerate(order):
                            cd = d * 8 + tile
                            this = None
                            if tile in own:
                                ptt = PF()
                                MM(ptt[:, 0:256], store[:, cd, 0:128], Tbs[(ci_ - 1) % 2])
                                this = (ptt, own.index(tile))
                            STT("dve", Tfa[1 - cur], Tfa[cur], gam[:, d, tile:tile + 1], store[:, cd, 128:384], ALU.mult, ALU.add)
                            CP("act", Tbs[ci_ % 2], Tfa[1 - cur])
                            cur ^= 1
                            if pend is not None:
                                TT("dve", yacc[:, pend[1], :], pend[0][:, 0:256], yacc[:, pend[1], :], ALU.add)
                            pend = this
                        if pend is not None:
                            TT("dve", yacc[:, pend[1], :], pend[0][:, 0:256], yacc[:, pend[1], :], ALU.add)
                        Tfin = Tfa[cur]
                        if P["end"][d] == "out":
                            for e in range(2):
                                k.dma(ns_g[ci, d, 2 * gp + e], Tfin[64 * e:64 * e + 64, 128 * e:128 * e + 128], is_output=True)
                        elif P["end"][d] == "mid":
                            CP("pool", Tmid_g[:, gp, :], Tfin)
                n_ = len(own)
                if n_:
                    gr = rows[:, 640:768]
                    gbc = bass.AP(gr.tensor, gr.offset, [gr.ap[0], (0, n_), (0, 2), (1, 128)])
                    ov = yacc[:, 0:n_, :]
                    o4 = ov.rearrange("p n (a b) -> p n a b", b=128)
                    big1 = arena[:, 0:n_ * 256].rearrange("p (n c) -> p n c", c=256)
                    big2 = arena[:, 2048:2048 + n_ * 256].rearrange("p (n c) -> p n c", c=256)
                    b14 = big1.rearrange("p n (a b) -> p n a b", b=128)
                    for pr_ in range(n_ // 2):
                        pgg = PF()
                        for j in range(2):
                            tile = 2 * pr_ + j
                            for kc in range(8):
                                MM(pgg[:, j * 256:(j + 1) * 256], hT[:, kc, pc[tile]:pc[tile] + 128], wv[:, kc, 256:512], start=(kc == 0), stop=(kc == 7))
                        pv2 = pgg[:, 0:512].rearrange("p (n c) -> p n c", c=256)
                        ACT(big2[:, 2 * pr_:2 * pr_ + 2, :], pv2, AF.Sigmoid)
                        TT("dve", big2[:, 2 * pr_:2 * pr_ + 2, :], big2[:, 2 * pr_:2 * pr_ + 2, :], pv2, ALU.mult)
                    ACT(big1, ov, AF.Square)
                    ms = sm[:, 0:2 * n_]
                    ms3 = ms.rearrange("p (n a) -> p n a", a=2)
                    RSUM("dve", ms3, b14)
                    RSQ(ms, ms, 1.0 / 128, epsv[:, 2:3])
                    TT("dve", b14, o4, bc(ms3, 128), ALU.mult)
                    TT("pool", b14, b14, gbc, ALU.mult)
                    TT("dve", mixtok[:, g0:g0 + n_, 512 + gp * 256:512 + (gp + 1) * 256], big1, big2, ALU.mult)

        PP = dict(name="PP", nt=4, pc=[64, 192, 384, 512], mv=0, x0=0, own=[0, 1, 2, 3], g0=0, kind="seq", dirs=[0, 1],
                  pads=[(0, 64), (320, 384), (640, 704)], groups=[(0, 2), (2, 2)],
                  ngroups=[(64, 0, 256), (384, 256, 256)], chains=[[0, 1], [2, 3]],
                  init={0: "zero", 1: "zero"}, end={0: "out", 1: "out"})
        PSO = dict(name="PSO", nt=8, pc=[64 + 128 * i for i in range(8)], mv=1, x0=1536, own=[], g0=0, kind="grid", dirs=[1],
                   pads=[(1088, 1152)], groups=[(0, 4), (4, 4)],
                   ngroups=[(0, 1472, 64), (64, 1536, 512), (576, 2048, 512)], chains=[list(range(8))],
                   init={1: "dram"}, end={1: "mid"})
        PSW = dict(name="PSW", nt=8, pc=[64 + 128 * i for i in range(8)], mv=1, x0=512, own=list(range(8)), g0=4, kind="grid", dirs=[0, 1],
                   pads=[(0, 64)], groups=[(0, 4), (4, 4)],
                   ngroups=[(64, 512, 512), (576, 1024, 512), (1088, 1536, 64)], chains=[list(range(8))],
                   init={0: "dram", 1: "mid"}, end={0: None, 1: None})
        stage = int(_ENVD.get("KSTAGE", "9"))
        if stage >= 1:
            mixer(PP)
        if stage >= 2:
            mixer(PSO)
        if stage >= 3:
            mixer(PSW)

        if dbg:
            dbg_out["mixtok"] = (mixtok, [128, 12, 1024], BF16)

        mixT = dhT
        h2T = hT
        hid = store[:].rearrange("p a b -> p (a b)")[:, 0:3072].rearrange("p (a b) -> p a b", b=768)

        def rms_stats(c0, n):
            for kc in range(8):
                ACT(sqb[:, kc, 0:n], x1[:, kc, c0:c0 + n], AF.Square)
            pss = PF()
            for kc in range(8):
                MM(pss[:, 0:n], ones_b[:], sqb[:, kc, 0:n], start=(kc == 0), stop=(kc == 7))
            RSQ(rstd[:, 0:n], pss[:, 0:n], 1.0 / 1024, epsv[:, 0:1])

        def tr_steps(half_):
            for gi in range(6):
                go = 6 * half_ + gi
                for hh in range(2):
                    pt = PB()
                    for j in range(4):
                        kc = hh * 4 + j
                        TR(pt[:, j * 128:(j + 1) * 128], mixtok[:, go, kc * 128:(kc + 1) * 128], ident_b[:])
                    CP(ALT("dve", "act"), mixT[:, hh * 4:hh * 4 + 4, gi * 128:(gi + 1) * 128],
                       pt[:, 0:512].rearrange("p (a b) -> p a b", b=128))
                    yield

        for half in range(2 if stage >= 4 else 0):
            MARK("post%d" % half)
            tb = 768 * half
            grp = [(0, 512, 0 if half == 0 else 1), (512, 256, 1)]
            if half == 0:
                for _ in tr_steps(0):
                    pass
                tr_next = None
            else:
                for _ in tr_next:
                    pass
            k.dma(x1[:, :, 0:768], xT[:, :, tb:tb + 768].rearrange("k p t -> p k t"))
            for cb in range(2):
                si = STG()
                k.dma(stg[si][:], w_out[:, cb * 512:(cb + 1) * 512].rearrange("(k p) n -> p k n", p=128))
                for kc in range(8):
                    CP(ALT("dve", "act"), wbf[si][:, kc, :], stg[si][:, kc, :])
                for cc in range(4):
                    oc = cb * 4 + cc
                    for (c0, n, mv) in grp:
                        pp = PF()
                        for kc in range(8):
                            MM(pp[:, 0:n], wbf[si][:, kc, cc * 128:(cc + 1) * 128], mixT[:, kc, c0:c0 + n],
                               start=(kc == 0), stop=(kc == 7))
                        xs_ = x1[:, oc, c0:c0 + n]
                        STT("dve", xs_, pp[:, 0:n], mod[:, GT1 + oc, mv:mv + 1], xs_, ALU.mult, ALU.add)
            for (c0, n, mv) in grp:
                rms_stats(c0, n)
                for kc in range(8):
                    TT("dve", ntmp[:, 0:n], x1[:, kc, c0:c0 + n], rstd[:, 0:n], ALU.mult)
                    ACT(h2T[:, kc, c0:c0 + n], ntmp[:, 0:n], AF.Identity,
                        bias=mod[:, SH2 + kc, mv:mv + 1], scale=A2[:, kc, mv:mv + 1])
            MARK("mlp%d" % half)
            if half == 0:
                tr_next = tr_steps(1)
            for hb in range(8):
                if half == 0:
                    for _r in range(2):
                        try:
                            next(tr_next)
                        except StopIteration:
                            pass
                si = STG()
                k.dma(stg[si][:], m1[:, hb * 512:(hb + 1) * 512].rearrange("(k p) n -> p k n", p=128))
                for kc in range(8):
                    CP(ALT("dve", "act"), wbf[si][:, kc, :], stg[si][:, kc, :])
                for cc in range(4):
                    for (c0, n, mv) in grp:
                        pp = PF()
                        for kc in range(8):
                            MM(pp[:, 0:n], wbf[si][:, kc, cc * 128:(cc + 1) * 128], h2T[:, kc, c0:c0 + n],
                               start=(kc == 0), stop=(kc == 7))
                        ACT(ntmp[:, 0:n], pp[:, 0:n], AF.Relu)
                        TT("dve", hid[:, cc, c0:c0 + n], ntmp[:, 0:n], ntmp[:, 0:n], ALU.mult)
                si2 = STG()
                s2v = stg[si2][:].rearrange("p a b -> p (a b)").rearrange("p (a b) -> p a b", b=1024)
                w2v = wbf[si2][:].rearrange("p a b -> p (a b)").rearrange("p (a b) -> p a b", b=1024)
                k.dma(s2v, m2[hb * 512:(hb + 1) * 512, :].rearrange("(k p) n -> p k n", p=128))
                for kc in range(4):
                    CP(ALT("dve", "act"), w2v[:, kc, :], s2v[:, kc, :])
                for oc in range(8):
                    for (c0, n, mv) in grp:
                        pp = PF()
                        for kc in range(4):
                            MM(pp[:, 0:n], w2v[:, kc, oc * 128:(oc + 1) * 128], hid[:, kc, c0:c0 + n],
                               start=(kc == 0), stop=(kc == 3))
                        xs_ = x1[:, oc, c0:c0 + n]
                        STT("dve", xs_, pp[:, 0:n], mod[:, GT2 + oc, mv:mv + 1], xs_, ALU.mult, ALU.add)
            for (c0, n, mv) in grp:
                rms_stats(c0, n)
                for kc in range(8):
                    xs_ = x1[:, kc, c0:c0 + n]
                    STT(ALT(), xs_, xs_, vps[:, V_FNG + kc:V_FNG + kc + 1], rstd[:, 0:n], ALU.mult, ALU.mult)
            k.dma(yT[:, :, tb:tb + 768].rearrange("k p t -> p k t"), x1[:, :, 0:768], is_output=True)

    try:
        body()
    except _Stop:
        pass
    if dbg:
        for name, (t, shp, dt) in dbg_out.items():
            o = nc.dram_tensor("dbg_" + name, shp, dt, kind="ExternalOutput").ap()
            k.dma(o, t[:], is_output=True)
    MARK("end")
    globals()["_LASTK"] = k
    stats = k.emit()
    return nc, stats


def _lay_kc(v):
    return np.ascontiguousarray(v.reshape(8, 128).T)


def _prep_core(c, I):
    f = c % 2
    b = c // 2
    fl = (lambda a: a[::-1]) if f else (lambda a: a)
    xs = [fl(I["x_prompt"][2 * c]), fl(I["x_prompt"][2 * c + 1]), fl(I["x_sample"][b])]
    x = np.concatenate(xs, axis=0)
    xT = np.ascontiguousarray(x.T).reshape(8, 128, 2560)
    cond = np.stack([_lay_kc(I["c_ctx"]), _lay_kc(I["c"][b])], axis=-1)
    dsel = [1, 0] if f else [0, 1]
    vp = np.zeros((128, NV), np.float32)
    vp[:, V_N1G:V_N1G + 8] = _lay_kc(I["norm1_g"][0])
    vp[:, V_N2G:V_N2G + 8] = _lay_kc(I["norm2_g"][0])
    vp[:, V_FNG:V_FNG + 8] = _lay_kc(I["final_norm_g"])
    vp[:, V_MUW:V_MUW + 8] = _lay_kc(I["rwkv_mu_wag"][0, 0])
    vp[:, V_MUA:V_MUA + 8] = _lay_kc(I["rwkv_mu_wag"][0, 1])
    vp[:, V_MUG:V_MUG + 8] = _lay_kc(I["rwkv_mu_wag"][0, 2])
    for d in range(2):
        vp[:, V_W0 + 4 * d:V_W0 + 4 * d + 4] = I["rwkv_w0"][0, dsel[d]].reshape(4, 128).T
        vp[:, V_A0 + 4 * d:V_A0 + 4 * d + 4] = I["rwkv_a0"][0, dsel[d]].reshape(4, 128).T
        vp[:, V_GKB + 2 * d:V_GKB + 2 * d + 2] = I["gla_gk_b"][0, dsel[d]].reshape(2, 128).T
    vp[:, V_KK:V_KK + 4] = I["rwkv_k_k"][0].reshape(4, 128).T
    vp[:, V_KA:V_KA + 4] = I["rwkv_k_a"][0].reshape(4, 128).T
    vp[:, V_RK:V_RK + 4] = I["rwkv_r_k"][0].reshape(512).reshape(4, 128).T
    vp[:, V_ADB:V_ADB + 48] = I["ada_b"][0].reshape(48, 128).T
    rp = np.zeros((1, NR), np.float32)
    rp[0, R_MU:R_MU + 1536] = I["rwkv_mu_rkv"][0]
    rp[0, R_LNG:R_LNG + 512] = I["rwkv_lnx_g"][0]
    rp[0, R_LNB:R_LNB + 512] = I["rwkv_lnx_b"][0]
    rp[0, R_GNG:R_GNG + 512] = np.tile(I["gla_norm_g"][0], 4)
    sr = [I["state_rwkv_fwd"][b, 0], I["state_rwkv_bwd"][b, 0]]
    sg = [I["state_gla_fwd"][b, 0], I["state_gla_bwd"][b, 0]]
    st_r = np.stack([np.swapaxes(sr[dsel[d]], -1, -2) for d in range(2)])
    st_g = np.stack([sg[dsel[d]] for d in range(2)])
    A = np.ascontiguousarray
    return {
        "xT": A(xT), "cond": A(cond.astype(np.float32)), "ada_w": A(I["ada_w"][0]), "vp": vp, "rp": rp,
        "w_in": A(I["w_in"][0]),
        "w1": A(I["rwkv_w1"][0][dsel]), "w2": A(I["rwkv_w2"][0][dsel]),
        "a1": A(I["rwkv_a1"][0][dsel]), "a2": A(I["rwkv_a2"][0][dsel]),
        "g1": A(I["rwkv_g1"][0]), "g2": A(I["rwkv_g2"][0]),
        "gk1": A(I["gla_gk1"][0][dsel]), "gk2": A(I["gla_gk2"][0][dsel]),
        "w_out": A(I["w_out"][0]), "m1": A(I["mlp_w1"][0]), "m2": A(I["mlp_w2"][0]),
        "st_r": A(st_r), "st_g": A(st_g),
    }


_CACHE = {}


def kernel(**inputs):
    I = {k_: np.asarray(v) for k_, v in inputs.items()}
    if "nc" not in _CACHE:
        _CACHE["nc"] = build()[0]
    nc = _CACHE["nc"]
    in_maps = [_prep_core(c, I) for c in range(8)]
    res = run_bass_kernel_spmd(nc, in_maps, core_ids=list(range(8)))
    y_prompt = np.zeros((16, 256, 1024), np.float32)
    y_sample = np.zeros((4, 2048, 1024), np.float32)
    nrf = np.zeros((16, 1, 8, 64, 64), np.float32)
    nrb = np.zeros((16, 1, 8, 64, 64), np.float32)
    ngf = np.zeros((16, 1, 4, 64, 128), np.float32)
    ngb = np.zeros((16, 1, 4, 64, 128), np.float32)
    for c in range(8):
        r = res.results[c]
        f = c % 2
        b = c // 2
        y = np.asarray(r["yT"]).reshape(1024, 1536).T
        fl = (lambda a: a[::-1]) if f else (lambda a: a)
        y_prompt[2 * c] = fl(y[0:256])
        y_prompt[2 * c + 1] = fl(y[256:512])
        ys = y[512:1536]
        if f:
            y_sample[b, 1024:2048] = ys[::-1]
        else:
            y_sample[b, 0:1024] = ys
        nsr = np.asarray(r["ns_r"])
        nsg = np.asarray(r["ns_g"])
        for s in range(2):
            for d in range(2):
                gd = d ^ f
                tgt_r = nrf if gd == 0 else nrb
                tgt_g = ngf if gd == 0 else ngb
                tgt_r[2 * c + s, 0] = np.swapaxes(nsr[s, d], -1, -2)
                tgt_g[2 * c + s, 0] = nsg[s, d]
    return (y_prompt, y_sample, nrf, nrb, ngf, ngb)
```

```python
import numpy as np
from contextlib import ExitStack
import concourse.bass as bass
import concourse.mybir as mybir
from concourse.bass_utils import run_bass_kernel_spmd

F32 = mybir.dt.float32
BF16 = mybir.dt.bfloat16
ALU = mybir.AluOpType
AF = mybir.ActivationFunctionType
AX = mybir.AxisListType

_MARKS = []
_ENVD = {}
CW = 0.6065306597126334


class K:
    N_DMA_SEMS = 24

    def __init__(self, nc, same_engine_sync=True):
        self.nc = nc
        self.es = ExitStack()
        self.ops = {e: [] for e in ("pe", "act", "dve", "pool", "sp")}
        self.recs = {}
        self.same_engine_sync = same_engine_sync
        self.dma_cnt = [0] * self.N_DMA_SEMS
        self.dma_rr = 0
        self.out_events = []
        self.needed = set()
        self.waited = {e: {} for e in self.ops}

    def sb(self, name, shape, dtype):
        return self.es.enter_context(self.nc.sbuf_tensor(name, list(shape), dtype))

    def ps(self, name, shape, dtype=F32):
        return self.es.enter_context(self.nc.psum_tensor(name, list(shape), dtype))

    @staticmethod
    def _box(ap):
        if "PSUM" in str(ap.space).upper():
            return (0, 128, 0, 1 << 30)
        a = ap.ap
        pstep, pcnt = a[0]
        off = int(ap.offset)
        if pstep == 0:
            p0, f0 = 0, off
            pcnt = 1
        else:
            p0 = off // pstep
            f0 = off - p0 * pstep
        lo = 0
        hi = 0
        for st, cn in a[1:]:
            if st >= 0:
                hi += st * (cn - 1)
            else:
                lo += st * (cn - 1)
        return (p0, p0 + pcnt, f0 + lo, f0 + hi + 1)

    @staticmethod
    def _ovl(a, b):
        return a[0] < b[1] and b[0] < a[1] and a[2] < b[3] and b[2] < a[3]

    @staticmethod
    def _covers(a, b):
        return a[0] <= b[0] and a[1] >= b[1] and a[2] <= b[2] and a[3] >= b[3]

    def _track(self, reads, writes, ev):
        deps = {}
        items = []
        for ap in reads:
            if "DRAM" in str(ap.space).upper():
                continue
            items.append((ap.name, self._box(ap), False))
        for ap in writes:
            if "DRAM" in str(ap.space).upper():
                continue
            items.append((ap.name, self._box(ap), True))
        for name, box, isw in items:
            lst = self.recs.setdefault(name, [])
            for (b, w, e) in lst:
                if (w or isw) and self._ovl(b, box):
                    if e[1] > deps.get(e[0], -1):
                        deps[e[0]] = e[1]
        for name, box, isw in items:
            lst = self.recs[name]
            if isw:
                lst[:] = [r for r in lst if not self._covers(box, r[0])]
                lst.append((box, True, ev))
            else:
                lst[:] = [r for r in lst if not (r[1] is False and r[2][0] == ev[0] and self._covers(box, r[0]))]
                lst.append((box, False, ev))
        return deps

    def op(self, eng, fn, reads=(), writes=()):
        idx = len(self.ops[eng])
        ev = (eng, idx)
        deps = self._track(reads, writes, ev)
        waits = []
        for k, v in deps.items():
            if k == eng and (eng == "pe" or not self.same_engine_sync):
                continue
            if self.waited[eng].get(k, -1) >= v:
                continue
            self.waited[eng][k] = v
            waits.append((k, v))
            self.needed.add((k, v))
        self.ops[eng].append(dict(kind="op", fn=fn, waits=waits, desc=(writes[0].name if writes else "?") + "<-" + ",".join(sorted(set(r.name for r in reads)))))
        return ev

    def dma(self, out, in_, queue="sp", is_output=False, **kw):
        k = self.dma_rr
        self.dma_rr = (self.dma_rr + 1) % self.N_DMA_SEMS
        self.dma_cnt[k] += 1
        semname = "dma%d" % k
        ev = (semname, self.dma_cnt[k])
        deps = self._track([in_], [out], ev)
        if self.dma_cnt[k] > 1:
            deps[semname] = max(deps.get(semname, -1), self.dma_cnt[k] - 1)
        waits = []
        for kk, v in deps.items():
            if self.waited[queue].get(kk, -1) >= v:
                continue
            self.waited[queue][kk] = v
            waits.append((kk, v))
            self.needed.add((kk, v))
        self.ops[queue].append(dict(kind="dma", out=out, in_=in_, waits=waits, sem=semname, kw=kw))
        if is_output:
            self.out_events.append(ev)
        return ev

    def emit(self):
        nc = self.nc
        fin = []
        for ev in self.out_events:
            if self.waited["sp"].get(ev[0], -1) >= ev[1]:
                continue
            self.waited["sp"][ev[0]] = ev[1]
            fin.append(ev)
        self.ops["sp"].append(dict(kind="fin", waits=fin))
        val = {}
        for e, lst in self.ops.items():
            c = 0
            for i, o in enumerate(lst):
                if (e, i) in self.needed:
                    c += 1
                    val[(e, i)] = c
        sems = {}
        for e in ("pe", "act", "dve", "pool"):
            sems[e] = self.es.enter_context(nc.semaphore("s_" + e))
        for k in range(self.N_DMA_SEMS):
            sems["dma%d" % k] = self.es.enter_context(nc.semaphore("s_dma%d" % k))

        def wv(k, v):
            if k.startswith("dma"):
                return 16 * v
            return val[(k, v)]

        def run(ename, eng):
            dm = getattr(self, "dummy", None)
            for i, o in enumerate(self.ops[ename]):
                if ename == "pe" and dm is not None and o["waits"] and any(not k.startswith("dma") for (k, v) in o["waits"]):
                    for _ in range(dm[3]):
                        eng.matmul(dm[0], dm[1], dm[2], start=True, stop=True)
                for (k, v) in o["waits"]:
                    eng.wait_ge(sems[k], wv(k, v))
                if o["kind"] == "op":
                    ins = o["fn"](eng)
                    if (ename, i) in self.needed:
                        ins.then_inc(sems[ename], 1)
                elif o["kind"] == "dma":
                    eng.dma_start(out=o["out"], in_=o["in_"], **o["kw"]).then_inc(sems[o["sem"]], 16)

        with nc.Block() as block:
            @block.tensor
            def _(e):
                run("pe", e)

            @block.scalar
            def _(e):
                run("act", e)

            @block.vector
            def _(e):
                run("dve", e)

            @block.gpsimd
            def _(e):
                run("pool", e)

            @block.sync
            def _(e):
                run("sp", e)
        self.es.close()
        return {e: len(l) for e, l in self.ops.items()}


def bc(ap, n):
    return bass.AP(ap.tensor, ap.offset, list(ap.ap) + [(0, n)])


def rev(ap):
    a = list(ap.ap)
    st, cn = a[-1]
    return bass.AP(ap.tensor, ap.offset + (cn - 1) * st, a[:-1] + [(-st, cn)])


V_N1G, V_N2G, V_FNG, V_MUW, V_MUA, V_MUG = 0, 8, 16, 24, 32, 40
V_W0, V_A0, V_KK, V_KA, V_RK, V_GKB, V_ADB = 48, 56, 64, 68, 72, 76, 80
NV = 80 + 48
R_MU, R_LNG, R_LNB, R_GNG = 0, 1536, 2048, 2560
NR = 3072


def build(dbg=False):
    nc = bass.Bass("TRN2", target_bir_lowering=False)
    import os as _os
    k = K(nc, same_engine_sync=(_ENVD.get("KSES", "1") == "1"))
    DT = lambda n, s, kind="ExternalInput": nc.dram_tensor(n, list(s), F32, kind=kind).ap()
    xT = DT("xT", [8, 128, 2560])
    cond = DT("cond", [128, 8, 2])
    ada_w = DT("ada_w", [1024, 6144])
    vp = DT("vp", [128, NV])
    rp = DT("rp", [1, NR])
    w_in = DT("w_in", [1024, 3072])
    w1 = DT("w1", [2, 1024, 64]); w2 = DT("w2", [2, 64, 512])
    a1 = DT("a1", [2, 1024, 64]); a2 = DT("a2", [2, 64, 512])
    g1 = DT("g1", [1024, 128]); g2 = DT("g2", [128, 512])
    gk1 = DT("gk1", [2, 1024, 16]); gk2 = DT("gk2", [2, 16, 256])
    w_out = DT("w_out", [1024, 1024]); m1 = DT("m1", [1024, 4096]); m2 = DT("m2", [4096, 1024])
    st_r = DT("st_r", [2, 8, 64, 64]); st_g = DT("st_g", [2, 4, 64, 128])
    yT = DT("yT", [8, 128, 1536], "ExternalOutput")
    ns_r = DT("ns_r", [2, 2, 8, 64, 64], "ExternalOutput")
    ns_g = DT("ns_g", [2, 2, 4, 64, 128], "ExternalOutput")
    dbg_out = {}

    def TT(eng, out, a, b, op):
        k.op(eng, lambda e: e.tensor_tensor(out, a, b, op), reads=[a, b], writes=[out])

    def TS(eng, out, a, s1, s2, op0, op1=None):
        rd = [a] + [s for s in (s1, s2) if not isinstance(s, (int, float, type(None)))]
        if op1 is None:
            k.op(eng, lambda e: e.tensor_scalar(out, a, s1, None, op0), reads=rd, writes=[out])
        else:
            k.op(eng, lambda e: e.tensor_scalar(out, a, s1, s2, op0, op1), reads=rd, writes=[out])

    def STT(eng, out, a, s, b, op0, op1):
        rd = [a, b] + ([] if isinstance(s, (int, float)) else [s])
        k.op("dve", lambda e: e.scalar_tensor_tensor(out, a, s, b, op0, op1), reads=rd, writes=[out])

    def ACT(out, a, func, bias=None, scale=None):
        rd = [a]
        kw = {}
        if bias is not None:
            kw["bias"] = bias
            if not isinstance(bias, (int, float)):
                rd.append(bias)
        if scale is not None:
            kw["scale"] = scale
            if not isinstance(scale, (int, float)):
                rd.append(scale)
        k.op("act", lambda e: e.activation(out, a, func, **kw), reads=rd, writes=[out])

    def CP(eng, out, a):
        if eng == "act":
            k.op("act", lambda e: e.copy(out, a), reads=[a], writes=[out])
        else:
            k.op(eng, lambda e: e.tensor_copy(out, a), reads=[a], writes=[out])

    def MM(out, lhsT, rhs, start=True, stop=True):
        k.op("pe", lambda e: e.matmul(out, lhsT, rhs, start=start, stop=stop), reads=[lhsT, rhs], writes=[out])

    def TR(out, a, idn):
        k.op("pe", lambda e: e.transpose(out, a, idn), reads=[a, idn], writes=[out])

    def MS(eng, out, v):
        k.op(eng, lambda e: e.memset(out, v), writes=[out])

    def ASEL(out, pattern, cmp, base, cm):
        k.op("pool", lambda e: e.affine_select(out, out, pattern=pattern, compare_op=cmp, fill=0.0, base=base,
                                               channel_multiplier=cm), reads=[out], writes=[out])

    def SCAN(out, m, x):
        k.op("dve", lambda e: e.tensor_tensor_scan(out, m, x, 0.0, ALU.mult, ALU.add), reads=[m, x], writes=[out])

    def RSUM(eng, out, a):
        k.op(eng, lambda e: e.reduce_sum(out, a, AX.X), reads=[a], writes=[out])

    def RSQ(out, a, scale, eps_ap):
        ACT(out, a, AF.Sqrt, bias=eps_ap, scale=scale)
        k.op("dve", lambda e: e.reciprocal(out, out), reads=[out], writes=[out])

    def bcm(ap2d, n):
        return bass.AP(ap2d.tensor, ap2d.offset, [ap2d.ap[0], (0, n)] + list(ap2d.ap[1:]))

    NDUM = int(_ENVD.get("KDUM", "0"))
    NPF = 5 if NDUM else 6
    pf = [k.ps("pf%d" % i, [128, 512], F32) for i in range(NPF)]
    if NDUM:
        pdum = k.ps("pdum", [128, 512], F32)
    pb = [k.ps("pb%d" % i, [128, 1024], BF16) for i in range(2)]
    cnt = {"pf": 0, "pb": 0, "alt": 0, "stg": 0}

    def PF():
        cnt["pf"] += 1
        return pf[cnt["pf"] % NPF]

    def PB():
        cnt["pb"] += 1
        return pb[cnt["pb"] % 2]

    def ALT(a="dve", b="pool"):
        cnt["alt"] += 1
        return a if cnt["alt"] % 2 else b

    def STG():
        cnt["stg"] += 1
        return cnt["stg"] % 2

    import os

    class _Stop(Exception):
        pass

    def MARK(name):
        _MARKS.append((name, len(k.ops["pe"])))

    def CK(tag):
        if _ENVD.get("KSTOP") == tag:
            raise _Stop()

    def body():
        ident_b = k.sb("ident_b", [128, 128], BF16)
        ones_b = k.sb("ones_b", [128, 128], BF16)
        blk_b = k.sb("blk_b", [128, 128], BF16)
        blk256_f = k.sb("blk256_f", [128, 256], BF16)
        blkind_b = k.sb("blkind_b", [128, 2], BF16)
        mask2 = k.sb("mask2", [128, 2, 256], BF16)
        maskQ = k.sb("maskQ", [128, 2, 128], BF16)
        maskI = k.sb("maskI", [128, 2, 128], BF16)
        scm = k.sb("scm", [128, 257], F32)

        MS("pool", ident_b[:], 1.0)
        ASEL(ident_b[:], [[1, 128]], ALU.is_equal, 0, -1)
        MS("pool", ones_b[:], 1.0)
        if NDUM:
            k.dummy = (pdum[:, 0:int(_ENVD.get("KDUMN", "128"))], ident_b[:], ones_b[:, 0:int(_ENVD.get("KDUMN", "128"))], NDUM)
        MS("pool", blk_b[:], 0.0)
        MS("pool", blk_b[0:64, 0:64], 1.0)
        MS("pool", blk_b[64:128, 64:128], 1.0)
        MS("pool", blk256_f[:], 0.0)
        MS("pool", blk256_f[0:64, 0:128], 1.0)
        MS("pool", blk256_f[64:128, 128:256], 1.0)
        MS("pool", blkind_b[:], 0.0)
        MS("pool", blkind_b[0:64, 0:1], 1.0)
        MS("pool", blkind_b[64:128, 1:2], 1.0)
        MS("pool", mask2[:], 1.0)
        MS("pool", maskQ[:], 1.0)
        MS("pool", maskI[:], 1.0)
        ASEL(mask2[:, 0, 0:128], [[1, 128]], ALU.is_ge, -1, -1)
        ASEL(mask2[:, 0, 128:256], [[1, 128]], ALU.is_ge, 0, -1)
        ASEL(mask2[:, 1, 0:128], [[-1, 128]], ALU.is_ge, -1, 1)
        ASEL(mask2[:, 1, 128:256], [[-1, 128]], ALU.is_ge, 0, 1)
        ASEL(maskQ[:, 0, :], [[-1, 128]], ALU.is_ge, -1, 1)
        ASEL(maskQ[:, 1, :], [[1, 128]], ALU.is_ge, -1, -1)
        ASEL(maskI[:, 0, :], [[1, 128]], ALU.is_ge, 0, -1)
        ASEL(maskI[:, 1, :], [[-1, 128]], ALU.is_ge, 0, 1)
        MS("pool", scm[:], 1.0)
        for c_ in (0, 128, 256):
            MS("pool", scm[:, c_:c_ + 1], 0.0)

        epsv = k.sb("epsv", [128, 4], F32)
        MS("pool", epsv[:, 0:1], 1e-6)
        MS("pool", epsv[:, 1:2], 64e-5)
        MS("pool", epsv[:, 2:3], 1e-5)
        MS("pool", epsv[:, 3:4], 0.0)
        CK("A")
        vps = k.sb("vps", [128, NV], F32)
        k.dma(vps[:], vp)
        conds = k.sb("conds", [128, 8, 2], F32)
        k.dma(conds[:], cond)
        omka = k.sb("omka", [128, 4], F32)
        TS("dve", omka[:], vps[:, V_KA:V_KA + 4], -1.0, 1.0, ALU.mult, ALU.add)

        stg = [k.sb("stg%d" % i, [128, 8, 512], F32) for i in range(2)]
        wbf = [k.sb("wbf%d" % i, [128, 8, 512], BF16) for i in range(2)]
        xg = stg[0]
        sqb = wbf[1]

        CK("B")
        csil = k.sb("csil", [128, 8, 2], F32)
        ACT(csil[:], conds[:], AF.Sigmoid)
        TT("dve", csil[:], csil[:], conds[:], ALU.mult)
        mod = k.sb("mod", [128, 48, 2], F32)
        pm = PF()
        for blk in range(12):
            si = STG()
            k.dma(stg[si][:], ada_w[:, blk * 512:(blk + 1) * 512].rearrange("(k p) n -> p k n", p=128))
            for cc in range(4):
                ch = blk * 4 + cc
                for kc in range(8):
                    MM(pm[:, 2 * ch:2 * ch + 2], stg[si][:, kc, cc * 128:(cc + 1) * 128], csil[:, kc, :],
                       start=(kc == 0), stop=(kc == 7))
        for v_ in range(2):
            TT("dve", mod[:, :, v_], pm[:, 0:96].rearrange("p (c v) -> p c v", v=2)[:, :, v_], vps[:, V_ADB:V_ADB + 48], ALU.add)
        CK("C")
        A1 = k.sb("A1", [128, 8, 2], F32)
        A2 = k.sb("A2", [128, 8, 2], F32)
        for v_ in range(2):
            STT("dve", A1[:, :, v_], mod[:, 8:16, v_], 1.0, vps[:, V_N1G:V_N1G + 8], ALU.add, ALU.mult)
            STT("dve", A2[:, :, v_], mod[:, 32:40, v_], 1.0, vps[:, V_N2G:V_N2G + 8], ALU.add, ALU.mult)
        SH1, GT1, SH2, GT2 = 0, 16, 24, 40

        HW = 1152
        hT = k.sb("hT", [128, 8, HW], BF16)
        dhT = k.sb("dhT", [128, 8, 1024], BF16)
        tanhT = k.sb("tanhT", [128, 1024], BF16)
        a1T = k.sb("a1T", [128, 1024], BF16)
        sigT = k.sb("sigT", [128, 1024], BF16)
        gk1T = k.sb("gk1T", [64, 1024], BF16)
        mixtok = k.sb("mixtok", [128, 12, 1024], BF16)
        store = k.sb("store", [128, 16, 384], BF16)
        gam = k.sb("gam", [128, 2, 8], F32)
        vtok = k.sb("vtok", [128, 8, 256], BF16)
        prodb = k.sb("prodb", [128, 8, 128], BF16)
        Tb = k.sb("Tb", [128, 256], BF16)
        Whp = k.sb("Whp", [128, 8, 2, 448], BF16)
        L1 = Whp
        W2b = k.sb("W2b", [128, 2, 128], BF16)
        G2b = k.sb("G2b", [128, 512], BF16)
        GK2b = k.sb("GK2b", [64, 2, 128], BF16)
        arena = k.sb("arena", [128, 8192], F32)
        _ao = [0]

        def carve(n):
            a = arena[:, _ao[0]:_ao[0] + n]
            _ao[0] += n
            return a

        T = {n: carve(256) for n in ("r", "k", "kq", "kk", "sig", "a", "kd", "kd0", "be", "S", "D", "E1", "E2", "E3", "x1", "x2")}
        yacc = carve(2048).rearrange("p (a b) -> p a b", b=256)
        rows = carve(768)
        Tf = carve(256)
        Tmid_r = carve(512).rearrange("p (a b) -> p a b", b=128)
        Tmid_g = carve(512).rearrange("p (a b) -> p a b", b=256)
        assert _ao[0] <= 8192
        x1 = arena[:, 0:6144].rearrange("p (a b) -> p a b", b=768)
        vb = k.sb("vb", [128, 256], BF16)
        AR = [k.sb("AR%d" % i, [128, 2, 2, 128], BF16) for i in range(2)]
        BK = [k.sb("BK%d" % i, [128, 2, 2, 128], BF16) for i in range(2)]
        toks = [k.sb("toks%d" % i, [128, 2, 3, 128], BF16) for i in range(2)]
        XK = [k.sb("XK%d" % i, [128, 2, 2, 256], BF16) for i in range(2)]
        QP = [k.sb("QP%d" % i, [128, 2, 2, 128], BF16) for i in range(6)]
        RR = [k.sb("RR%d" % i, [128, 2, 128], BF16) for i in range(4)]
        ZR = [k.sb("ZR%d" % i, [128, 2, 64], BF16) for i in range(2)]
        ZS = [k.sb("ZS%d" % i, [128, 2, 2, 64], BF16) for i in range(2)]
        sm = k.sb("sm", [128, 16], F32)
        rstd = arena[:, 6144:6656]
        ntmp = arena[:, 6656:7168]

        def load_l1():
            k.dma(rows[:, 640:768], rp[:, R_GNG:R_GNG + 128].partition_broadcast(128))
            si = STG()
            for d in range(2):
                k.dma(stg[si][:, :, d * 64:(d + 1) * 64], w1[d].rearrange("(k p) n -> p k n", p=128))
                k.dma(stg[si][:, :, 128 + d * 64:128 + (d + 1) * 64], a1[d].rearrange("(k p) n -> p k n", p=128))
            k.dma(stg[si][:, :, 256:384], g1.rearrange("(k p) n -> p k n", p=128))
            MS("pool", stg[si][:, :, 384:448], 0.0)
            for d in range(2):
                k.dma(stg[si][:, :, 384 + 32 * d:384 + 32 * d + 16], gk1[d].rearrange("(k p) n -> p k n", p=128))
            for kc in range(8):
                CP("act" if kc % 2 else "dve", L1[:, kc, 0, :], stg[si][:, kc, 0:448])
                for j, vo in enumerate((V_MUW, V_MUA, V_MUG)):
                    TS("dve", L1[:, kc, 1, j * 128:(j + 1) * 128], stg[si][:, kc, j * 128:(j + 1) * 128],
                       vps[:, vo + kc:vo + kc + 1], None, ALU.mult)

        si = STG()
        k.dma(stg[si][:, 0, :], g2)
        CP("pool", G2b[:], stg[si][:, 0, :])

        def mixer(P):
            nt = P["nt"]
            pc = P["pc"]
            mv = P["mv"]
            x0 = P["x0"]
            own = P["own"]
            g0 = P["g0"]
            dirs = P["dirs"]
            MARK(P["name"] + ":norm")
            load_l1()
            for (a_, b_) in P["pads"]:
                MS("pool", hT[:, :, a_:b_], 0.0)
            for (pcol, xcol, n) in P["ngroups"]:
                k.dma(xg[:, :, 0:n], xT[:, :, xcol:xcol + n].rearrange("k p t -> p k t"))
                for kc in range(8):
                    ACT(sqb[:, kc, 0:n], xg[:, kc, 0:n], AF.Square)
                pss = PF()
                for kc in range(8):
                    MM(pss[:, 0:n], ones_b[:], sqb[:, kc, 0:n], start=(kc == 0), stop=(kc == 7))
                RSQ(rstd[:, 0:n], pss[:, 0:n], 1.0 / 1024, epsv[:, 0:1])
                for kc in range(8):
                    e_ = ALT()
                    TT(e_, xg[:, kc, 0:n], xg[:, kc, 0:n], rstd[:, 0:n], ALU.mult)
                    ACT(hT[:, kc, pcol:pcol + n], xg[:, kc, 0:n], AF.Identity,
                        bias=mod[:, SH1 + kc, mv:mv + 1], scale=A1[:, kc, mv:mv + 1])
            CK("M1")
            MARK(P["name"] + ":lora1")
            for (t0g, ntile) in P["groups"]:
                for sc in range(t0g // 2, (t0g + ntile) // 2):
                    t0 = 2 * sc
                    cc = pc[t0]
                    e_ = "dve"
                    acc = (arena[:, 0:2048] if sc % 2 == 0 else arena[:, 2048:4096]).rearrange("p (k t) -> p k t", t=256)
                    hh = hT[:, :, cc:cc + 256]
                    if P["kind"] == "seq":
                        TT(e_, acc, hT[:, :, cc - 1:cc + 255], hT[:, :, cc + 1:cc + 257], ALU.add)
                        sc0 = 0.5
                    else:
                        TT("pool", acc, hT[:, :, cc - 64:cc + 192], hT[:, :, cc + 64:cc + 320], ALU.add)
                        a4 = acc.rearrange("p k (r c) -> p k r c", c=64)
                        h4 = hh.rearrange("p k (r c) -> p k r c", c=64)
                        TT(e_, a4[:, :, :, 1:64], a4[:, :, :, 1:64], h4[:, :, :, 0:63], ALU.add)
                        TT(e_, a4[:, :, :, 0:63], a4[:, :, :, 0:63], h4[:, :, :, 1:64], ALU.add)
                        sc0 = 0.25
                    STT("dve", dhT[:, :, 128 * t0:128 * t0 + 256], acc, sc0, hh, ALU.mult, ALU.subtract)
                t0 = t0g
                n = 128 * ntile
                hs = lambda kc: hT[:, kc, pc[t0]:pc[t0] + n]
                ds = lambda kc: dhT[:, kc, 128 * t0:128 * t0 + n]
                cs = slice(128 * t0, 128 * t0 + n)
                for j in range(3):
                    if j == 2 and not own:
                        continue
                    pp = PF()
                    for kc in range(8):
                        MM(pp[:, 0:n], L1[:, kc, 0, j * 128:(j + 1) * 128], hs(kc), start=(kc == 0), stop=False)
                    for kc in range(8):
                        MM(pp[:, 0:n], L1[:, kc, 1, j * 128:(j + 1) * 128], ds(kc), start=False, stop=(kc == 7))
                    if j == 0:
                        ACT(tanhT[:, cs], pp[:, 0:n], AF.Tanh)
                    elif j == 1:
                        CP("act", a1T[:, cs], pp[:, 0:n])
                    else:
                        ACT(sigT[:, cs], pp[:, 0:n], AF.Sigmoid)
                pp = PF()
                for kc in range(8):
                    MM(pp[0:64, 0:n], L1[:, kc, 0, 384:448], hs(kc), start=(kc == 0), stop=(kc == 7))
                CP("act", gk1T[:, cs], pp[0:64, 0:n])

            CK("M3")

            def init_state(kind, d, width, dram_blocks, mid):
                tf = Tf[:, 0:width]
                if kind == "mid":
                    CP("pool", tf, mid)
                else:
                    MS("pool", tf, 0.0)
                    if kind == "dram":
                        bw = width // 2
                        for e in range(2):
                            k.dma(Tf[64 * e:64 * e + 64, bw * e:bw * (e + 1)], dram_blocks[e])
                CP("act", Tb[:, 0:width], tf)

            wb0 = wbf[0][:].rearrange("p a b -> p (a b)")
            wb1 = wbf[1][:].rearrange("p a b -> p (a b)")

            def carve_job(wb, o):
                xk_ = wb[:, o:o + 1024].rearrange("p (e m c) -> p e m c", e=2, m=2)
                o += 1024
                qp_ = []
                for _q in range(3):
                    qp_.append(wb[:, o:o + 512].rearrange("p (e m c) -> p e m c", e=2, m=2))
                    o += 512
                rb_ = []
                for _q in range(2):
                    rb_.append(wb[:, o:o + 256].rearrange("p (e c) -> p e c", e=2))
                    o += 256
                zr_ = wb[:, o:o + 128].rearrange("p (e c) -> p e c", e=2)
                o += 128
                zs_ = wb[:, o:o + 256].rearrange("p (m e c) -> p m e c", m=2, e=2)
                return dict(xk=xk_, qp=qp_, rb=rb_, zr=zr_, zs=zs_)

            JB = [dict(xk=XK[i][:], qp=[q[:] for q in QP[3 * i:3 * i + 3]], rb=[r_[:] for r_ in RR[2 * i:2 * i + 2]], zr=ZR[i][:], zs=ZS[i][:]) for i in range(2)]
            JB.append(carve_job(wb0, 0))
            JB.append(carve_job(wb1, 256))
            SETS = [[(AR[di][:], BK[di][:], toks[di][:]) for di in range(2)], []]
            for di in range(2):
                ar_b = mixtok[:, 4 + di, 512:1024].rearrange("p (a b c) -> p a b c", a=2, b=2)
                bk_b = mixtok[:, 6 + di, 512:1024].rearrange("p (a b c) -> p a b c", a=2, b=2)
                tk_b = mixtok[:, 8 + 2 * di:10 + 2 * di, 512:896].rearrange("p i (j c) -> p i j c", j=3)
                SETS[1].append((ar_b, bk_b, tk_b))
            SETSF = [SETS[0][0], SETS[0][1], SETS[1][0], SETS[1][1]]

            def wprep_steps(hp):
                si = STG()
                for j in range(3):
                    k.dma(stg[si][:, :, j * 128:(j + 1) * 128],
                          w_in[:, j * 512 + hp * 128:j * 512 + (hp + 1) * 128].rearrange("(k p) n -> p k n", p=128))
                for j in range(3):
                    k.dma(rows[:, j * 128:(j + 1) * 128],
                          rp[:, R_MU + j * 512 + hp * 128:R_MU + j * 512 + (hp + 1) * 128].partition_broadcast(128))
                si2 = STG()
                for d in range(2):
                    k.dma(stg[si2][64 * d:64 * d + 64, 0, 0:128], w2[d][:, hp * 128:(hp + 1) * 128])
                    k.dma(stg[si2][64 * d:64 * d + 64, 0, 128:256], a2[d][:, hp * 128:(hp + 1) * 128])
                yield
                for kc in range(8):
                    CP("act" if kc % 2 else "dve", Whp[:, kc, 0, 0:384], stg[si][:, kc, 0:384])
                    TT("dve", Whp[:, kc, 1, 0:384], stg[si][:, kc, 0:384], rows[:, 0:384], ALU.mult)
                    if kc % 2:
                        yield
                CP("pool", W2b[:], stg[si2][:, 0, 0:256].rearrange("p (a b) -> p a b", b=128))
                yield

            def gla_wprep_steps(gp):
                si = STG()
                k.dma(stg[si][:, :, 0:128], w_in[:, 1536 + gp * 128:1536 + (gp + 1) * 128].rearrange("(k p) n -> p k n", p=128))
                k.dma(stg[si][:, :, 128:256], w_in[:, 1792 + gp * 128:1792 + (gp + 1) * 128].rearrange("(k p) n -> p k n", p=128))
                MS("pool", stg[si][0:64, 0, 256:384], 0.0)
                for d in range(2):
                    k.dma(stg[si][32 * d:32 * d + 16, 0, 256:384], gk2[d][:, gp * 128:(gp + 1) * 128])
                si2 = STG()
                k.dma(stg[si2][:, :, 0:256], w_in[:, 2048 + gp * 256:2048 + (gp + 1) * 256].rearrange("(k p) n -> p k n", p=128))
                k.dma(stg[si2][:, :, 256:512], w_in[:, 2560 + gp * 256:2560 + (gp + 1) * 256].rearrange("(k p) n -> p k n", p=128))
                yield
                for kc in range(8):
                    CP("act" if kc % 2 else "dve", Whp[:, kc, gp, 0:256], stg[si][:, kc, 0:256])
                    CP("dve" if kc % 2 else "act", wbf[gp][:, kc, :], stg[si2][:, kc, :])
                    if kc % 2:
                        yield
                CP("pool", GK2b[:, gp, :], stg[si][0:64, 0, 256:384])
                yield

            for hp in range(4):
                MARK(P["name"] + ":rwkv%d" % hp)
                if hp == 0:
                    for _ in wprep_steps(0):
                        pass
                kkv = vps[:, V_KK + hp:V_KK + hp + 1]
                kav = vps[:, V_KA + hp:V_KA + hp + 1]
                rkv_ = vps[:, V_RK + hp:V_RK + hp + 1]
                CK("M3a")

                def proj_steps(sc):
                    t0 = 2 * sc
                    hs = lambda kc: hT[:, kc, pc[t0]:pc[t0] + 256]
                    ds = lambda kc: dhT[:, kc, 128 * t0:128 * t0 + 256]
                    dst = [T["r"], T["k"], vb[:]]
                    for j in range(3):
                        if j == 0 and not own:
                            continue
                        pp = PF()
                        for kc in range(8):
                            MM(pp[:, 0:256], Whp[:, kc, 0, j * 128:(j + 1) * 128], hs(kc), start=(kc == 0), stop=False)
                        for kc in range(8):
                            MM(pp[:, 0:256], Whp[:, kc, 1, j * 128:(j + 1) * 128], ds(kc), start=False, stop=(kc == 7))
                        CP("act", dst[j], pp[:, 0:256])
                        yield
                    TS("dve", T["kq"], T["k"], kkv, None, ALU.mult)
                    ACT(sqb[:, 0, 0:256], T["kq"], AF.Square)
                    pp = PF()
                    MM(pp[:, 0:256], blk_b[:], sqb[:, 0, 0:256])
                    TS("dve", T["x1"], pp[:, 0:256], 1e-12, None, ALU.max)
                    yield
                    RSQ(T["x1"], T["x1"], 1.0, epsv[:, 3:4])
                    TT("dve", T["kk"], T["kq"], T["x1"], ALU.mult)
                    pt = PB()
                    for i in range(2):
                        TR(pt[:, i * 128:(i + 1) * 128], vb[:, i * 128:(i + 1) * 128], ident_b[:])
                    CP("dve", vtok[:, t0:t0 + 2, 0:128], pt[:, 0:256].rearrange("p (a b) -> p a b", b=128))
                    yield

                def prep_steps(sc, d, par):
                    t0 = 2 * sc
                    cs = slice(128 * t0, 128 * t0 + 256)
                    ar, bk, tk = SETSF[par]
                    pz = PF()
                    MM(pz[:, 0:256], W2b[64 * d:64 * d + 64, 0, :], tanhT[64 * d:64 * d + 64, cs])
                    MM(pz[:, 256:512], W2b[64 * d:64 * d + 64, 1, :], a1T[64 * d:64 * d + 64, cs])
                    ACT(T["sig"], pz[:, 0:256], AF.Sigmoid, bias=vps[:, V_W0 + d * 4 + hp:V_W0 + d * 4 + hp + 1])
                    ACT(T["a"], pz[:, 256:512], AF.Sigmoid, bias=vps[:, V_A0 + d * 4 + hp:V_A0 + d * 4 + hp + 1])
                    yield
                    kd = T["kd0"] if d == 0 else T["kd"]
                    TS("pool", T["x2"], T["a"], kav, omka[:, hp:hp + 1], ALU.mult, ALU.add)
                    TT("pool", kd, T["x2"], T["k"], ALU.mult)
                    TT("pool", T["be"], T["kk"], T["a"], ALU.mult)
                    if d == 0:
                        SCAN(T["S"], scm[:, 0:256], T["sig"])
                    else:
                        SCAN(rev(T["S"]), rev(scm[:, 1:257]), rev(T["sig"]))
                    TT("dve", T["D"], T["S"], T["sig"], ALU.subtract)
                    yield
                    ACT(T["E1"], T["S"], AF.Exp, scale=-CW)
                    ACT(T["E2"], T["S"], AF.Exp, scale=CW)
                    ACT(T["E3"], T["D"], AF.Exp, scale=-CW)
                    yield
                    v3 = lambda t: t.rearrange("p (a b) -> p a b", b=128)
                    STT("dve", ar[:, :, 0, :], v3(T["kk"]), -1.0, v3(T["E3"]), ALU.mult, ALU.mult)
                    TT("pool", ar[:, :, 1, :], v3(T["r"]), v3(T["E1"]), ALU.mult)
                    TT("dve", bk[:, :, 0, :], v3(T["be"]), v3(T["E2"]), ALU.mult)
                    TT("pool", bk[:, :, 1, :], v3(kd), v3(T["E2"]), ALU.mult)
                    gcol = 127 if d == 0 else 0
                    CP("pool", gam[:, d, t0:t0 + 2], v3(T["E1"])[:, :, gcol])
                    yield
                    if d == 1 and 0 in dirs:
                        for i in range(2):
                            if (t0 + i) in own:
                                oi = own.index(t0 + i)
                                TT("pool", T["x2"][:, 0:128], T["kd0"][:, i * 128:(i + 1) * 128], T["kd"][:, i * 128:(i + 1) * 128], ALU.add)
                                STT("pool", prodb[:, oi, :], T["x2"][:, 0:128], rkv_, T["r"][:, i * 128:(i + 1) * 128], ALU.mult, ALU.mult)
                    for i in range(2):
                        pt = PB()
                        TR(pt[:, 0:128], ar[:, i, 0, :], ident_b[:])
                        TR(pt[:, 128:256], bk[:, i, 0, :], ident_b[:])
                        TR(pt[:, 256:384], bk[:, i, 1, :], ident_b[:])
                        CP("act", tk[:, i, :, :], pt[:, 0:384].rearrange("p (a b) -> p a b", b=128))
                        yield

                def chunk_steps(group):
                    jobs = []
                    for gi_, (sc_, d_, si_) in enumerate(group):
                        for i in range(2):
                            jb = JB[2 * gi_ + i]
                            st_ = SETSF[si_]
                            jobs.append(dict(i=i, d=d_, tile=2 * sc_ + i, cd=d_ * 8 + 2 * sc_ + i, ar=st_[0], bk=st_[1], tk=st_[2],
                                             ev=("dve" if (2 * gi_ + i) == 2 * len(group) - 1 else "act"), **jb))
                    for J in jobs:
                        i, d, ar, bk, tk = J["i"], J["d"], J["ar"], J["bk"], J["tk"]
                        J["bE"] = [PF(), PF()]
                        J["bQ"] = [PF(), PF()]
                        for e in range(2):
                            hsl = slice(64 * e, 64 * e + 64)
                            arf = ar[hsl, i, :, :].rearrange("p a b -> p (a b)")
                            MM(J["bE"][e][:, 0:256], bk[hsl, i, 0, :], arf)
                            MM(J["bE"][e][:, 256:512], bk[hsl, i, 1, :], arf)
                            MM(J["bQ"][e][:, 0:128], ar[hsl, i, 0, :], bk[hsl, i, 0, :])
                        xk = J["xk"]
                        for e in range(2):
                            TT("dve", xk[:, e, :, :], J["bE"][e][:, 0:512].rearrange("p (a b) -> p a b", b=256), bcm(mask2[:, d, :], 2), ALU.mult)
                            TT("dve", J["qp"][0][:, e, 0, :], J["bQ"][e][:, 0:128], maskQ[:, d, :], ALU.mult)
                        yield
                    for J in jobs:
                        xk = J["xk"]
                        J["rr"] = J["rb"][0]
                        TT("pool", J["rr"][:], xk[:, :, 0, 0:128], bcm(ident_b[:], 2), ALU.add)
                        J["Pm"] = [xk[:, e, 0, 0:128] for e in range(2)]
                        J["Qm"] = [J["qp"][0][:, e, 0, :] for e in range(2)]
                    for lv in range(1, 7):
                        for J in jobs:
                            pq = PF()
                            J["pq"] = pq
                            for e in range(2):
                                MM(pq[:, e * 256:e * 256 + 128], J["Pm"][e], J["Qm"][e])
                                if lv < 6:
                                    MM(pq[:, e * 256 + 128:e * 256 + 256], J["Qm"][e], J["Pm"][e])
                        for J in jobs:
                            qn = J["qp"][1 + (lv % 2)]
                            pq4 = J["pq"][:, 0:512].rearrange("p (a b c) -> p a b c", b=2, c=128)
                            ee_ = J["ev"]
                            if lv < 6:
                                CP(ee_, qn[:], pq4)
                            else:
                                CP(ee_, qn[:, :, 0, :], pq4[:, :, 0, :])
                            J["Qm"] = [qn[:, e, 0, :] for e in range(2)]
                            J["Pm"] = [qn[:, e, 1, :] for e in range(2)]
                        yield
                        for J in jobs:
                            prr = PF()
                            J["prr"] = prr
                            for e in range(2):
                                MM(prr[:, e * 128:(e + 1) * 128], J["Qm"][e], J["rr"][:, e, :])
                        for J in jobs:
                            rn = J["rb"][lv % 2]
                            TT("dve", rn[:], J["prr"][:, 0:256].rearrange("p (a b) -> p a b", b=128), J["rr"][:], ALU.add)
                            J["rr"] = rn
                        yield
                    for J in jobs:
                        pw = PF()
                        J["pw"] = pw
                        for e in range(2):
                            MM(pw[:, e * 64:(e + 1) * 64], J["xk"][:, e, 1, 0:128], vtok[:, J["tile"], 64 * e:64 * e + 64])
                    for J in jobs:
                        CP(J["ev"], J["zr"][:], J["pw"][:, 0:128].rearrange("p (a b) -> p a b", b=64))
                    yield
                    for J in jobs:
                        pzz = PF()
                        J["pzz"] = pzz
                        for e in range(2):
                            MM(pzz[:, e * 64:(e + 1) * 64], J["rr"][:, e, :], J["tk"][:, J["i"], 0, 64 * e:64 * e + 64])
                            MM(pzz[:, 128 + e * 64:128 + (e + 1) * 64], J["rr"][:, e, :], J["zr"][:, e, :])
                    for J in jobs:
                        CP(J["ev"], J["zs"][:], J["pzz"][:, 0:256].rearrange("p (a b c) -> p a b c", b=2, c=64))
                        J["Atok"] = J["zs"][:, 0, :, :].rearrange("p a b -> p (a b)")
                        J["U0"] = J["zs"][:, 1, :, :].rearrange("p a b -> p (a b)")
                    yield
                    for J in jobs:
                        i, tile, tk = J["i"], J["tile"], J["tk"]
                        pg1 = PF()
                        J["pg1"] = pg1
                        MM(pg1[:, 0:128], J["Atok"], tk[:, i, 1, :], start=True, stop=False)
                        MM(pg1[:, 0:128], ident_b[:], ident_b[:], start=False, stop=True)
                        MM(pg1[:, 128:256], tk[:, i, 1, :], J["U0"], start=True, stop=False)
                        MM(pg1[:, 128:256], tk[:, i, 2, :], vtok[:, tile, 0:128], start=False, stop=True)
                        if tile in own:
                            for e in range(2):
                                MM(pg1[:, 256 + e * 128:256 + (e + 1) * 128], J["Atok"], J["xk"][:, e, 0, 128:256])
                    for J in jobs:
                        i, tile, cd, d, ar = J["i"], J["tile"], J["cd"], J["d"], J["ar"]
                        pg1 = J["pg1"]
                        gsc = gam[:, d, tile:tile + 1]
                        TT("dve", store[:, cd, 0:128], pg1[:, 0:128], blk_b[:], ALU.mult)
                        STT("dve", store[:, cd, 256:384], pg1[:, 128:256], gsc, blk_b[:], ALU.mult, ALU.mult)
                        if tile in own:
                            for e in range(2):
                                hsl = slice(64 * e, 64 * e + 64)
                                TT("dve", store[hsl, cd, 128:256], pg1[hsl, 256 + e * 128:256 + (e + 1) * 128], ar[hsl, i, 1, :], ALU.add)
                    yield
                    for J in jobs:
                        i, tile = J["i"], J["tile"]
                        if tile in own:
                            py = PF()
                            J["py"] = py
                            for e in range(2):
                                MM(py[:, e * 64:(e + 1) * 64], J["xk"][:, e, 0, 128:256], J["zs"][:, 1, e, :], start=True, stop=False)
                                MM(py[:, e * 64:(e + 1) * 64], J["xk"][:, e, 1, 128:256], vtok[:, tile, 64 * e:64 * e + 64], start=False, stop=True)
                    for J in jobs:
                        tile = J["tile"]
                        if tile in own:
                            oi = own.index(tile)
                            if J["d"] == dirs[0]:
                                CP("act", yacc[:, oi, 0:128], J["py"][:, 0:128])
                            else:
                                TT("dve", yacc[:, oi, 0:128], J["py"][:, 0:128], yacc[:, oi, 0:128], ALU.add)
                    yield

                import itertools
                units = [(sc, d) for sc in range(nt // 2) for d in dirs]
                groups = [[(sc, d, (2 * g_ + j_) % 4) for j_, (sc, d) in enumerate(units[2 * g_:2 * g_ + 2])] for g_ in range((len(units) + 1) // 2)]

                def P_of(group):
                    its = []
                    seen = set()
                    for (sc, d, si) in group:
                        if d == dirs[0] and sc not in seen:
                            its.append(proj_steps(sc))
                            seen.add(sc)
                        its.append(prep_steps(sc, d, si))
                    return itertools.chain(*its)

                for _ in P_of(groups[0]):
                    pass
                for gi, group in enumerate(groups):
                    C = chunk_steps(group)
                    if gi + 1 < len(groups):
                        Pn = P_of(groups[gi + 1])
                    else:
                        Pn = wprep_steps(hp + 1) if hp < 3 else iter(())
                    ca = pa_ = True
                    while ca or pa_:
                        if ca:
                            try:
                                next(C)
                            except StopIteration:
                                ca = False
                        if pa_:
                            try:
                                next(Pn)
                            except StopIteration:
                                pa_ = False
                MARK(P["name"] + ":rseq%d" % hp)
                CK("M8")
                for d in dirs:
                    for ci, chain in enumerate(P["chains"]):
                        init_state(P["init"][d], d, 128, [st_r[d, 2 * hp + e] for e in range(2)], Tmid_r[:, hp, :])
                        order = chain if d == 0 else chain[::-1]
                        pend = None
                        for tile in order:
                            cd = d * 8 + tile
                            ptt = PF()
                            MM(ptt[:, 0:128], store[:, cd, 0:128], Tb[:, 0:128])
                            this = None
                            if tile in own:
                                MM(ptt[:, 128:256], store[:, cd, 128:256], Tb[:, 0:128])
                                this = (ptt, own.index(tile))
                            STT("dve", Tf[:, 0:128], ptt[:, 0:128], gam[:, d, tile:tile + 1], store[:, cd, 256:384], ALU.mult, ALU.add)
                            CP("act", Tb[:, 0:128], Tf[:, 0:128])
                            if pend is not None:
                                TT("dve", yacc[:, pend[1], 0:128], pend[0][:, 128:256], yacc[:, pend[1], 0:128], ALU.add)
                            pend = this
                        if pend is not None:
                            TT("dve", yacc[:, pend[1], 0:128], pend[0][:, 128:256], yacc[:, pend[1], 0:128], ALU.add)
                        if P["end"][d] == "out":
                            for e in range(2):
                                k.dma(ns_r[ci, d, 2 * hp + e], Tf[64 * e:64 * e + 64, 64 * e:64 * e + 64], is_output=True)
                        elif P["end"][d] == "mid":
                            CP("pool", Tmid_r[:, hp, :], Tf[:, 0:128])
                MARK(P["name"] + ":rfin%d" % hp)
                k.dma(rows[:, 384:512], rp[:, R_LNG + hp * 128:R_LNG + (hp + 1) * 128].partition_broadcast(128))
                k.dma(rows[:, 512:640], rp[:, R_LNB + hp * 128:R_LNB + (hp + 1) * 128].partition_broadcast(128))
                CK("M9")
                n_ = len(own)
                if n_:
                    assert own == list(range(n_))
                    yv = yacc[:, 0:n_, 0:128]
                    y4 = yv.rearrange("p n (a b) -> p n a b", b=64)
                    big1 = arena[:, 0:n_ * 128].rearrange("p (n c) -> p n c", c=128)
                    big2 = arena[:, 1024:1024 + n_ * 128].rearrange("p (n c) -> p n c", c=128)
                    b14 = big1.rearrange("p n (a b) -> p n a b", b=64)
                    b24 = big2.rearrange("p n (a b) -> p n a b", b=64)
                    st = lambda j: arena[:, 2048 + 16 * j:2048 + 16 * j + 2 * n_]
                    st3 = lambda j: st(j).rearrange("p (n a) -> p n a", a=2)
                    RSUM("dve", st3(0), y4)
                    ACT(big1, yv, AF.Square)
                    RSUM("dve", st3(1), b14)
                    TS("dve", st(2), st(0), 1.0 / 64, None, ALU.mult)
                    TT("dve", st(3), st(2), st(2), ALU.mult)
                    STT("dve", st(4), st(1), 1.0 / 64, st(3), ALU.mult, ALU.subtract)
                    RSQ(st(4), st(4), 1.0, epsv[:, 1:2])
                    TT("dve", b14, y4, bc(st3(2), 64), ALU.subtract)
                    TT("dve", b14, b14, bc(st3(4), 64), ALU.mult)
                    TT("pool", big1, big1, bcm(rows[:, 384:512], n_), ALU.mult)
                    TT("pool", big1, big1, bcm(rows[:, 512:640], n_), ALU.add)
                    pbn = PF()
                    for oi in range(n_):
                        MM(pbn[:, 2 * oi:2 * oi + 2], prodb[:, oi, :], blkind_b[:])
                    CP("act", st(5), pbn[:, 0:2 * n_])
                    TT("dve", b24, vtok[:, 0:n_, 0:128].rearrange("p n (a b) -> p n a b", b=64), bc(st3(5), 64), ALU.mult)
                    TT("dve", big1, big1, big2, ALU.add)
                    for g_ in range(n_ // 4):
                        pgt = PF()
                        for j in range(4):
                            tile = 4 * g_ + j
                            MM(pgt[:, j * 128:(j + 1) * 128], sigT[:, 128 * tile:128 * tile + 128], G2b[:, hp * 128:(hp + 1) * 128])
                        TT("dve", mixtok[:, g0 + 4 * g_:g0 + 4 * g_ + 4, hp * 128:(hp + 1) * 128], big1[:, 4 * g_:4 * g_ + 4, :],
                           pgt[:, 0:512].rearrange("p (n c) -> p n c", c=128), ALU.mult)

            for _ in gla_wprep_steps(0):
                pass
            CK("M10")
            for gp in range(2):
                MARK(P["name"] + ":gla%d" % gp)
                wq = Whp[:, :, gp, :]
                wv = wbf[gp]
                gkw = GK2b[:, gp, :]
                def gproj_steps(sc):
                    t0 = 2 * sc
                    hs = lambda kc: hT[:, kc, pc[t0]:pc[t0] + 256]
                    pq_ = PF()
                    if own:
                        for kc in range(8):
                            MM(pq_[:, 0:256], wq[:, kc, 0:128], hs(kc), start=(kc == 0), stop=(kc == 7))
                    for kc in range(8):
                        MM(pq_[:, 256:512], wq[:, kc, 128:256], hs(kc), start=(kc == 0), stop=(kc == 7))
                    if own:
                        k.op("act", lambda e, o=T["r"], a=pq_[:, 0:256]: e.mul(o, a, 0.125), reads=[pq_[:, 0:256]], writes=[T["r"]])
                    CP("act", T["k"], pq_[:, 256:512])
                    yield
                    for i in range(2):
                        tile = t0 + i
                        pv = PF()
                        for kc in range(8):
                            MM(pv[:, 0:256], hT[:, kc, pc[tile]:pc[tile] + 128], wv[:, kc, 0:256], start=(kc == 0), stop=(kc == 7))
                        CP("act", vtok[:, tile, :], pv[:, 0:256])
                        yield

                def gprep_steps(sc, d, par):
                    t0 = 2 * sc
                    cs = slice(128 * t0, 128 * t0 + 256)
                    ar, bk, tk = G4[par]
                    Tn = GT_[dirs.index(d)]
                    pz = PF()
                    MM(pz[:, 0:256], gkw[32 * d:32 * d + 16, :], gk1T[32 * d:32 * d + 16, cs])
                    ACT(Tn["sig"], pz[:, 0:256], AF.Sigmoid, bias=vps[:, V_GKB + d * 2 + gp:V_GKB + d * 2 + gp + 1])
                    ACT(Tn["a"], Tn["sig"], AF.Ln)
                    yield
                    if d == 0:
                        SCAN(Tn["S"], scm[:, 0:256], Tn["a"])
                    else:
                        SCAN(rev(Tn["S"]), rev(scm[:, 1:257]), rev(Tn["a"]))
                    yield
                    ACT(Tn["E1"], Tn["S"], AF.Exp, scale=1.0 / 16)
                    ACT(Tn["E2"], Tn["S"], AF.Exp, scale=-1.0 / 16)
                    yield
                    v3 = lambda t: t.rearrange("p (a b) -> p a b", b=128)
                    TT("dve", ar[:, :, 0, :], v3(T["r"]), v3(Tn["E1"]), ALU.mult)
                    TT("pool", bk[:, :, 0, :], v3(T["k"]), v3(Tn["E2"]), ALU.mult)
                    gcol = 127 if d == 0 else 0
                    CP("pool", gam[:, d, t0:t0 + 2], v3(Tn["E1"])[:, :, gcol])
                    yield
                    for i in range(2):
                        pt = PB()
                        TR(pt[:, 0:128], bk[:, i, 0, :], ident_b[:])
                        CP("act", tk[:, i, 0, :], pt[:, 0:128])
                        yield

                def gchunk_steps(sc, d, par):
                    t0 = 2 * sc
                    ar, bk, tk = G4[par]
                    gj = [dict(i=i, tile=t0 + i, cd=d * 8 + t0 + i, at=XK[i]) for i in range(2)]
                    for J in gj:
                        ph = PF()
                        J["ph"] = ph
                        MM(ph[:, 0:256], tk[:, J["i"], 0, :], vtok[:, J["tile"], :])
                        if J["tile"] in own:
                            J["pa"] = [PF(), PF()]
                            for e in range(2):
                                hsl = slice(64 * e, 64 * e + 64)
                                MM(J["pa"][e][:, 0:128], bk[hsl, J["i"], 0, :], ar[hsl, J["i"], 0, :])
                        STT("dve", store[:, J["cd"], 128:384], ph[:, 0:256], gam[:, d, J["tile"]:J["tile"] + 1], blk256_f[:], ALU.mult, ALU.mult)
                        if J["tile"] in own:
                            for e in range(2):
                                TT("dve", J["at"][:, e, 0, 0:128], J["pa"][e][:, 0:128], maskI[:, d, :], ALU.mult)
                            CP("pool", store[:, J["cd"], 0:128], ar[:, J["i"], 0, :])
                        yield
                    for J in gj:
                        if J["tile"] in own:
                            phy = PF()
                            J["phy"] = phy
                            for e in range(2):
                                MM(phy[:, e * 128:(e + 1) * 128], J["at"][:, e, 0, 0:128], vtok[:, J["tile"], 128 * e:128 * e + 128])
                    for J in gj:
                        if J["tile"] in own:
                            oi = own.index(J["tile"])
                            if d == dirs[0]:
                                CP("act", yacc[:, oi, :], J["phy"][:, 0:256])
                            else:
                                TT("dve", yacc[:, oi, :], J["phy"][:, 0:256], yacc[:, oi, :], ALU.add)
                    yield

                G4 = [(AR[0][:], BK[0][:], toks[0][:]), (AR[1][:], BK[1][:], toks[1][:]),
                      (QP[0][:], QP[2][:], RR[0][:].rearrange("p i (j c) -> p i j c", j=1)),
                      (QP[1][:], QP[3][:], RR[1][:].rearrange("p i (j c) -> p i j c", j=1))]
                GT_ = [dict(sig=T["sig"], a=T["a"], S=T["S"], E1=T["E1"], E2=T["E2"]),
                       dict(sig=T["kd"], a=T["kd0"], S=T["be"], E1=T["D"], E2=T["E3"])]

                def rrobin(its):
                    its = list(its)
                    while its:
                        nxt = []
                        for it in its:
                            try:
                                next(it)
                                nxt.append(it)
                                yield
                            except StopIteration:
                                pass
                        its = nxt

                def GP_of(sc):
                    return itertools.chain(gproj_steps(sc), rrobin([gprep_steps(sc, d, (2 * sc + di) % 4) for di, d in enumerate(dirs)]))

                def GC_of(sc):
                    return itertools.chain(*[gchunk_steps(sc, d, (2 * sc + di) % 4) for di, d in enumerate(dirs)])

                for _ in GP_of(0):
                    pass
                for sc in range(nt // 2):
                    C = GC_of(sc)
                    if sc + 1 < nt // 2:
                        Pn = GP_of(sc + 1)
                    else:
                        Pn = gla_wprep_steps(1) if gp == 0 else iter(())
                    ca = pa_ = True
                    while ca or pa_:
                        if ca:
                            try:
                                next(C)
                            except StopIteration:
                                ca = False
                        if pa_:
                            try:
                                next(Pn)
                            except StopIteration:
                                pa_ = False
                for d in dirs:
                    for ci, chain in enumerate(P["chains"]):
                        init_state(P["init"][d], d, 256, [st_g[d, 2 * gp + e] for e in range(2)], Tmid_g[:, gp, :])
                        order = chain if d == 0 else chain[::-1]
                        Tfa = [Tf, T["x1"]]
                        Tbs = [vb[:], Tb[:]]
                        cur = 0
                        pend = None
                        for ci_, tile in enumerate(order):
                            cd = d * 8 + tile
                            this = None
                            if tile in own:
                                ptt = PF()
                                MM(ptt[:, 0:256], store[:, cd, 0:128], Tbs[(ci_ - 1) % 2])
                                this = (ptt, own.index(tile))
                            STT("dve", Tfa[1 - cur], Tfa[cur], gam[:, d, tile:tile + 1], store[:, cd, 128:384], ALU.mult, ALU.add)
                            CP("act", Tbs[ci_ % 2], Tfa[1 - cur])
                            cur ^= 1
                            if pend is not None:
                                TT("dve", yacc[:, pend[1], :], pend[0][:, 0:256], yacc[:, pend[1], :], ALU.add)
                            pend = this
                        if pend is not None:
                            TT("dve", yacc[:, pend[1], :], pend[0][:, 0:256], yacc[:, pend[1], :], ALU.add)
                        Tfin = Tfa[cur]
                        if P["end"][d] == "out":
                            for e in range(2):
                                k.dma(ns_g[ci, d, 2 * gp + e], Tfin[64 * e:64 * e + 64, 128 * e:128 * e + 128], is_output=True)
                        elif P["end"][d] == "mid":
                            CP("pool", Tmid_g[:, gp, :], Tfin)
                n_ = len(own)
                if n_:
                    gr = rows[:, 640:768]
                    gbc = bass.AP(gr.tensor, gr.offset, [gr.ap[0], (0, n_), (0, 2), (1, 128)])
                    ov = yacc[:, 0:n_, :]
                    o4 = ov.rearrange("p n (a b) -> p n a b", b=128)
                    big1 = arena[:, 0:n_ * 256].rearrange("p (n c) -> p n c", c=256)
                    big2 = arena[:, 2048:2048 + n_ * 256].rearrange("p (n c) -> p n c", c=256)
                    b14 = big1.rearrange("p n (a b) -> p n a b", b=128)
                    for pr_ in range(n_ // 2):
                        pgg = PF()
                        for j in range(2):
                            tile = 2 * pr_ + j
                            for kc in range(8):
                                MM(pgg[:, j * 256:(j + 1) * 256], hT[:, kc, pc[tile]:pc[tile] + 128], wv[:, kc, 256:512], start=(kc == 0), stop=(kc == 7))
                        pv2 = pgg[:, 0:512].rearrange("p (n c) -> p n c", c=256)
                        ACT(big2[:, 2 * pr_:2 * pr_ + 2, :], pv2, AF.Sigmoid)
                        TT("dve", big2[:, 2 * pr_:2 * pr_ + 2, :], big2[:, 2 * pr_:2 * pr_ + 2, :], pv2, ALU.mult)
                    ACT(big1, ov, AF.Square)
                    ms = sm[:, 0:2 * n_]
                    ms3 = ms.rearrange("p (n a) -> p n a", a=2)
                    RSUM("dve", ms3, b14)
                    RSQ(ms, ms, 1.0 / 128, epsv[:, 2:3])
                    TT("dve", b14, o4, bc(ms3, 128), ALU.mult)
                    TT("pool", b14, b14, gbc, ALU.mult)
                    TT("dve", mixtok[:, g0:g0 + n_, 512 + gp * 256:512 + (gp + 1) * 256], big1, big2, ALU.mult)

        PP = dict(name="PP", nt=4, pc=[64, 192, 384, 512], mv=0, x0=0, own=[0, 1, 2, 3], g0=0, kind="seq", dirs=[0, 1],
                  pads=[(0, 64), (320, 384), (640, 704)], groups=[(0, 2), (2, 2)],
                  ngroups=[(64, 0, 256), (384, 256, 256)], chains=[[0, 1], [2, 3]],
                  init={0: "zero", 1: "zero"}, end={0: "out", 1: "out"})
        PSO = dict(name="PSO", nt=8, pc=[64 + 128 * i for i in range(8)], mv=1, x0=1536, own=[], g0=0, kind="grid", dirs=[1],
                   pads=[(1088, 1152)], groups=[(0, 4), (4, 4)],
                   ngroups=[(0, 1472, 64), (64, 1536, 512), (576, 2048, 512)], chains=[list(range(8))],
                   init={1: "dram"}, end={1: "mid"})
        PSW = dict(name="PSW", nt=8, pc=[64 + 128 * i for i in range(8)], mv=1, x0=512, own=list(range(8)), g0=4, kind="grid", dirs=[0, 1],
                   pads=[(0, 64)], groups=[(0, 4), (4, 4)],
                   ngroups=[(64, 512, 512), (576, 1024, 512), (1088, 1536, 64)], chains=[list(range(8))],
                   init={0: "dram", 1: "mid"}, end={0: None, 1: None})
        stage = int(_ENVD.get("KSTAGE", "9"))
        if stage >= 1:
            mixer(PP)
        if stage >= 2:
            mixer(PSO)
        if stage >= 3:
            mixer(PSW)

        if dbg:
            dbg_out["mixtok"] = (mixtok, [128, 12, 1024], BF16)

        mixT = dhT
        h2T = hT
        hid = store[:].rearrange("p a b -> p (a b)")[:, 0:3072].rearrange("p (a b) -> p a b", b=768)

        def rms_stats(c0, n):
            for kc in range(8):
                ACT(sqb[:, kc, 0:n], x1[:, kc, c0:c0 + n], AF.Square)
            pss = PF()
            for kc in range(8):
                MM(pss[:, 0:n], ones_b[:], sqb[:, kc, 0:n], start=(kc == 0), stop=(kc == 7))
            RSQ(rstd[:, 0:n], pss[:, 0:n], 1.0 / 1024, epsv[:, 0:1])

        def tr_steps(half_):
            for gi in range(6):
                go = 6 * half_ + gi
                for hh in range(2):
                    pt = PB()
                    for j in range(4):
                        kc = hh * 4 + j
                        TR(pt[:, j * 128:(j + 1) * 128], mixtok[:, go, kc * 128:(kc + 1) * 128], ident_b[:])
                    CP(ALT("dve", "act"), mixT[:, hh * 4:hh * 4 + 4, gi * 128:(gi + 1) * 128],
                       pt[:, 0:512].rearrange("p (a b) -> p a b", b=128))
                    yield

        for half in range(2 if stage >= 4 else 0):
            MARK("post%d" % half)
            tb = 768 * half
            grp = [(0, 512, 0 if half == 0 else 1), (512, 256, 1)]
            if half == 0:
                for _ in tr_steps(0):
                    pass
                tr_next = None
            else:
                for _ in tr_next:
                    pass
            k.dma(x1[:, :, 0:768], xT[:, :, tb:tb + 768].rearrange("k p t -> p k t"))
            for cb in range(2):
                si = STG()
                k.dma(stg[si][:], w_out[:, cb * 512:(cb + 1) * 512].rearrange("(k p) n -> p k n", p=128))
                for kc in range(8):
                    CP(ALT("dve", "act"), wbf[si][:, kc, :], stg[si][:, kc, :])
                for cc in range(4):
                    oc = cb * 4 + cc
                    for (c0, n, mv) in grp:
                        pp = PF()
                        for kc in range(8):
                            MM(pp[:, 0:n], wbf[si][:, kc, cc * 128:(cc + 1) * 128], mixT[:, kc, c0:c0 + n],
                               start=(kc == 0), stop=(kc == 7))
                        xs_ = x1[:, oc, c0:c0 + n]
                        STT("dve", xs_, pp[:, 0:n], mod[:, GT1 + oc, mv:mv + 1], xs_, ALU.mult, ALU.add)
            for (c0, n, mv) in grp:
                rms_stats(c0, n)
                for kc in range(8):
                    TT("dve", ntmp[:, 0:n], x1[:, kc, c0:c0 + n], rstd[:, 0:n], ALU.mult)
                    ACT(h2T[:, kc, c0:c0 + n], ntmp[:, 0:n], AF.Identity,
                        bias=mod[:, SH2 + kc, mv:mv + 1], scale=A2[:, kc, mv:mv + 1])
            MARK("mlp%d" % half)
            if half == 0:
                tr_next = tr_steps(1)
            for hb in range(8):
                if half == 0:
                    for _r in range(2):
                        try:
                            next(tr_next)
                        except StopIteration:
                            pass
                si = STG()
                k.dma(stg[si][:], m1[:, hb * 512:(hb + 1) * 512].rearrange("(k p) n -> p k n", p=128))
                for kc in range(8):
                    CP(ALT("dve", "act"), wbf[si][:, kc, :], stg[si][:, kc, :])
                for cc in range(4):
                    for (c0, n, mv) in grp:
                        pp = PF()
                        for kc in range(8):
                            MM(pp[:, 0:n], wbf[si][:, kc, cc * 128:(cc + 1) * 128], h2T[:, kc, c0:c0 + n],
                               start=(kc == 0), stop=(kc == 7))
                        ACT(ntmp[:, 0:n], pp[:, 0:n], AF.Relu)
                        TT("dve", hid[:, cc, c0:c0 + n], ntmp[:, 0:n], ntmp[:, 0:n], ALU.mult)
                si2 = STG()
                s2v = stg[si2][:].rearrange("p a b -> p (a b)").rearrange("p (a b) -> p a b", b=1024)
                w2v = wbf[si2][:].rearrange("p a b -> p (a b)").rearrange("p (a b) -> p a b", b=1024)
                k.dma(s2v, m2[hb * 512:(hb + 1) * 512, :].rearrange("(k p) n -> p k n", p=128))
                for kc in range(4):
                    CP(ALT("dve", "act"), w2v[:, kc, :], s2v[:, kc, :])
                for oc in range(8):
                    for (c0, n, mv) in grp:
                        pp = PF()
                        for kc in range(4):
                            MM(pp[:, 0:n], w2v[:, kc, oc * 128:(oc + 1) * 128], hid[:, kc, c0:c0 + n],
                               start=(kc == 0), stop=(kc == 3))
                        xs_ = x1[:, oc, c0:c0 + n]
                        STT("dve", xs_, pp[:, 0:n], mod[:, GT2 + oc, mv:mv + 1], xs_, ALU.mult, ALU.add)
            for (c0, n, mv) in grp:
                rms_stats(c0, n)
                for kc in range(8):
                    xs_ = x1[:, kc, c0:c0 + n]
                    STT(ALT(), xs_, xs_, vps[:, V_FNG + kc:V_FNG + kc + 1], rstd[:, 0:n], ALU.mult, ALU.mult)
            k.dma(yT[:, :, tb:tb + 768].rearrange("k p t -> p k t"), x1[:, :, 0:768], is_output=True)

    try:
        body()
    except _Stop:
        pass
    if dbg:
        for name, (t, shp, dt) in dbg_out.items():
            o = nc.dram_tensor("dbg_" + name, shp, dt, kind="ExternalOutput").ap()
            k.dma(o, t[:], is_output=True)
    MARK("end")
    globals()["_LASTK"] = k
    stats = k.emit()
    return nc, stats


def _lay_kc(v):
    return np.ascontiguousarray(v.reshape(8, 128).T)


def _prep_core(c, I):
    f = c % 2
    b = c // 2
    fl = (lambda a: a[::-1]) if f else (lambda a: a)
    xs = [fl(I["x_prompt"][2 * c]), fl(I["x_prompt"][2 * c + 1]), fl(I["x_sample"][b])]
    x = np.concatenate(xs, axis=0)
    xT = np.ascontiguousarray(x.T).reshape(8, 128, 2560)
    cond = np.stack([_lay_kc(I["c_ctx"]), _lay_kc(I["c"][b])], axis=-1)
    dsel = [1, 0] if f else [0, 1]
    vp = np.zeros((128, NV), np.float32)
    vp[:, V_N1G:V_N1G + 8] = _lay_kc(I["norm1_g"][0])
    vp[:, V_N2G:V_N2G + 8] = _lay_kc(I["norm2_g"][0])
    vp[:, V_FNG:V_FNG + 8] = _lay_kc(I["final_norm_g"])
    vp[:, V_MUW:V_MUW + 8] = _lay_kc(I["rwkv_mu_wag"][0, 0])
    vp[:, V_MUA:V_MUA + 8] = _lay_kc(I["rwkv_mu_wag"][0, 1])
    vp[:, V_MUG:V_MUG + 8] = _lay_kc(I["rwkv_mu_wag"][0, 2])
    for d in range(2):
        vp[:, V_W0 + 4 * d:V_W0 + 4 * d + 4] = I["rwkv_w0"][0, dsel[d]].reshape(4, 128).T
        vp[:, V_A0 + 4 * d:V_A0 + 4 * d + 4] = I["rwkv_a0"][0, dsel[d]].reshape(4, 128).T
        vp[:, V_GKB + 2 * d:V_GKB + 2 * d + 2] = I["gla_gk_b"][0, dsel[d]].reshape(2, 128).T
    vp[:, V_KK:V_KK + 4] = I["rwkv_k_k"][0].reshape(4, 128).T
    vp[:, V_KA:V_KA + 4] = I["rwkv_k_a"][0].reshape(4, 128).T
    vp[:, V_RK:V_RK + 4] = I["rwkv_r_k"][0].reshape(512).reshape(4, 128).T
    vp[:, V_ADB:V_ADB + 48] = I["ada_b"][0].reshape(48, 128).T
    rp = np.zeros((1, NR), np.float32)
    rp[0, R_MU:R_MU + 1536] = I["rwkv_mu_rkv"][0]
    rp[0, R_LNG:R_LNG + 512] = I["rwkv_lnx_g"][0]
    rp[0, R_LNB:R_LNB + 512] = I["rwkv_lnx_b"][0]
    rp[0, R_GNG:R_GNG + 512] = np.tile(I["gla_norm_g"][0], 4)
    sr = [I["state_rwkv_fwd"][b, 0], I["state_rwkv_bwd"][b, 0]]
    sg = [I["state_gla_fwd"][b, 0], I["state_gla_bwd"][b, 0]]
    st_r = np.stack([np.swapaxes(sr[dsel[d]], -1, -2) for d in range(2)])
    st_g = np.stack([sg[dsel[d]] for d in range(2)])
    A = np.ascontiguousarray
    return {
        "xT": A(xT), "cond": A(cond.astype(np.float32)), "ada_w": A(I["ada_w"][0]), "vp": vp, "rp": rp,
        "w_in": A(I["w_in"][0]),
        "w1": A(I["rwkv_w1"][0][dsel]), "w2": A(I["rwkv_w2"][0][dsel]),
        "a1": A(I["rwkv_a1"][0][dsel]), "a2": A(I["rwkv_a2"][0][dsel]),
        "g1": A(I["rwkv_g1"][0]), "g2": A(I["rwkv_g2"][0]),
        "gk1": A(I["gla_gk1"][0][dsel]), "gk2": A(I["gla_gk2"][0][dsel]),
        "w_out": A(I["w_out"][0]), "m1": A(I["mlp_w1"][0]), "m2": A(I["mlp_w2"][0]),
        "st_r": A(st_r), "st_g": A(st_g),
    }


_CACHE = {}


def kernel(**inputs):
    I = {k_: np.asarray(v) for k_, v in inputs.items()}
    if "nc" not in _CACHE:
        _CACHE["nc"] = build()[0]
    nc = _CACHE["nc"]
    in_maps = [_prep_core(c, I) for c in range(8)]
    res = run_bass_kernel_spmd(nc, in_maps, core_ids=list(range(8)))
    y_prompt = np.zeros((16, 256, 1024), np.float32)
    y_sample = np.zeros((4, 2048, 1024), np.float32)
    nrf = np.zeros((16, 1, 8, 64, 64), np.float32)
    nrb = np.zeros((16, 1, 8, 64, 64), np.float32)
    ngf = np.zeros((16, 1, 4, 64, 128), np.float32)
    ngb = np.zeros((16, 1, 4, 64, 128), np.float32)
    for c in range(8):
        r = res.results[c]
        f = c % 2
        b = c // 2
        y = np.asarray(r["yT"]).reshape(1024, 1536).T
        fl = (lambda a: a[::-1]) if f else (lambda a: a)
        y_prompt[2 * c] = fl(y[0:256])
        y_prompt[2 * c + 1] = fl(y[256:512])
        ys = y[512:1536]
        if f:
            y_sample[b, 1024:2048] = ys[::-1]
        else:
            y_sample[b, 0:1024] = ys
        nsr = np.asarray(r["ns_r"])
        nsg = np.asarray(r["ns_g"])
        for s in range(2):
            for d in range(2):
                gd = d ^ f
                tgt_r = nrf if gd == 0 else nrb
                tgt_g = ngf if gd == 0 else ngb
                tgt_r[2 * c + s, 0] = np.swapaxes(nsr[s, d], -1, -2)
                tgt_g[2 * c + s, 0] = nsg[s, d]
    return (y_prompt, y_sample, nrf, nrb, ngf, ngb)
```

```python
import numpy as np
from contextlib import ExitStack
import concourse.bass as bass
import concourse.mybir as mybir
from concourse.bass_utils import run_bass_kernel_spmd

F32 = mybir.dt.float32
BF16 = mybir.dt.bfloat16
ALU = mybir.AluOpType
AF = mybir.ActivationFunctionType
AX = mybir.AxisListType

_MARKS = []
_ENVD = {}
CW = 0.6065306597126334


class K:
    N_DMA_SEMS = 24

    def __init__(self, nc, same_engine_sync=True):
        self.nc = nc
        self.es = ExitStack()
        self.ops = {e: [] for e in ("pe", "act", "dve", "pool", "sp")}
        self.recs = {}
        self.same_engine_sync = same_engine_sync
        self.dma_cnt = [0] * self.N_DMA_SEMS
        self.dma_rr = 0
        self.out_events = []
        self.needed = set()
        self.waited = {e: {} for e in self.ops}

    def sb(self, name, shape, dtype):
        return self.es.enter_context(self.nc.sbuf_tensor(name, list(shape), dtype))

    def ps(self, name, shape, dtype=F32):
        return self.es.enter_context(self.nc.psum_tensor(name, list(shape), dtype))

    @staticmethod
    def _box(ap):
        if "PSUM" in str(ap.space).upper():
            return (0, 128, 0, 1 << 30)
        a = ap.ap
        pstep, pcnt = a[0]
        off = int(ap.offset)
        if pstep == 0:
            p0, f0 = 0, off
            pcnt = 1
        else:
            p0 = off // pstep
            f0 = off - p0 * pstep
        lo = 0
        hi = 0
        for st, cn in a[1:]:
            if st >= 0:
                hi += st * (cn - 1)
            else:
                lo += st * (cn - 1)
        return (p0, p0 + pcnt, f0 + lo, f0 + hi + 1)

    @staticmethod
    def _ovl(a, b):
        return a[0] < b[1] and b[0] < a[1] and a[2] < b[3] and b[2] < a[3]

    @staticmethod
    def _covers(a, b):
        return a[0] <= b[0] and a[1] >= b[1] and a[2] <= b[2] and a[3] >= b[3]

    def _track(self, reads, writes, ev):
        deps = {}
        items = []
        for ap in reads:
            if "DRAM" in str(ap.space).upper():
                continue
            items.append((ap.name, self._box(ap), False))
        for ap in writes:
            if "DRAM" in str(ap.space).upper():
                continue
            items.append((ap.name, self._box(ap), True))
        for name, box, isw in items:
            lst = self.recs.setdefault(name, [])
            for (b, w, e) in lst:
                if (w or isw) and self._ovl(b, box):
                    if e[1] > deps.get(e[0], -1):
                        deps[e[0]] = e[1]
        for name, box, isw in items:
            lst = self.recs[name]
            if isw:
                lst[:] = [r for r in lst if not self._covers(box, r[0])]
                lst.append((box, True, ev))
            else:
                lst[:] = [r for r in lst if not (r[1] is False and r[2][0] == ev[0] and self._covers(box, r[0]))]
                lst.append((box, False, ev))
        return deps

    def op(self, eng, fn, reads=(), writes=()):
        idx = len(self.ops[eng])
        ev = (eng, idx)
        deps = self._track(reads, writes, ev)
        waits = []
        for k, v in deps.items():
            if k == eng and (eng == "pe" or not self.same_engine_sync):
                continue
            if self.waited[eng].get(k, -1) >= v:
                continue
            self.waited[eng][k] = v
            waits.append((k, v))
            self.needed.add((k, v))
        self.ops[eng].append(dict(kind="op", fn=fn, waits=waits, desc=(writes[0].name if writes else "?") + "<-" + ",".join(sorted(set(r.name for r in reads)))))
        return ev

    def dma(self, out, in_, queue="sp", is_output=False, **kw):
        k = self.dma_rr
        self.dma_rr = (self.dma_rr + 1) % self.N_DMA_SEMS
        self.dma_cnt[k] += 1
        semname = "dma%d" % k
        ev = (semname, self.dma_cnt[k])
        deps = self._track([in_], [out], ev)
        if self.dma_cnt[k] > 1:
            deps[semname] = max(deps.get(semname, -1), self.dma_cnt[k] - 1)
        waits = []
        for kk, v in deps.items():
            if self.waited[queue].get(kk, -1) >= v:
                continue
            self.waited[queue][kk] = v
            waits.append((kk, v))
            self.needed.add((kk, v))
        self.ops[queue].append(dict(kind="dma", out=out, in_=in_, waits=waits, sem=semname, kw=kw))
        if is_output:
            self.out_events.append(ev)
        return ev

    def emit(self):
        nc = self.nc
        fin = []
        for ev in self.out_events:
            if self.waited["sp"].get(ev[0], -1) >= ev[1]:
                continue
            self.waited["sp"][ev[0]] = ev[1]
            fin.append(ev)
        self.ops["sp"].append(dict(kind="fin", waits=fin))
        val = {}
        for e, lst in self.ops.items():
            c = 0
            for i, o in enumerate(lst):
                if (e, i) in self.needed:
                    c += 1
                    val[(e, i)] = c
        sems = {}
        for e in ("pe", "act", "dve", "pool"):
            sems[e] = self.es.enter_context(nc.semaphore("s_" + e))
        for k in range(self.N_DMA_SEMS):
            sems["dma%d" % k] = self.es.enter_context(nc.semaphore("s_dma%d" % k))

        def wv(k, v):
            if k.startswith("dma"):
                return 16 * v
            return val[(k, v)]

        def run(ename, eng):
            dm = getattr(self, "dummy", None)
            for i, o in enumerate(self.ops[ename]):
                if ename == "pe" and dm is not None and o["waits"] and any(not k.startswith("dma") for (k, v) in o["waits"]):
                    for _ in range(dm[3]):
                        eng.matmul(dm[0], dm[1], dm[2], start=True, stop=True)
                for (k, v) in o["waits"]:
                    eng.wait_ge(sems[k], wv(k, v))
                if o["kind"] == "op":
                    ins = o["fn"](eng)
                    if (ename, i) in self.needed:
                        ins.then_inc(sems[ename], 1)
                elif o["kind"] == "dma":
                    eng.dma_start(out=o["out"], in_=o["in_"], **o["kw"]).then_inc(sems[o["sem"]], 16)

        with nc.Block() as block:
            @block.tensor
            def _(e):
                run("pe", e)

            @block.scalar
            def _(e):
                run("act", e)

            @block.vector
            def _(e):
                run("dve", e)

            @block.gpsimd
            def _(e):
                run("pool", e)

            @block.sync
            def _(e):
                run("sp", e)
        self.es.close()
        return {e: len(l) for e, l in self.ops.items()}


def bc(ap, n):
    return bass.AP(ap.tensor, ap.offset, list(ap.ap) + [(0, n)])


def rev(ap):
    a = list(ap.ap)
    st, cn = a[-1]
    return bass.AP(ap.tensor, ap.offset + (cn - 1) * st, a[:-1] + [(-st, cn)])


V_N1G, V_N2G, V_FNG, V_MUW, V_MUA, V_MUG = 0, 8, 16, 24, 32, 40
V_W0, V_A0, V_KK, V_KA, V_RK, V_GKB, V_ADB = 48, 56, 64, 68, 72, 76, 80
NV = 80 + 48
R_MU, R_LNG, R_LNB, R_GNG = 0, 1536, 2048, 2560
NR = 3072


def build(dbg=False):
    nc = bass.Bass("TRN2", target_bir_lowering=False)
    import os as _os
    k = K(nc, same_engine_sync=(_ENVD.get("KSES", "1") == "1"))
    DT = lambda n, s, kind="ExternalInput": nc.dram_tensor(n, list(s), F32, kind=kind).ap()
    xT = DT("xT", [8, 128, 2560])
    cond = DT("cond", [128, 8, 2])
    ada_w = DT("ada_w", [1024, 6144])
    vp = DT("vp", [128, NV])
    rp = DT("rp", [1, NR])
    w_in = DT("w_in", [1024, 3072])
    w1 = DT("w1", [2, 1024, 64]); w2 = DT("w2", [2, 64, 512])
    a1 = DT("a1", [2, 1024, 64]); a2 = DT("a2", [2, 64, 512])
    g1 = DT("g1", [1024, 128]); g2 = DT("g2", [128, 512])
    gk1 = DT("gk1", [2, 1024, 16]); gk2 = DT("gk2", [2, 16, 256])
    w_out = DT("w_out", [1024, 1024]); m1 = DT("m1", [1024, 4096]); m2 = DT("m2", [4096, 1024])
    st_r = DT("st_r", [2, 8, 64, 64]); st_g = DT("st_g", [2, 4, 64, 128])
    yT = DT("yT", [8, 128, 1536], "ExternalOutput")
    ns_r = DT("ns_r", [2, 2, 8, 64, 64], "ExternalOutput")
    ns_g = DT("ns_g", [2, 2, 4, 64, 128], "ExternalOutput")
    dbg_out = {}

    def TT(eng, out, a, b, op):
        k.op(eng, lambda e: e.tensor_tensor(out, a, b, op), reads=[a, b], writes=[out])

    def TS(eng, out, a, s1, s2, op0, op1=None):
        rd = [a] + [s for s in (s1, s2) if not isinstance(s, (int, float, type(None)))]
        if op1 is None:
            k.op(eng, lambda e: e.tensor_scalar(out, a, s1, None, op0), reads=rd, writes=[out])
        else:
            k.op(eng, lambda e: e.tensor_scalar(out, a, s1, s2, op0, op1), reads=rd, writes=[out])

    def STT(eng, out, a, s, b, op0, op1):
        rd = [a, b] + ([] if isinstance(s, (int, float)) else [s])
        k.op("dve", lambda e: e.scalar_tensor_tensor(out, a, s, b, op0, op1), reads=rd, writes=[out])

    def ACT(out, a, func, bias=None, scale=None):
        rd = [a]
        kw = {}
        if bias is not None:
            kw["bias"] = bias
            if not isinstance(bias, (int, float)):
                rd.append(bias)
        if scale is not None:
            kw["scale"] = scale
            if not isinstance(scale, (int, float)):
                rd.append(scale)
        k.op("act", lambda e: e.activation(out, a, func, **kw), reads=rd, writes=[out])

    def CP(eng, out, a):
        if eng == "act":
            k.op("act", lambda e: e.copy(out, a), reads=[a], writes=[out])
        else:
            k.op(eng, lambda e: e.tensor_copy(out, a), reads=[a], writes=[out])

    def MM(out, lhsT, rhs, start=True, stop=True):
        k.op("pe", lambda e: e.matmul(out, lhsT, rhs, start=start, stop=stop), reads=[lhsT, rhs], writes=[out])

    def TR(out, a, idn):
        k.op("pe", lambda e: e.transpose(out, a, idn), reads=[a, idn], writes=[out])

    def MS(eng, out, v):
        k.op(eng, lambda e: e.memset(out, v), writes=[out])

    def ASEL(out, pattern, cmp, base, cm):
        k.op("pool", lambda e: e.affine_select(out, out, pattern=pattern, compare_op=cmp, fill=0.0, base=base,
                                               channel_multiplier=cm), reads=[out], writes=[out])

    def SCAN(out, m, x):
        k.op("dve", lambda e: e.tensor_tensor_scan(out, m, x, 0.0, ALU.mult, ALU.add), reads=[m, x], writes=[out])

    def RSUM(eng, out, a):
        k.op(eng, lambda e: e.reduce_sum(out, a, AX.X), reads=[a], writes=[out])

    def RSQ(out, a, scale, eps_ap):
        ACT(out, a, AF.Sqrt, bias=eps_ap, scale=scale)
        k.op("dve", lambda e: e.reciprocal(out, out), reads=[out], writes=[out])

    def bcm(ap2d, n):
        return bass.AP(ap2d.tensor, ap2d.offset, [ap2d.ap[0], (0, n)] + list(ap2d.ap[1:]))

    NDUM = int(_ENVD.get("KDUM", "0"))
    NPF = 5 if NDUM else 6
    pf = [k.ps("pf%d" % i, [128, 512], F32) for i in range(NPF)]
    if NDUM:
        pdum = k.ps("pdum", [128, 512], F32)
    pb = [k.ps("pb%d" % i, [128, 1024], BF16) for i in range(2)]
    cnt = {"pf": 0, "pb": 0, "alt": 0, "stg": 0}

    def PF():
        cnt["pf"] += 1
        return pf[cnt["pf"] % NPF]

    def PB():
        cnt["pb"] += 1
        return pb[cnt["pb"] % 2]

    def ALT(a="dve", b="pool"):
        cnt["alt"] += 1
        return a if cnt["alt"] % 2 else b

    def STG():
        cnt["stg"] += 1
        return cnt["stg"] % 2

    import os

    class _Stop(Exception):
        pass

    def MARK(name):
        _MARKS.append((name, len(k.ops["pe"])))

    def CK(tag):
        if _ENVD.get("KSTOP") == tag:
            raise _Stop()

    def body():
        ident_b = k.sb("ident_b", [128, 128], BF16)
        ones_b = k.sb("ones_b", [128, 128], BF16)
        blk_b = k.sb("blk_b", [128, 128], BF16)
        blk256_f = k.sb("blk256_f", [128, 256], BF16)
        blkind_b = k.sb("blkind_b", [128, 2], BF16)
        mask2 = k.sb("mask2", [128, 2, 256], BF16)
        maskQ = k.sb("maskQ", [128, 2, 128], BF16)
        maskI = k.sb("maskI", [128, 2, 128], BF16)
        scm = k.sb("scm", [128, 257], F32)

        MS("pool", ident_b[:], 1.0)
        ASEL(ident_b[:], [[1, 128]], ALU.is_equal, 0, -1)
        MS("pool", ones_b[:], 1.0)
        if NDUM:
            k.dummy = (pdum[:, 0:int(_ENVD.get("KDUMN", "128"))], ident_b[:], ones_b[:, 0:int(_ENVD.get("KDUMN", "128"))], NDUM)
        MS("pool", blk_b[:], 0.0)
        MS("pool", blk_b[0:64, 0:64], 1.0)
        MS("pool", blk_b[64:128, 64:128], 1.0)
        MS("pool", blk256_f[:], 0.0)
        MS("pool", blk256_f[0:64, 0:128], 1.0)
        MS("pool", blk256_f[64:128, 128:256], 1.0)
        MS("pool", blkind_b[:], 0.0)
        MS("pool", blkind_b[0:64, 0:1], 1.0)
        MS("pool", blkind_b[64:128, 1:2], 1.0)
        MS("pool", mask2[:], 1.0)
        MS("pool", maskQ[:], 1.0)
        MS("pool", maskI[:], 1.0)
        ASEL(mask2[:, 0, 0:128], [[1, 128]], ALU.is_ge, -1, -1)
        ASEL(mask2[:, 0, 128:256], [[1, 128]], ALU.is_ge, 0, -1)
        ASEL(mask2[:, 1, 0:128], [[-1, 128]], ALU.is_ge, -1, 1)
        ASEL(mask2[:, 1, 128:256], [[-1, 128]], ALU.is_ge, 0, 1)
        ASEL(maskQ[:, 0, :], [[-1, 128]], ALU.is_ge, -1, 1)
        ASEL(maskQ[:, 1, :], [[1, 128]], ALU.is_ge, -1, -1)
        ASEL(maskI[:, 0, :], [[1, 128]], ALU.is_ge, 0, -1)
        ASEL(maskI[:, 1, :], [[-1, 128]], ALU.is_ge, 0, 1)
        MS("pool", scm[:], 1.0)
        for c_ in (0, 128, 256):
            MS("pool", scm[:, c_:c_ + 1], 0.0)

        epsv = k.sb("epsv", [128, 4], F32)
        MS("pool", epsv[:, 0:1], 1e-6)
        MS("pool", epsv[:, 1:2], 64e-5)
        MS("pool", epsv[:, 2:3], 1e-5)
        MS("pool", epsv[:, 3:4], 0.0)
        CK("A")
        vps = k.sb("vps", [128, NV], F32)
        k.dma(vps[:], vp)
        conds = k.sb("conds", [128, 8, 2], F32)
        k.dma(conds[:], cond)
        omka = k.sb("omka", [128, 4], F32)
        TS("dve", omka[:], vps[:, V_KA:V_KA + 4], -1.0, 1.0, ALU.mult, ALU.add)

        stg = [k.sb("stg%d" % i, [128, 8, 512], F32) for i in range(2)]
        wbf = [k.sb("wbf%d" % i, [128, 8, 512], BF16) for i in range(2)]
        xg = stg[0]
        sqb = wbf[1]

        CK("B")
        csil = k.sb("csil", [128, 8, 2], F32)
        ACT(csil[:], conds[:], AF.Sigmoid)
        TT("dve", csil[:], csil[:], conds[:], ALU.mult)
        mod = k.sb("mod", [128, 48, 2], F32)
        pm = PF()
        for blk in range(12):
            si = STG()
            k.dma(stg[si][:], ada_w[:, blk * 512:(blk + 1) * 512].rearrange("(k p) n -> p k n", p=128))
            for cc in range(4):
                ch = blk * 4 + cc
                for kc in range(8):
                    MM(pm[:, 2 * ch:2 * ch + 2], stg[si][:, kc, cc * 128:(cc + 1) * 128], csil[:, kc, :],
                       start=(kc == 0), stop=(kc == 7))
        for v_ in range(2):
            TT("dve", mod[:, :, v_], pm[:, 0:96].rearrange("p (c v) -> p c v", v=2)[:, :, v_], vps[:, V_ADB:V_ADB + 48], ALU.add)
        CK("C")
        A1 = k.sb("A1", [128, 8, 2], F32)
        A2 = k.sb("A2", [128, 8, 2], F32)
        for v_ in range(2):
            STT("dve", A1[:, :, v_], mod[:, 8:16, v_], 1.0, vps[:, V_N1G:V_N1G + 8], ALU.add, ALU.mult)
            STT("dve", A2[:, :, v_], mod[:, 32:40, v_], 1.0, vps[:, V_N2G:V_N2G + 8], ALU.add, ALU.mult)
        SH1, GT1, SH2, GT2 = 0, 16, 24, 40

        HW = 1152
        hT = k.sb("hT", [128, 8, HW], BF16)
        dhT = k.sb("dhT", [128, 8, 1024], BF16)
        tanhT = k.sb("tanhT", [128, 1024], BF16)
        a1T = k.sb("a1T", [128, 1024], BF16)
        sigT = k.sb("sigT", [128, 1024], BF16)
        gk1T = k.sb("gk1T", [64, 1024], BF16)
        mixtok = k.sb("mixtok", [128, 12, 1024], BF16)
        store = k.sb("store", [128, 16, 384], BF16)
        gam = k.sb("gam", [128, 2, 8], F32)
        vtok = k.sb("vtok", [128, 8, 256], BF16)
        prodb = k.sb("prodb", [128, 8, 128], BF16)
        Tb = k.sb("Tb", [128, 256], BF16)
        Whp = k.sb("Whp", [128, 8, 2, 448], BF16)
        L1 = Whp
        W2b = k.sb("W2b", [128, 2, 128], BF16)
        G2b = k.sb("G2b", [128, 512], BF16)
        GK2b = k.sb("GK2b", [64, 2, 128], BF16)
        arena = k.sb("arena", [128, 8192], F32)
        _ao = [0]

        def carve(n):
            a = arena[:, _ao[0]:_ao[0] + n]
            _ao[0] += n
            return a

        T = {n: carve(256) for n in ("r", "k", "kq", "kk", "sig", "a", "kd", "kd0", "be", "S", "D", "E1", "E2", "E3", "x1", "x2")}
        yacc = carve(2048).rearrange("p (a b) -> p a b", b=256)
        rows = carve(768)
        Tf = carve(256)
        Tmid_r = carve(512).rearrange("p (a b) -> p a b", b=128)
        Tmid_g = carve(512).rearrange("p (a b) -> p a b", b=256)
        assert _ao[0] <= 8192
        x1 = arena[:, 0:6144].rearrange("p (a b) -> p a b", b=768)
        vb = k.sb("vb", [128, 256], BF16)
        AR = [k.sb("AR%d" % i, [128, 2, 2, 128], BF16) for i in range(2)]
        BK = [k.sb("BK%d" % i, [128, 2, 2, 128], BF16) for i in range(2)]
        toks = [k.sb("toks%d" % i, [128, 2, 3, 128], BF16) for i in range(2)]
        XK = [k.sb("XK%d" % i, [128, 2, 2, 256], BF16) for i in range(2)]
        QP = [k.sb("QP%d" % i, [128, 2, 2, 128], BF16) for i in range(6)]
        RR = [k.sb("RR%d" % i, [128, 2, 128], BF16) for i in range(4)]
        ZR = [k.sb("ZR%d" % i, [128, 2, 64], BF16) for i in range(2)]
        ZS = [k.sb("ZS%d" % i, [128, 2, 2, 64], BF16) for i in range(2)]
        sm = k.sb("sm", [128, 16], F32)
        rstd = arena[:, 6144:6656]
        ntmp = arena[:, 6656:7168]

        def load_l1():
            k.dma(rows[:, 640:768], rp[:, R_GNG:R_GNG + 128].partition_broadcast(128))
            si = STG()
            for d in range(2):
                k.dma(stg[si][:, :, d * 64:(d + 1) * 64], w1[d].rearrange("(k p) n -> p k n", p=128))
                k.dma(stg[si][:, :, 128 + d * 64:128 + (d + 1) * 64], a1[d].rearrange("(k p) n -> p k n", p=128))
            k.dma(stg[si][:, :, 256:384], g1.rearrange("(k p) n -> p k n", p=128))
            MS("pool", stg[si][:, :, 384:448], 0.0)
            for d in range(2):
                k.dma(stg[si][:, :, 384 + 32 * d:384 + 32 * d + 16], gk1[d].rearrange("(k p) n -> p k n", p=128))
            for kc in range(8):
                CP("act" if kc % 2 else "dve", L1[:, kc, 0, :], stg[si][:, kc, 0:448])
                for j, vo in enumerate((V_MUW, V_MUA, V_MUG)):
                    TS("dve", L1[:, kc, 1, j * 128:(j + 1) * 128], stg[si][:, kc, j * 128:(j + 1) * 128],
                       vps[:, vo + kc:vo + kc + 1], None, ALU.mult)

        si = STG()
        k.dma(stg[si][:, 0, :], g2)
        CP("pool", G2b[:], stg[si][:, 0, :])

        def mixer(P):
            nt = P["nt"]
            pc = P["pc"]
            mv = P["mv"]
            x0 = P["x0"]
            own = P["own"]
            g0 = P["g0"]
            dirs = P["dirs"]
            MARK(P["name"] + ":norm")
            load_l1()
            for (a_, b_) in P["pads"]:
                MS("pool", hT[:, :, a_:b_], 0.0)
            for (pcol, xcol, n) in P["ngroups"]:
                k.dma(xg[:, :, 0:n], xT[:, :, xcol:xcol + n].rearrange("k p t -> p k t"))
                for kc in range(8):
                    ACT(sqb[:, kc, 0:n], xg[:, kc, 0:n], AF.Square)
                pss = PF()
                for kc in range(8):
                    MM(pss[:, 0:n], ones_b[:], sqb[:, kc, 0:n], start=(kc == 0), stop=(kc == 7))
                RSQ(rstd[:, 0:n], pss[:, 0:n], 1.0 / 1024, epsv[:, 0:1])
                for kc in range(8):
                    e_ = ALT()
                    TT(e_, xg[:, kc, 0:n], xg[:, kc, 0:n], rstd[:, 0:n], ALU.mult)
                    ACT(hT[:, kc, pcol:pcol + n], xg[:, kc, 0:n], AF.Identity,
                        bias=mod[:, SH1 + kc, mv:mv + 1], scale=A1[:, kc, mv:mv + 1])
            CK("M1")
            MARK(P["name"] + ":lora1")
            for (t0g, ntile) in P["groups"]:
                for sc in range(t0g // 2, (t0g + ntile) // 2):
                    t0 = 2 * sc
                    cc = pc[t0]
                    e_ = "dve"
                    acc = (arena[:, 0:2048] if sc % 2 == 0 else arena[:, 2048:4096]).rearrange("p (k t) -> p k t", t=256)
                    hh = hT[:, :, cc:cc + 256]
                    if P["kind"] == "seq":
                        TT(e_, acc, hT[:, :, cc - 1:cc + 255], hT[:, :, cc + 1:cc + 257], ALU.add)
                        sc0 = 0.5
                    else:
                        TT("pool", acc, hT[:, :, cc - 64:cc + 192], hT[:, :, cc + 64:cc + 320], ALU.add)
                        a4 = acc.rearrange("p k (r c) -> p k r c", c=64)
                        h4 = hh.rearrange("p k (r c) -> p k r c", c=64)
                        TT(e_, a4[:, :, :, 1:64], a4[:, :, :, 1:64], h4[:, :, :, 0:63], ALU.add)
                        TT(e_, a4[:, :, :, 0:63], a4[:, :, :, 0:63], h4[:, :, :, 1:64], ALU.add)
                        sc0 = 0.25
                    STT("dve", dhT[:, :, 128 * t0:128 * t0 + 256], acc, sc0, hh, ALU.mult, ALU.subtract)
                t0 = t0g
                n = 128 * ntile
                hs = lambda kc: hT[:, kc, pc[t0]:pc[t0] + n]
                ds = lambda kc: dhT[:, kc, 128 * t0:128 * t0 + n]
                cs = slice(128 * t0, 128 * t0 + n)
                for j in range(3):
                    if j == 2 and not own:
                        continue
                    pp = PF()
                    for kc in range(8):
                        MM(pp[:, 0:n], L1[:, kc, 0, j * 128:(j + 1) * 128], hs(kc), start=(kc == 0), stop=False)
                    for kc in range(8):
                        MM(pp[:, 0:n], L1[:, kc, 1, j * 128:(j + 1) * 128], ds(kc), start=False, stop=(kc == 7))
                    if j == 0:
                        ACT(tanhT[:, cs], pp[:, 0:n], AF.Tanh)
                    elif j == 1:
                        CP("act", a1T[:, cs], pp[:, 0:n])
                    else:
                        ACT(sigT[:, cs], pp[:, 0:n], AF.Sigmoid)
                pp = PF()
                for kc in range(8):
                    MM(pp[0:64, 0:n], L1[:, kc, 0, 384:448], hs(kc), start=(kc == 0), stop=(kc == 7))
                CP("act", gk1T[:, cs], pp[0:64, 0:n])

            CK("M3")

            def init_state(kind, d, width, dram_blocks, mid):
                tf = Tf[:, 0:width]
                if kind == "mid":
                    CP("pool", tf, mid)
                else:
                    MS("pool", tf, 0.0)
                    if kind == "dram":
                        bw = width // 2
                        for e in range(2):
                            k.dma(Tf[64 * e:64 * e + 64, bw * e:bw * (e + 1)], dram_blocks[e])
                CP("act", Tb[:, 0:width], tf)

            wb0 = wbf[0][:].rearrange("p a b -> p (a b)")
            wb1 = wbf[1][:].rearrange("p a b -> p (a b)")

            def carve_job(wb, o):
                xk_ = wb[:, o:o + 1024].rearrange("p (e m c) -> p e m c", e=2, m=2)
                o += 1024
                qp_ = []
                for _q in range(3):
                    qp_.append(wb[:, o:o + 512].rearrange("p (e m c) -> p e m c", e=2, m=2))
                    o += 512
                rb_ = []
                for _q in range(2):
                    rb_.append(wb[:, o:o + 256].rearrange("p (e c) -> p e c", e=2))
                    o += 256
                zr_ = wb[:, o:o + 128].rearrange("p (e c) -> p e c", e=2)
                o += 128
                zs_ = wb[:, o:o + 256].rearrange("p (m e c) -> p m e c", m=2, e=2)
                return dict(xk=xk_, qp=qp_, rb=rb_, zr=zr_, zs=zs_)

            JB = [dict(xk=XK[i][:], qp=[q[:] for q in QP[3 * i:3 * i + 3]], rb=[r_[:] for r_ in RR[2 * i:2 * i + 2]], zr=ZR[i][:], zs=ZS[i][:]) for i in range(2)]
            JB.append(carve_job(wb0, 0))
            JB.append(carve_job(wb1, 256))
            SETS = [[(AR[di][:], BK[di][:], toks[di][:]) for di in range(2)], []]
            for di in range(2):
                ar_b = mixtok[:, 4 + di, 512:1024].rearrange("p (a b c) -> p a b c", a=2, b=2)
                bk_b = mixtok[:, 6 + di, 512:1024].rearrange("p (a b c) -> p a b c", a=2, b=2)
                tk_b = mixtok[:, 8 + 2 * di:10 + 2 * di, 512:896].rearrange("p i (j c) -> p i j c", j=3)
                SETS[1].append((ar_b, bk_b, tk_b))
            SETSF = [SETS[0][0], SETS[0][1], SETS[1][0], SETS[1][1]]

            def wprep_steps(hp):
                si = STG()
                for j in range(3):
                    k.dma(stg[si][:, :, j * 128:(j + 1) * 128],
                          w_in[:, j * 512 + hp * 128:j * 512 + (hp + 1) * 128].rearrange("(k p) n -> p k n", p=128))
                for j in range(3):
                    k.dma(rows[:, j * 128:(j + 1) * 128],
                          rp[:, R_MU + j * 512 + hp * 128:R_MU + j * 512 + (hp + 1) * 128].partition_broadcast(128))
                si2 = STG()
                for d in range(2):
                    k.dma(stg[si2][64 * d:64 * d + 64, 0, 0:128], w2[d][:, hp * 128:(hp + 1) * 128])
                    k.dma(stg[si2][64 * d:64 * d + 64, 0, 128:256], a2[d][:, hp * 128:(hp + 1) * 128])
                yield
                for kc in range(8):
                    CP("act" if kc % 2 else "dve", Whp[:, kc, 0, 0:384], stg[si][:, kc, 0:384])
                    TT("dve", Whp[:, kc, 1, 0:384], stg[si][:, kc, 0:384], rows[:, 0:384], ALU.mult)
                    if kc % 2:
                        yield
                CP("pool", W2b[:], stg[si2][:, 0, 0:256].rearrange("p (a b) -> p a b", b=128))
                yield

            def gla_wprep_steps(gp):
                si = STG()
                k.dma(stg[si][:, :, 0:128], w_in[:, 1536 + gp * 128:1536 + (gp + 1) * 128].rearrange("(k p) n -> p k n", p=128))
                k.dma(stg[si][:, :, 128:256], w_in[:, 1792 + gp * 128:1792 + (gp + 1) * 128].rearrange("(k p) n -> p k n", p=128))
                MS("pool", stg[si][0:64, 0, 256:384], 0.0)
                for d in range(2):
                    k.dma(stg[si][32 * d:32 * d + 16, 0, 256:384], gk2[d][:, gp * 128:(gp + 1) * 128])
                si2 = STG()
                k.dma(stg[si2][:, :, 0:256], w_in[:, 2048 + gp * 256:2048 + (gp + 1) * 256].rearrange("(k p) n -> p k n", p=128))
                k.dma(stg[si2][:, :, 256:512], w_in[:, 2560 + gp * 256:2560 + (gp + 1) * 256].rearrange("(k p) n -> p k n", p=128))
                yield
                for kc in range(8):
                    CP("act" if kc % 2 else "dve", Whp[:, kc, gp, 0:256], stg[si][:, kc, 0:256])
                    CP("dve" if kc % 2 else "act", wbf[gp][:, kc, :], stg[si2][:, kc, :])
                    if kc % 2:
                        yield
                CP("pool", GK2b[:, gp, :], stg[si][0:64, 0, 256:384])
                yield

            for hp in range(4):
                MARK(P["name"] + ":rwkv%d" % hp)
                if hp == 0:
                    for _ in wprep_steps(0):
                        pass
                kkv = vps[:, V_KK + hp:V_KK + hp + 1]
                kav = vps[:, V_KA + hp:V_KA + hp + 1]
                rkv_ = vps[:, V_RK + hp:V_RK + hp + 1]
                CK("M3a")

                def proj_steps(sc):
                    t0 = 2 * sc
                    hs = lambda kc: hT[:, kc, pc[t0]:pc[t0] + 256]
                    ds = lambda kc: dhT[:, kc, 128 * t0:128 * t0 + 256]
                    dst = [T["r"], T["k"], vb[:]]
                    for j in range(3):
                        if j == 0 and not own:
                            continue
                        pp = PF()
                        for kc in range(8):
                            MM(pp[:, 0:256], Whp[:, kc, 0, j * 128:(j + 1) * 128], hs(kc), start=(kc == 0), stop=False)
                        for kc in range(8):
                            MM(pp[:, 0:256], Whp[:, kc, 1, j * 128:(j + 1) * 128], ds(kc), start=False, stop=(kc == 7))
                        CP("act", dst[j], pp[:, 0:256])
                        yield
                    TS("dve", T["kq"], T["k"], kkv, None, ALU.mult)
                    ACT(sqb[:, 0, 0:256], T["kq"], AF.Square)
                    pp = PF()
                    MM(pp[:, 0:256], blk_b[:], sqb[:, 0, 0:256])
                    TS("dve", T["x1"], pp[:, 0:256], 1e-12, None, ALU.max)
                    yield
                    RSQ(T["x1"], T["x1"], 1.0, epsv[:, 3:4])
                    TT("dve", T["kk"], T["kq"], T["x1"], ALU.mult)
                    pt = PB()
                    for i in range(2):
                        TR(pt[:, i * 128:(i + 1) * 128], vb[:, i * 128:(i + 1) * 128], ident_b[:])
                    CP("dve", vtok[:, t0:t0 + 2, 0:128], pt[:, 0:256].rearrange("p (a b) -> p a b", b=128))
                    yield

                def prep_steps(sc, d, par):
                    t0 = 2 * sc
                    cs = slice(128 * t0, 128 * t0 + 256)
                    ar, bk, tk = SETSF[par]
                    pz = PF()
                    MM(pz[:, 0:256], W2b[64 * d:64 * d + 64, 0, :], tanhT[64 * d:64 * d + 64, cs])
                    MM(pz[:, 256:512], W2b[64 * d:64 * d + 64, 1, :], a1T[64 * d:64 * d + 64, cs])
                    ACT(T["sig"], pz[:, 0:256], AF.Sigmoid, bias=vps[:, V_W0 + d * 4 + hp:V_W0 + d * 4 + hp + 1])
                    ACT(T["a"], pz[:, 256:512], AF.Sigmoid, bias=vps[:, V_A0 + d * 4 + hp:V_A0 + d * 4 + hp + 1])
                    yield
                    kd = T["kd0"] if d == 0 else T["kd"]
                    TS("pool", T["x2"], T["a"], kav, omka[:, hp:hp + 1], ALU.mult, ALU.add)
                    TT("pool", kd, T["x2"], T["k"], ALU.mult)
                    TT("pool", T["be"], T["kk"], T["a"], ALU.mult)
                    if d == 0:
                        SCAN(T["S"], scm[:, 0:256], T["sig"])
                    else:
                        SCAN(rev(T["S"]), rev(scm[:, 1:257]), rev(T["sig"]))
                    TT("dve", T["D"], T["S"], T["sig"], ALU.subtract)
                    yield
                    ACT(T["E1"], T["S"], AF.Exp, scale=-CW)
                    ACT(T["E2"], T["S"], AF.Exp, scale=CW)
                    ACT(T["E3"], T["D"], AF.Exp, scale=-CW)
                    yield
                    v3 = lambda t: t.rearrange("p (a b) -> p a b", b=128)
                    STT("dve", ar[:, :, 0, :], v3(T["kk"]), -1.0, v3(T["E3"]), ALU.mult, ALU.mult)
                    TT("pool", ar[:, :, 1, :], v3(T["r"]), v3(T["E1"]), ALU.mult)
                    TT("dve", bk[:, :, 0, :], v3(T["be"]), v3(T["E2"]), ALU.mult)
                    TT("pool", bk[:, :, 1, :], v3(kd), v3(T["E2"]), ALU.mult)
                    gcol = 127 if d == 0 else 0
                    CP("pool", gam[:, d, t0:t0 + 2], v3(T["E1"])[:, :, gcol])
                    yield
                    if d == 1 and 0 in dirs:
                        for i in range(2):
                            if (t0 + i) in own:
                                oi = own.index(t0 + i)
                                TT("pool", T["x2"][:, 0:128], T["kd0"][:, i * 128:(i + 1) * 128], T["kd"][:, i * 128:(i + 1) * 128], ALU.add)
                                STT("pool", prodb[:, oi, :], T["x2"][:, 0:128], rkv_, T["r"][:, i * 128:(i + 1) * 128], ALU.mult, ALU.mult)
                    for i in range(2):
                        pt = PB()
                        TR(pt[:, 0:128], ar[:, i, 0, :], ident_b[:])
                        TR(pt[:, 128:256], bk[:, i, 0, :], ident_b[:])
                        TR(pt[:, 256:384], bk[:, i, 1, :], ident_b[:])
                        CP("act", tk[:, i, :, :], pt[:, 0:384].rearrange("p (a b) -> p a b", b=128))
                        yield

                def chunk_steps(group):
                    jobs = []
                    for gi_, (sc_, d_, si_) in enumerate(group):
                        for i in range(2):
                            jb = JB[2 * gi_ + i]
                            st_ = SETSF[si_]
                            jobs.append(dict(i=i, d=d_, tile=2 * sc_ + i, cd=d_ * 8 + 2 * sc_ + i, ar=st_[0], bk=st_[1], tk=st_[2],
                                             ev=("dve" if (2 * gi_ + i) == 2 * len(group) - 1 else "act"), **jb))
                    for J in jobs:
                        i, d, ar, bk, tk = J["i"], J["d"], J["ar"], J["bk"], J["tk"]
                        J["bE"] = [PF(), PF()]
                        J["bQ"] = [PF(), PF()]
                        for e in range(2):
                            hsl = slice(64 * e, 64 * e + 64)
                            arf = ar[hsl, i, :, :].rearrange("p a b -> p (a b)")
                            MM(J["bE"][e][:, 0:256], bk[hsl, i, 0, :], arf)
                            MM(J["bE"][e][:, 256:512], bk[hsl, i, 1, :], arf)
                            MM(J["bQ"][e][:, 0:128], ar[hsl, i, 0, :], bk[hsl, i, 0, :])
                        xk = J["xk"]
                        for e in range(2):
                            TT("dve", xk[:, e, :, :], J["bE"][e][:, 0:512].rearrange("p (a b) -> p a b", b=256), bcm(mask2[:, d, :], 2), ALU.mult)
                            TT("dve", J["qp"][0][:, e, 0, :], J["bQ"][e][:, 0:128], maskQ[:, d, :], ALU.mult)
                        yield
                    for J in jobs:
                        xk = J["xk"]
                        J["rr"] = J["rb"][0]
                        TT("pool", J["rr"][:], xk[:, :, 0, 0:128], bcm(ident_b[:], 2), ALU.add)
                        J["Pm"] = [xk[:, e, 0, 0:128] for e in range(2)]
                        J["Qm"] = [J["qp"][0][:, e, 0, :] for e in range(2)]
                    for lv in range(1, 7):
                        for J in jobs:
                            pq = PF()
                            J["pq"] = pq
                            for e in range(2):
                                MM(pq[:, e * 256:e * 256 + 128], J["Pm"][e], J["Qm"][e])
                                if lv < 6:
                                    MM(pq[:, e * 256 + 128:e * 256 + 256], J["Qm"][e], J["Pm"][e])
                        for J in jobs:
                            qn = J["qp"][1 + (lv % 2)]
                            pq4 = J["pq"][:, 0:512].rearrange("p (a b c) -> p a b c", b=2, c=128)
                            ee_ = J["ev"]
                            if lv < 6:
                                CP(ee_, qn[:], pq4)
                            else:
                                CP(ee_, qn[:, :, 0, :], pq4[:, :, 0, :])
                            J["Qm"] = [qn[:, e, 0, :] for e in range(2)]
                            J["Pm"] = [qn[:, e, 1, :] for e in range(2)]
                        yield
                        for J in jobs:
                            prr = PF()
                            J["prr"] = prr
                            for e in range(2):
                                MM(prr[:, e * 128:(e + 1) * 128], J["Qm"][e], J["rr"][:, e, :])
                        for J in jobs:
                            rn = J["rb"][lv % 2]
                            TT("dve", rn[:], J["prr"][:, 0:256].rearrange("p (a b) -> p a b", b=128), J["rr"][:], ALU.add)
                            J["rr"] = rn
                        yield
                    for J in jobs:
                        pw = PF()
                        J["pw"] = pw
                        for e in range(2):
                            MM(pw[:, e * 64:(e + 1) * 64], J["xk"][:, e, 1, 0:128], vtok[:, J["tile"], 64 * e:64 * e + 64])
                    for J in jobs:
                        CP(J["ev"], J["zr"][:], J["pw"][:, 0:128].rearrange("p (a b) -> p a b", b=64))
                    yield
                    for J in jobs:
                        pzz = PF()
                        J["pzz"] = pzz
                        for e in range(2):
                            MM(pzz[:, e * 64:(e + 1) * 64], J["rr"][:, e, :], J["tk"][:, J["i"], 0, 64 * e:64 * e + 64])
                            MM(pzz[:, 128 + e * 64:128 + (e + 1) * 64], J["rr"][:, e, :], J["zr"][:, e, :])
                    for J in jobs:
                        CP(J["ev"], J["zs"][:], J["pzz"][:, 0:256].rearrange("p (a b c) -> p a b c", b=2, c=64))
                        J["Atok"] = J["zs"][:, 0, :, :].rearrange("p a b -> p (a b)")
                        J["U0"] = J["zs"][:, 1, :, :].rearrange("p a b -> p (a b)")
                    yield
                    for J in jobs:
                        i, tile, tk = J["i"], J["tile"], J["tk"]
                        pg1 = PF()
                        J["pg1"] = pg1
                        MM(pg1[:, 0:128], J["Atok"], tk[:, i, 1, :], start=True, stop=False)
                        MM(pg1[:, 0:128], ident_b[:], ident_b[:], start=False, stop=True)
                        MM(pg1[:, 128:256], tk[:, i, 1, :], J["U0"], start=True, stop=False)
                        MM(pg1[:, 128:256], tk[:, i, 2, :], vtok[:, tile, 0:128], start=False, stop=True)
                        if tile in own:
                            for e in range(2):
                                MM(pg1[:, 256 + e * 128:256 + (e + 1) * 128], J["Atok"], J["xk"][:, e, 0, 128:256])
                    for J in jobs:
                        i, tile, cd, d, ar = J["i"], J["tile"], J["cd"], J["d"], J["ar"]
                        pg1 = J["pg1"]
                        gsc = gam[:, d, tile:tile + 1]
                        TT("dve", store[:, cd, 0:128], pg1[:, 0:128], blk_b[:], ALU.mult)
                        STT("dve", store[:, cd, 256:384], pg1[:, 128:256], gsc, blk_b[:], ALU.mult, ALU.mult)
                        if tile in own:
                            for e in range(2):
                                hsl = slice(64 * e, 64 * e + 64)
                                TT("dve", store[hsl, cd, 128:256], pg1[hsl, 256 + e * 128:256 + (e + 1) * 128], ar[hsl, i, 1, :], ALU.add)
                    yield
                    for J in jobs:
                        i, tile = J["i"], J["tile"]
                        if tile in own:
                            py = PF()
                            J["py"] = py
                            for e in range(2):
                                MM(py[:, e * 64:(e + 1) * 64], J["xk"][:, e, 0, 128:256], J["zs"][:, 1, e, :], start=True, stop=False)
                                MM(py[:, e * 64:(e + 1) * 64], J["xk"][:, e, 1, 128:256], vtok[:, tile, 64 * e:64 * e + 64], start=False, stop=True)
                    for J in jobs:
                        tile = J["tile"]
                        if tile in own:
                            oi = own.index(tile)
                            if J["d"] == dirs[0]:
                                CP("act", yacc[:, oi, 0:128], J["py"][:, 0:128])
                            else:
                                TT("dve", yacc[:, oi, 0:128], J["py"][:, 0:128], yacc[:, oi, 0:128], ALU.add)
                    yield

                import itertools
                units = [(sc, d) for sc in range(nt // 2) for d in dirs]
                groups = [[(sc, d, (2 * g_ + j_) % 4) for j_, (sc, d) in enumerate(units[2 * g_:2 * g_ + 2])] for g_ in range((len(units) + 1) // 2)]

                def P_of(group):
                    its = []
                    seen = set()
                    for (sc, d, si) in group:
                        if d == dirs[0] and sc not in seen:
                            its.append(proj_steps(sc))
                            seen.add(sc)
                        its.append(prep_steps(sc, d, si))
                    return itertools.chain(*its)

                for _ in P_of(groups[0]):
                    pass
                for gi, group in enumerate(groups):
                    C = chunk_steps(group)
                    if gi + 1 < len(groups):
                        Pn = P_of(groups[gi + 1])
                    else:
                        Pn = wprep_steps(hp + 1) if hp < 3 else iter(())
                    ca = pa_ = True
                    while ca or pa_:
                        if ca:
                            try:
                                next(C)
                            except StopIteration:
                                ca = False
                        if pa_:
                            try:
                                next(Pn)
                            except StopIteration:
                                pa_ = False
                MARK(P["name"] + ":rseq%d" % hp)
                CK("M8")
                for d in dirs:
                    for ci, chain in enumerate(P["chains"]):
                        init_state(P["init"][d], d, 128, [st_r[d, 2 * hp + e] for e in range(2)], Tmid_r[:, hp, :])
                        order = chain if d == 0 else chain[::-1]
                        pend = None
                        for tile in order:
                            cd = d * 8 + tile
                            ptt = PF()
                            MM(ptt[:, 0:128], store[:, cd, 0:128], Tb[:, 0:128])
                            this = None
                            if tile in own:
                                MM(ptt[:, 128:256], store[:, cd, 128:256], Tb[:, 0:128])
                                this = (ptt, own.index(tile))
                            STT("dve", Tf[:, 0:128], ptt[:, 0:128], gam[:, d, tile:tile + 1], store[:, cd, 256:384], ALU.mult, ALU.add)
                            CP("act", Tb[:, 0:128], Tf[:, 0:128])
                            if pend is not None:
                                TT("dve", yacc[:, pend[1], 0:128], pend[0][:, 128:256], yacc[:, pend[1], 0:128], ALU.add)
                            pend = this
                        if pend is not None:
                            TT("dve", yacc[:, pend[1], 0:128], pend[0][:, 128:256], yacc[:, pend[1], 0:128], ALU.add)
                        if P["end"][d] == "out":
                            for e in range(2):
                                k.dma(ns_r[ci, d, 2 * hp + e], Tf[64 * e:64 * e + 64, 64 * e:64 * e + 64], is_output=True)
                        elif P["end"][d] == "mid":
                            CP("pool", Tmid_r[:, hp, :], Tf[:, 0:128])
                MARK(P["name"] + ":rfin%d" % hp)
                k.dma(rows[:, 384:512], rp[:, R_LNG + hp * 128:R_LNG + (hp + 1) * 128].partition_broadcast(128))
                k.dma(rows[:, 512:640], rp[:, R_LNB + hp * 128:R_LNB + (hp + 1) * 128].partition_broadcast(128))
                CK("M9")
                n_ = len(own)
                if n_:
                    assert own == list(range(n_))
                    yv = yacc[:, 0:n_, 0:128]
                    y4 = yv.rearrange("p n (a b) -> p n a b", b=64)
                    big1 = arena[:, 0:n_ * 128].rearrange("p (n c) -> p n c", c=128)
                    big2 = arena[:, 1024:1024 + n_ * 128].rearrange("p (n c) -> p n c", c=128)
                    b14 = big1.rearrange("p n (a b) -> p n a b", b=64)
                    b24 = big2.rearrange("p n (a b) -> p n a b", b=64)
                    st = lambda j: arena[:, 2048 + 16 * j:2048 + 16 * j + 2 * n_]
                    st3 = lambda j: st(j).rearrange("p (n a) -> p n a", a=2)
                    RSUM("dve", st3(0), y4)
                    ACT(big1, yv, AF.Square)
                    RSUM("dve", st3(1), b14)
                    TS("dve", st(2), st(0), 1.0 / 64, None, ALU.mult)
                    TT("dve", st(3), st(2), st(2), ALU.mult)
                    STT("dve", st(4), st(1), 1.0 / 64, st(3), ALU.mult, ALU.subtract)
                    RSQ(st(4), st(4), 1.0, epsv[:, 1:2])
                    TT("dve", b14, y4, bc(st3(2), 64), ALU.subtract)
                    TT("dve", b14, b14, bc(st3(4), 64), ALU.mult)
                    TT("dve", big1, big1, bcm(rows[:, 384:512], n_), ALU.mult)
                    TT("dve", big1, big1, bcm(rows[:, 512:640], n_), ALU.add)
                    pbn = PF()
                    for oi in range(n_):
                        MM(pbn[:, 2 * oi:2 * oi + 2], prodb[:, oi, :], blkind_b[:])
                    CP("act", st(5), pbn[:, 0:2 * n_])
                    TT("dve", b24, vtok[:, 0:n_, 0:128].rearrange("p n (a b) -> p n a b", b=64), bc(st3(5), 64), ALU.mult)
                    TT("dve", big1, big1, big2, ALU.add)
                    for g_ in range(n_ // 4):
                        pgt = PF()
                        for j in range(4):
                            tile = 4 * g_ + j
                            MM(pgt[:, j * 128:(j + 1) * 128], sigT[:, 128 * tile:128 * tile + 128], G2b[:, hp * 128:(hp + 1) * 128])
                        TT("dve", mixtok[:, g0 + 4 * g_:g0 + 4 * g_ + 4, hp * 128:(hp + 1) * 128], big1[:, 4 * g_:4 * g_ + 4, :],
                           pgt[:, 0:512].rearrange("p (n c) -> p n c", c=128), ALU.mult)

            for _ in gla_wprep_steps(0):
                pass
            CK("M10")
            for gp in range(2):
                MARK(P["name"] + ":gla%d" % gp)
                wq = Whp[:, :, gp, :]
                wv = wbf[gp]
                gkw = GK2b[:, gp, :]
                def gproj_steps(sc):
                    t0 = 2 * sc
                    hs = lambda kc: hT[:, kc, pc[t0]:pc[t0] + 256]
                    pq_ = PF()
                    if own:
                        for kc in range(8):
                            MM(pq_[:, 0:256], wq[:, kc, 0:128], hs(kc), start=(kc == 0), stop=(kc == 7))
                    for kc in range(8):
                        MM(pq_[:, 256:512], wq[:, kc, 128:256], hs(kc), start=(kc == 0), stop=(kc == 7))
                    if own:
                        k.op("act", lambda e, o=T["r"], a=pq_[:, 0:256]: e.mul(o, a, 0.125), reads=[pq_[:, 0:256]], writes=[T["r"]])
                    CP("act", T["k"], pq_[:, 256:512])
                    yield
                    for i in range(2):
                        tile = t0 + i
                        pv = PF()
                        for kc in range(8):
                            MM(pv[:, 0:256], hT[:, kc, pc[tile]:pc[tile] + 128], wv[:, kc, 0:256], start=(kc == 0), stop=(kc == 7))
                        CP("act", vtok[:, tile, :], pv[:, 0:256])
                        yield

                def gprep_steps(sc, d, par):
                    t0 = 2 * sc
                    cs = slice(128 * t0, 128 * t0 + 256)
                    ar, bk, tk = G4[par]
                    Tn = GT_[dirs.index(d)]
                    pz = PF()
                    MM(pz[:, 0:256], gkw[32 * d:32 * d + 16, :], gk1T[32 * d:32 * d + 16, cs])
                    ACT(Tn["sig"], pz[:, 0:256], AF.Sigmoid, bias=vps[:, V_GKB + d * 2 + gp:V_GKB + d * 2 + gp + 1])
                    ACT(Tn["a"], Tn["sig"], AF.Ln)
                    yield
                    if d == 0:
                        SCAN(Tn["S"], scm[:, 0:256], Tn["a"])
                    else:
                        SCAN(rev(Tn["S"]), rev(scm[:, 1:257]), rev(Tn["a"]))
                    yield
                    ACT(Tn["E1"], Tn["S"], AF.Exp, scale=1.0 / 16)
                    ACT(Tn["E2"], Tn["S"], AF.Exp, scale=-1.0 / 16)
                    yield
                    v3 = lambda t: t.rearrange("p (a b) -> p a b", b=128)
                    TT("dve", ar[:, :, 0, :], v3(T["r"]), v3(Tn["E1"]), ALU.mult)
                    TT("pool", bk[:, :, 0, :], v3(T["k"]), v3(Tn["E2"]), ALU.mult)
                    gcol = 127 if d == 0 else 0
                    CP("pool", gam[:, d, t0:t0 + 2], v3(Tn["E1"])[:, :, gcol])
                    yield
                    for i in range(2):
                        pt = PB()
                        TR(pt[:, 0:128], bk[:, i, 0, :], ident_b[:])
                        CP("act", tk[:, i, 0, :], pt[:, 0:128])
                        yield

                def gchunk_steps(sc, d, par):
                    t0 = 2 * sc
                    ar, bk, tk = G4[par]
                    gj = [dict(i=i, tile=t0 + i, cd=d * 8 + t0 + i, at=XK[i]) for i in range(2)]
                    for J in gj:
                        ph = PF()
                        J["ph"] = ph
                        MM(ph[:, 0:256], tk[:, J["i"], 0, :], vtok[:, J["tile"], :])
                        if J["tile"] in own:
                            J["pa"] = [PF(), PF()]
                            for e in range(2):
                                hsl = slice(64 * e, 64 * e + 64)
                                MM(J["pa"][e][:, 0:128], bk[hsl, J["i"], 0, :], ar[hsl, J["i"], 0, :])
                        STT("dve", store[:, J["cd"], 128:384], ph[:, 0:256], gam[:, d, J["tile"]:J["tile"] + 1], blk256_f[:], ALU.mult, ALU.mult)
                        if J["tile"] in own:
                            for e in range(2):
                                TT("dve", J["at"][:, e, 0, 0:128], J["pa"][e][:, 0:128], maskI[:, d, :], ALU.mult)
                            CP("pool", store[:, J["cd"], 0:128], ar[:, J["i"], 0, :])
                        yield
                    for J in gj:
                        if J["tile"] in own:
                            phy = PF()
                            J["phy"] = phy
                            for e in range(2):
                                MM(phy[:, e * 128:(e + 1) * 128], J["at"][:, e, 0, 0:128], vtok[:, J["tile"], 128 * e:128 * e + 128])
                    for J in gj:
                        if J["tile"] in own:
                            oi = own.index(J["tile"])
                            if d == dirs[0]:
                                CP("act", yacc[:, oi, :], J["phy"][:, 0:256])
                            else:
                                TT("dve", yacc[:, oi, :], J["phy"][:, 0:256], yacc[:, oi, :], ALU.add)
                    yield

                G4 = [(AR[0][:], BK[0][:], toks[0][:]), (AR[1][:], BK[1][:], toks[1][:]),
                      (QP[0][:], QP[2][:], RR[0][:].rearrange("p i (j c) -> p i j c", j=1)),
                      (QP[1][:], QP[3][:], RR[1][:].rearrange("p i (j c) -> p i j c", j=1))]
                GT_ = [dict(sig=T["sig"], a=T["a"], S=T["S"], E1=T["E1"], E2=T["E2"]),
                       dict(sig=T["kd"], a=T["kd0"], S=T["be"], E1=T["D"], E2=T["E3"])]

                def rrobin(its):
                    its = list(its)
                    while its:
                        nxt = []
                        for it in its:
                            try:
                                next(it)
                                nxt.append(it)
                                yield
                            except StopIteration:
                                pass
                        its = nxt

                def GP_of(sc):
                    return itertools.chain(gproj_steps(sc), rrobin([gprep_steps(sc, d, (2 * sc + di) % 4) for di, d in enumerate(dirs)]))

                def GC_of(sc):
                    return itertools.chain(*[gchunk_steps(sc, d, (2 * sc + di) % 4) for di, d in enumerate(dirs)])

                for _ in GP_of(0):
                    pass
                for sc in range(nt // 2):
                    C = GC_of(sc)
                    if sc + 1 < nt // 2:
                        Pn = GP_of(sc + 1)
                    else:
                        Pn = gla_wprep_steps(1) if gp == 0 else iter(())
                    ca = pa_ = True
                    while ca or pa_:
                        if ca:
                            try:
                                next(C)
                            except StopIteration:
                                ca = False
                        if pa_:
                            try:
                                next(Pn)
                            except StopIteration:
                                pa_ = False
                for d in dirs:
                    for ci, chain in enumerate(P["chains"]):
                        init_state(P["init"][d], d, 256, [st_g[d, 2 * gp + e] for e in range(2)], Tmid_g[:, gp, :])
                        order = chain if d == 0 else chain[::-1]
                        Tfa = [Tf, T["x1"]]
                        Tbs = [vb[:], Tb[:]]
                        cur = 0
                        pend = None
                        for ci_, tile in enumerate(order):
                            cd = d * 8 + tile
                            this = None
                            if tile in own:
                                ptt = PF()
                                MM(ptt[:, 0:256], store[:, cd, 0:128], Tbs[(ci_ - 1) % 2])
                                this = (ptt, own.index(tile))
                            STT("dve", Tfa[1 - cur], Tfa[cur], gam[:, d, tile:tile + 1], store[:, cd, 128:384], ALU.mult, ALU.add)
                            CP("act", Tbs[ci_ % 2], Tfa[1 - cur])
                            cur ^= 1
                            if pend is not None:
                                TT("dve", yacc[:, pend[1], :], pend[0][:, 0:256], yacc[:, pend[1], :], ALU.add)
                            pend = this
                        if pend is not None:
                            TT("dve", yacc[:, pend[1], :], pend[0][:, 0:256], yacc[:, pend[1], :], ALU.add)
                        Tfin = Tfa[cur]
                        if P["end"][d] == "out":
                            for e in range(2):
                                k.dma(ns_g[ci, d, 2 * gp + e], Tfin[64 * e:64 * e + 64, 128 * e:128 * e + 128], is_output=True)
                        elif P["end"][d] == "mid":
                            CP("pool", Tmid_g[:, gp, :], Tfin)
                n_ = len(own)
                if n_:
                    gr = rows[:, 640:768]
                    gbc = bass.AP(gr.tensor, gr.offset, [gr.ap[0], (0, n_), (0, 2), (1, 128)])
                    ov = yacc[:, 0:n_, :]
                    o4 = ov.rearrange("p n (a b) -> p n a b", b=128)
                    big1 = arena[:, 0:n_ * 256].rearrange("p (n c) -> p n c", c=256)
                    big2 = arena[:, 2048:2048 + n_ * 256].rearrange("p (n c) -> p n c", c=256)
                    b14 = big1.rearrange("p n (a b) -> p n a b", b=128)
                    for pr_ in range(n_ // 2):
                        pgg = PF()
                        for j in range(2):
                            tile = 2 * pr_ + j
                            for kc in range(8):
                                MM(pgg[:, j * 256:(j + 1) * 256], hT[:, kc, pc[tile]:pc[tile] + 128], wv[:, kc, 256:512], start=(kc == 0), stop=(kc == 7))
                        pv2 = pgg[:, 0:512].rearrange("p (n c) -> p n c", c=256)
                        ACT(big2[:, 2 * pr_:2 * pr_ + 2, :], pv2, AF.Sigmoid)
                        TT("dve", big2[:, 2 * pr_:2 * pr_ + 2, :], big2[:, 2 * pr_:2 * pr_ + 2, :], pv2, ALU.mult)
                    ACT(big1, ov, AF.Square)
                    ms = sm[:, 0:2 * n_]
                    ms3 = ms.rearrange("p (n a) -> p n a", a=2)
                    RSUM("dve", ms3, b14)
                    RSQ(ms, ms, 1.0 / 128, epsv[:, 2:3])
                    TT("dve", b14, o4, bc(ms3, 128), ALU.mult)
                    TT("dve", b14, b14, gbc, ALU.mult)
                    TT("dve", mixtok[:, g0:g0 + n_, 512 + gp * 256:512 + (gp + 1) * 256], big1, big2, ALU.mult)

        PP = dict(name="PP", nt=4, pc=[64, 192, 384, 512], mv=0, x0=0, own=[0, 1, 2, 3], g0=0, kind="seq", dirs=[0, 1],
                  pads=[(0, 64), (320, 384), (640, 704)], groups=[(0, 2), (2, 2)],
                  ngroups=[(64, 0, 256), (384, 256, 256)], chains=[[0, 1], [2, 3]],
                  init={0: "zero", 1: "zero"}, end={0: "out", 1: "out"})
        PSO = dict(name="PSO", nt=8, pc=[64 + 128 * i for i in range(8)], mv=1, x0=1536, own=[], g0=0, kind="grid", dirs=[1],
                   pads=[(1088, 1152)], groups=[(0, 4), (4, 4)],
                   ngroups=[(0, 1472, 64), (64, 1536, 512), (576, 2048, 512)], chains=[list(range(8))],
                   init={1: "dram"}, end={1: "mid"})
        PSW = dict(name="PSW", nt=8, pc=[64 + 128 * i for i in range(8)], mv=1, x0=512, own=list(range(8)), g0=4, kind="grid", dirs=[0, 1],
                   pads=[(0, 64)], groups=[(0, 4), (4, 4)],
                   ngroups=[(64, 512, 512), (576, 1024, 512), (1088, 1536, 64)], chains=[list(range(8))],
                   init={0: "dram", 1: "mid"}, end={0: None, 1: None})
        stage = int(_ENVD.get("KSTAGE", "9"))
        if stage >= 1:
            mixer(PP)
        if stage >= 2:
            mixer(PSO)
        if stage >= 3:
            mixer(PSW)

        if dbg:
            dbg_out["mixtok"] = (mixtok, [128, 12, 1024], BF16)

        mixT = dhT
        h2T = hT
        hid = store[:].rearrange("p a b -> p (a b)")[:, 0:3072].rearrange("p (a b) -> p a b", b=768)

        def rms_stats(c0, n):
            for kc in range(8):
                ACT(sqb[:, kc, 0:n], x1[:, kc, c0:c0 + n], AF.Square)
            pss = PF()
            for kc in range(8):
                MM(pss[:, 0:n], ones_b[:], sqb[:, kc, 0:n], start=(kc == 0), stop=(kc == 7))
            RSQ(rstd[:, 0:n], pss[:, 0:n], 1.0 / 1024, epsv[:, 0:1])

        def tr_steps(half_):
            for gi in range(6):
                go = 6 * half_ + gi
                for hh in range(2):
                    pt = PB()
                    for j in range(4):
                        kc = hh * 4 + j
                        TR(pt[:, j * 128:(j + 1) * 128], mixtok[:, go, kc * 128:(kc + 1) * 128], ident_b[:])
                    CP(ALT("dve", "act"), mixT[:, hh * 4:hh * 4 + 4, gi * 128:(gi + 1) * 128],
                       pt[:, 0:512].rearrange("p (a b) -> p a b", b=128))
                    yield

        for half in range(2 if stage >= 4 else 0):
            MARK("post%d" % half)
            tb = 768 * half
            grp = [(0, 512, 0 if half == 0 else 1), (512, 256, 1)]
            if half == 0:
                for _ in tr_steps(0):
                    pass
                tr_next = None
            else:
                for _ in tr_next:
                    pass
            k.dma(x1[:, :, 0:768], xT[:, :, tb:tb + 768].rearrange("k p t -> p k t"))
            for cb in range(2):
                si = STG()
                k.dma(stg[si][:], w_out[:, cb * 512:(cb + 1) * 512].rearrange("(k p) n -> p k n", p=128))
                for kc in range(8):
                    CP(ALT("dve", "act"), wbf[si][:, kc, :], stg[si][:, kc, :])
                for cc in range(4):
                    oc = cb * 4 + cc
                    for (c0, n, mv) in grp:
                        pp = PF()
                        for kc in range(8):
                            MM(pp[:, 0:n], wbf[si][:, kc, cc * 128:(cc + 1) * 128], mixT[:, kc, c0:c0 + n],
                               start=(kc == 0), stop=(kc == 7))
                        xs_ = x1[:, oc, c0:c0 + n]
                        STT("dve", xs_, pp[:, 0:n], mod[:, GT1 + oc, mv:mv + 1], xs_, ALU.mult, ALU.add)
            for (c0, n, mv) in grp:
                rms_stats(c0, n)
                for kc in range(8):
                    TT("dve", ntmp[:, 0:n], x1[:, kc, c0:c0 + n], rstd[:, 0:n], ALU.mult)
                    ACT(h2T[:, kc, c0:c0 + n], ntmp[:, 0:n], AF.Identity,
                        bias=mod[:, SH2 + kc, mv:mv + 1], scale=A2[:, kc, mv:mv + 1])
            MARK("mlp%d" % half)
            if half == 0:
                tr_next = tr_steps(1)
            for hb in range(8):
                if half == 0:
                    for _r in range(2):
                        try:
                            next(tr_next)
                        except StopIteration:
                            pass
                si = STG()
                k.dma(stg[si][:], m1[:, hb * 512:(hb + 1) * 512].rearrange("(k p) n -> p k n", p=128))
                for kc in range(8):
                    CP(ALT("dve", "act"), wbf[si][:, kc, :], stg[si][:, kc, :])
                for cc in range(4):
                    for (c0, n, mv) in grp:
                        pp = PF()
                        for kc in range(8):
                            MM(pp[:, 0:n], wbf[si][:, kc, cc * 128:(cc + 1) * 128], h2T[:, kc, c0:c0 + n],
                               start=(kc == 0), stop=(kc == 7))
                        ACT(ntmp[:, 0:n], pp[:, 0:n], AF.Relu)
                        TT("dve", hid[:, cc, c0:c0 + n], ntmp[:, 0:n], ntmp[:, 0:n], ALU.mult)
                si2 = STG()
                s2v = stg[si2][:].rearrange("p a b -> p (a b)").rearrange("p (a b) -> p a b", b=1024)
                w2v = wbf[si2][:].rearrange("p a b -> p (a b)").rearrange("p (a b) -> p a b", b=1024)
                k.dma(s2v, m2[hb * 512:(hb + 1) * 512, :].rearrange("(k p) n -> p k n", p=128))
                for kc in range(4):
                    CP(ALT("dve", "act"), w2v[:, kc, :], s2v[:, kc, :])
                for oc in range(8):
                    for (c0, n, mv) in grp:
                        pp = PF()
                        for kc in range(4):
                            MM(pp[:, 0:n], w2v[:, kc, oc * 128:(oc + 1) * 128], hid[:, kc, c0:c0 + n],
                               start=(kc == 0), stop=(kc == 3))
                        xs_ = x1[:, oc, c0:c0 + n]
                        STT("dve", xs_, pp[:, 0:n], mod[:, GT2 + oc, mv:mv + 1], xs_, ALU.mult, ALU.add)
            for (c0, n, mv) in grp:
                rms_stats(c0, n)
                for kc in range(8):
                    xs_ = x1[:, kc, c0:c0 + n]
                    STT(ALT(), xs_, xs_, vps[:, V_FNG + kc:V_FNG + kc + 1], rstd[:, 0:n], ALU.mult, ALU.mult)
            k.dma(yT[:, :, tb:tb + 768].rearrange("k p t -> p k t"), x1[:, :, 0:768], is_output=True)

    try:
        body()
    except _Stop:
        pass
    if dbg:
        for name, (t, shp, dt) in dbg_out.items():
            o = nc.dram_tensor("dbg_" + name, shp, dt, kind="ExternalOutput").ap()
            k.dma(o, t[:], is_output=True)
    MARK("end")
    globals()["_LASTK"] = k
    stats = k.emit()
    return nc, stats


def _lay_kc(v):
    return np.ascontiguousarray(v.reshape(8, 128).T)


def _prep_core(c, I):
    f = c % 2
    b = c // 2
    fl = (lambda a: a[::-1]) if f else (lambda a: a)
    xs = [fl(I["x_prompt"][2 * c]), fl(I["x_prompt"][2 * c + 1]), fl(I["x_sample"][b])]
    x = np.concatenate(xs, axis=0)
    xT = np.ascontiguousarray(x.T).reshape(8, 128, 2560)
    cond = np.stack([_lay_kc(I["c_ctx"]), _lay_kc(I["c"][b])], axis=-1)
    dsel = [1, 0] if f else [0, 1]
    vp = np.zeros((128, NV), np.float32)
    vp[:, V_N1G:V_N1G + 8] = _lay_kc(I["norm1_g"][0])
    vp[:, V_N2G:V_N2G + 8] = _lay_kc(I["norm2_g"][0])
    vp[:, V_FNG:V_FNG + 8] = _lay_kc(I["final_norm_g"])
    vp[:, V_MUW:V_MUW + 8] = _lay_kc(I["rwkv_mu_wag"][0, 0])
    vp[:, V_MUA:V_MUA + 8] = _lay_kc(I["rwkv_mu_wag"][0, 1])
    vp[:, V_MUG:V_MUG + 8] = _lay_kc(I["rwkv_mu_wag"][0, 2])
    for d in range(2):
        vp[:, V_W0 + 4 * d:V_W0 + 4 * d + 4] = I["rwkv_w0"][0, dsel[d]].reshape(4, 128).T
        vp[:, V_A0 + 4 * d:V_A0 + 4 * d + 4] = I["rwkv_a0"][0, dsel[d]].reshape(4, 128).T
        vp[:, V_GKB + 2 * d:V_GKB + 2 * d + 2] = I["gla_gk_b"][0, dsel[d]].reshape(2, 128).T
    vp[:, V_KK:V_KK + 4] = I["rwkv_k_k"][0].reshape(4, 128).T
    vp[:, V_KA:V_KA + 4] = I["rwkv_k_a"][0].reshape(4, 128).T
    vp[:, V_RK:V_RK + 4] = I["rwkv_r_k"][0].reshape(512).reshape(4, 128).T
    vp[:, V_ADB:V_ADB + 48] = I["ada_b"][0].reshape(48, 128).T
    rp = np.zeros((1, NR), np.float32)
    rp[0, R_MU:R_MU + 1536] = I["rwkv_mu_rkv"][0]
    rp[0, R_LNG:R_LNG + 512] = I["rwkv_lnx_g"][0]
    rp[0, R_LNB:R_LNB + 512] = I["rwkv_lnx_b"][0]
    rp[0, R_GNG:R_GNG + 512] = np.tile(I["gla_norm_g"][0], 4)
    sr = [I["state_rwkv_fwd"][b, 0], I["state_rwkv_bwd"][b, 0]]
    sg = [I["state_gla_fwd"][b, 0], I["state_gla_bwd"][b, 0]]
    st_r = np.stack([np.swapaxes(sr[dsel[d]], -1, -2) for d in range(2)])
    st_g = np.stack([sg[dsel[d]] for d in range(2)])
    A = np.ascontiguousarray
    return {
        "xT": A(xT), "cond": A(cond.astype(np.float32)), "ada_w": A(I["ada_w"][0]), "vp": vp, "rp": rp,
        "w_in": A(I["w_in"][0]),
        "w1": A(I["rwkv_w1"][0][dsel]), "w2": A(I["rwkv_w2"][0][dsel]),
        "a1": A(I["rwkv_a1"][0][dsel]), "a2": A(I["rwkv_a2"][0][dsel]),
        "g1": A(I["rwkv_g1"][0]), "g2": A(I["rwkv_g2"][0]),
        "gk1": A(I["gla_gk1"][0][dsel]), "gk2": A(I["gla_gk2"][0][dsel]),
        "w_out": A(I["w_out"][0]), "m1": A(I["mlp_w1"][0]), "m2": A(I["mlp_w2"][0]),
        "st_r": A(st_r), "st_g": A(st_g),
    }


_CACHE = {}


def kernel(**inputs):
    I = {k_: np.asarray(v) for k_, v in inputs.items()}
    if "nc" not in _CACHE:
        _CACHE["nc"] = build()[0]
    nc = _CACHE["nc"]
    in_maps = [_prep_core(c, I) for c in range(8)]
    res = run_bass_kernel_spmd(nc, in_maps, core_ids=list(range(8)))
    y_prompt = np.zeros((16, 256, 1024), np.float32)
    y_sample = np.zeros((4, 2048, 1024), np.float32)
    nrf = np.zeros((16, 1, 8, 64, 64), np.float32)
    nrb = np.zeros((16, 1, 8, 64, 64), np.float32)
    ngf = np.zeros((16, 1, 4, 64, 128), np.float32)
    ngb = np.zeros((16, 1, 4, 64, 128), np.float32)
    for c in range(8):
        r = res.results[c]
        f = c % 2
        b = c // 2
        y = np.asarray(r["yT"]).reshape(1024, 1536).T
        fl = (lambda a: a[::-1]) if f else (lambda a: a)
        y_prompt[2 * c] = fl(y[0:256])
        y_prompt[2 * c + 1] = fl(y[256:512])
        ys = y[512:1536]
        if f:
            y_sample[b, 1024:2048] = ys[::-1]
        else:
            y_sample[b, 0:1024] = ys
        nsr = np.asarray(r["ns_r"])
        nsg = np.asarray(r["ns_g"])
        for s in range(2):
            for d in range(2):
                gd = d ^ f
                tgt_r = nrf if gd == 0 else nrb
                tgt_g = ngf if gd == 0 else ngb
                tgt_r[2 * c + s, 0] = np.swapaxes(nsr[s, d], -1, -2)
                tgt_g[2 * c + s, 0] = nsg[s, d]
    return (y_prompt, y_sample, nrf, nrb, ngf, ngb)
```

```python
import numpy as np
from contextlib import ExitStack
import concourse.bass as bass
import concourse.mybir as mybir
from concourse.bass_utils import run_bass_kernel_spmd

F32 = mybir.dt.float32
BF16 = mybir.dt.bfloat16
ALU = mybir.AluOpType
AF = mybir.ActivationFunctionType
AX = mybir.AxisListType

_MARKS = []
_ENVD = {}
CW = 0.6065306597126334


class K:
    N_DMA_SEMS = 24

    def __init__(self, nc, same_engine_sync=True):
        self.nc = nc
        self.es = ExitStack()
        self.ops = {e: [] for e in ("pe", "act", "dve", "pool", "sp")}
        self.recs = {}
        self.same_engine_sync = same_engine_sync
        self.dma_cnt = [0] * self.N_DMA_SEMS
        self.dma_rr = 0
        self.out_events = []
        self.needed = set()
        self.waited = {e: {} for e in self.ops}

    def sb(self, name, shape, dtype):
        return self.es.enter_context(self.nc.sbuf_tensor(name, list(shape), dtype))

    def ps(self, name, shape, dtype=F32):
        return self.es.enter_context(self.nc.psum_tensor(name, list(shape), dtype))

    @staticmethod
    def _box(ap):
        if "PSUM" in str(ap.space).upper():
            return (0, 128, 0, 1 << 30)
        a = ap.ap
        pstep, pcnt = a[0]
        off = int(ap.offset)
        if pstep == 0:
            p0, f0 = 0, off
            pcnt = 1
        else:
            p0 = off // pstep
            f0 = off - p0 * pstep
        lo = 0
        hi = 0
        for st, cn in a[1:]:
            if st >= 0:
                hi += st * (cn - 1)
            else:
                lo += st * (cn - 1)
        return (p0, p0 + pcnt, f0 + lo, f0 + hi + 1)

    @staticmethod
    def _ovl(a, b):
        return a[0] < b[1] and b[0] < a[1] and a[2] < b[3] and b[2] < a[3]

    @staticmethod
    def _covers(a, b):
        return a[0] <= b[0] and a[1] >= b[1] and a[2] <= b[2] and a[3] >= b[3]

    def _track(self, reads, writes, ev):
        deps = {}
        items = []
        for ap in reads:
            if "DRAM" in str(ap.space).upper():
                continue
            items.append((ap.name, self._box(ap), False))
        for ap in writes:
            if "DRAM" in str(ap.space).upper():
                continue
            items.append((ap.name, self._box(ap), True))
        for name, box, isw in items:
            lst = self.recs.setdefault(name, [])
            for (b, w, e) in lst:
                if (w or isw) and self._ovl(b, box):
                    if e[1] > deps.get(e[0], -1):
                        deps[e[0]] = e[1]
        for name, box, isw in items:
            lst = self.recs[name]
            if isw:
                lst[:] = [r for r in lst if not self._covers(box, r[0])]
                lst.append((box, True, ev))
            else:
                lst[:] = [r for r in lst if not (r[1] is False and r[2][0] == ev[0] and self._covers(box, r[0]))]
                lst.append((box, False, ev))
        return deps

    def op(self, eng, fn, reads=(), writes=()):
        idx = len(self.ops[eng])
        ev = (eng, idx)
        deps = self._track(reads, writes, ev)
        waits = []
        for k, v in deps.items():
            if k == eng and (eng == "pe" or not self.same_engine_sync):
                continue
            if self.waited[eng].get(k, -1) >= v:
                continue
            self.waited[eng][k] = v
            waits.append((k, v))
            self.needed.add((k, v))
        self.ops[eng].append(dict(kind="op", fn=fn, waits=waits, desc=(writes[0].name if writes else "?") + "<-" + ",".join(sorted(set(r.name for r in reads)))))
        return ev

    def dma(self, out, in_, queue="sp", is_output=False, **kw):
        k = self.dma_rr
        self.dma_rr = (self.dma_rr + 1) % self.N_DMA_SEMS
        self.dma_cnt[k] += 1
        semname = "dma%d" % k
        ev = (semname, self.dma_cnt[k])
        deps = self._track([in_], [out], ev)
        if self.dma_cnt[k] > 1:
            deps[semname] = max(deps.get(semname, -1), self.dma_cnt[k] - 1)
        waits = []
        for kk, v in deps.items():
            if self.waited[queue].get(kk, -1) >= v:
                continue
            self.waited[queue][kk] = v
            waits.append((kk, v))
            self.needed.add((kk, v))
        self.ops[queue].append(dict(kind="dma", out=out, in_=in_, waits=waits, sem=semname, kw=kw))
        if is_output:
            self.out_events.append(ev)
        return ev

    def emit(self):
        nc = self.nc
        fin = []
        for ev in self.out_events:
            if self.waited["sp"].get(ev[0], -1) >= ev[1]:
                continue
            self.waited["sp"][ev[0]] = ev[1]
            fin.append(ev)
        self.ops["sp"].append(dict(kind="fin", waits=fin))
        val = {}
        for e, lst in self.ops.items():
            c = 0
            for i, o in enumerate(lst):
                if (e, i) in self.needed:
                    c += 1
                    val[(e, i)] = c
        sems = {}
        for e in ("pe", "act", "dve", "pool"):
            sems[e] = self.es.enter_context(nc.semaphore("s_" + e))
        for k in range(self.N_DMA_SEMS):
            sems["dma%d" % k] = self.es.enter_context(nc.semaphore("s_dma%d" % k))

        def wv(k, v):
            if k.startswith("dma"):
                return 16 * v
            return val[(k, v)]

        def run(ename, eng):
            dm = getattr(self, "dummy", None)
            for i, o in enumerate(self.ops[ename]):
                if ename == "pe" and dm is not None and o["waits"] and any(not k.startswith("dma") for (k, v) in o["waits"]):
                    for _ in range(dm[3]):
                        eng.matmul(dm[0], dm[1], dm[2], start=True, stop=True)
                for (k, v) in o["waits"]:
                    eng.wait_ge(sems[k], wv(k, v))
                if o["kind"] == "op":
                    ins = o["fn"](eng)
                    if (ename, i) in self.needed:
                        ins.then_inc(sems[ename], 1)
                elif o["kind"] == "dma":
                    eng.dma_start(out=o["out"], in_=o["in_"], **o["kw"]).then_inc(sems[o["sem"]], 16)

        with nc.Block() as block:
            @block.tensor
            def _(e):
                run("pe", e)

            @block.scalar
            def _(e):
                run("act", e)

            @block.vector
            def _(e):
                run("dve", e)

            @block.gpsimd
            def _(e):
                run("pool", e)

            @block.sync
            def _(e):
                run("sp", e)
        self.es.close()
        return {e: len(l) for e, l in self.ops.items()}


def bc(ap, n):
    return bass.AP(ap.tensor, ap.offset, list(ap.ap) + [(0, n)])


def rev(ap):
    a = list(ap.ap)
    st, cn = a[-1]
    return bass.AP(ap.tensor, ap.offset + (cn - 1) * st, a[:-1] + [(-st, cn)])


V_N1G, V_N2G, V_FNG, V_MUW, V_MUA, V_MUG = 0, 8, 16, 24, 32, 40
V_W0, V_A0, V_KK, V_KA, V_RK, V_GKB, V_ADB = 48, 56, 64, 68, 72, 76, 80
NV = 80 + 48
R_MU, R_LNG, R_LNB, R_GNG = 0, 1536, 2048, 2560
NR = 3072


def build(dbg=False):
    nc = bass.Bass("TRN2", target_bir_lowering=False)
    import os as _os
    k = K(nc, same_engine_sync=(_ENVD.get("KSES", "1") == "1"))
    DT = lambda n, s, kind="ExternalInput": nc.dram_tensor(n, list(s), F32, kind=kind).ap()
    xT = DT("xT", [8, 128, 2560])
    cond = DT("cond", [128, 8, 2])
    ada_w = DT("ada_w", [1024, 6144])
    vp = DT("vp", [128, NV])
    rp = DT("rp", [1, NR])
    w_in = DT("w_in", [1024, 3072])
    w1 = DT("w1", [2, 1024, 64]); w2 = DT("w2", [2, 64, 512])
    a1 = DT("a1", [2, 1024, 64]); a2 = DT("a2", [2, 64, 512])
    g1 = DT("g1", [1024, 128]); g2 = DT("g2", [128, 512])
    gk1 = DT("gk1", [2, 1024, 16]); gk2 = DT("gk2", [2, 16, 256])
    w_out = DT("w_out", [1024, 1024]); m1 = DT("m1", [1024, 4096]); m2 = DT("m2", [4096, 1024])
    st_r = DT("st_r", [2, 8, 64, 64]); st_g = DT("st_g", [2, 4, 64, 128])
    yT = DT("yT", [8, 128, 1536], "ExternalOutput")
    ns_r = DT("ns_r", [2, 2, 8, 64, 64], "ExternalOutput")
    ns_g = DT("ns_g", [2, 2, 4, 64, 128], "ExternalOutput")
    dbg_out = {}

    def TT(eng, out, a, b, op):
        k.op(eng, lambda e: e.tensor_tensor(out, a, b, op), reads=[a, b], writes=[out])

    def TS(eng, out, a, s1, s2, op0, op1=None):
        rd = [a] + [s for s in (s1, s2) if not isinstance(s, (int, float, type(None)))]
        if op1 is None:
            k.op(eng, lambda e: e.tensor_scalar(out, a, s1, None, op0), reads=rd, writes=[out])
        else:
            k.op(eng, lambda e: e.tensor_scalar(out, a, s1, s2, op0, op1), reads=rd, writes=[out])

    def STT(eng, out, a, s, b, op0, op1):
        rd = [a, b] + ([] if isinstance(s, (int, float)) else [s])
        k.op("dve", lambda e: e.scalar_tensor_tensor(out, a, s, b, op0, op1), reads=rd, writes=[out])

    def ACT(out, a, func, bias=None, scale=None):
        rd = [a]
        kw = {}
        if bias is not None:
            kw["bias"] = bias
            if not isinstance(bias, (int, float)):
                rd.append(bias)
        if scale is not None:
            kw["scale"] = scale
            if not isinstance(scale, (int, float)):
                rd.append(scale)
        k.op("act", lambda e: e.activation(out, a, func, **kw), reads=rd, writes=[out])

    def CP(eng, out, a):
        if eng == "act":
            k.op("act", lambda e: e.copy(out, a), reads=[a], writes=[out])
        else:
            k.op(eng, lambda e: e.tensor_copy(out, a), reads=[a], writes=[out])

    def MM(out, lhsT, rhs, start=True, stop=True):
        k.op("pe", lambda e: e.matmul(out, lhsT, rhs, start=start, stop=stop), reads=[lhsT, rhs], writes=[out])

    def TR(out, a, idn):
        k.op("pe", lambda e: e.transpose(out, a, idn), reads=[a, idn], writes=[out])

    def MS(eng, out, v):
        k.op(eng, lambda e: e.memset(out, v), writes=[out])

    def ASEL(out, pattern, cmp, base, cm):
        k.op("pool", lambda e: e.affine_select(out, out, pattern=pattern, compare_op=cmp, fill=0.0, base=base,
                                               channel_multiplier=cm), reads=[out], writes=[out])

    def SCAN(out, m, x):
        k.op("dve", lambda e: e.tensor_tensor_scan(out, m, x, 0.0, ALU.mult, ALU.add), reads=[m, x], writes=[out])

    def RSUM(eng, out, a):
        k.op(eng, lambda e: e.reduce_sum(out, a, AX.X), reads=[a], writes=[out])

    def RSQ(out, a, scale, eps_ap):
        ACT(out, a, AF.Sqrt, bias=eps_ap, scale=scale)
        k.op("dve", lambda e: e.reciprocal(out, out), reads=[out], writes=[out])

    def bcm(ap2d, n):
        return bass.AP(ap2d.tensor, ap2d.offset, [ap2d.ap[0], (0, n)] + list(ap2d.ap[1:]))

    NDUM = int(_ENVD.get("KDUM", "0"))
    NPF = 5 if NDUM else 6
    pf = [k.ps("pf%d" % i, [128, 512], F32) for i in range(NPF)]
    if NDUM:
        pdum = k.ps("pdum", [128, 512], F32)
    pb = [k.ps("pb%d" % i, [128, 1024], BF16) for i in range(2)]
    cnt = {"pf": 0, "pb": 0, "alt": 0, "stg": 0}

    def PF():
        cnt["pf"] += 1
        return pf[cnt["pf"] % NPF]

    def PB():
        cnt["pb"] += 1
        return pb[cnt["pb"] % 2]

    def ALT(a="dve", b="pool"):
        cnt["alt"] += 1
        return a if cnt["alt"] % 2 else b

    def STG():
        cnt["stg"] += 1
        return cnt["stg"] % 2

    import os

    class _Stop(Exception):
        pass

    def MARK(name):
        _MARKS.append((name, len(k.ops["pe"])))

    def CK(tag):
        if _ENVD.get("KSTOP") == tag:
            raise _Stop()

    def body():
        ident_b = k.sb("ident_b", [128, 128], BF16)
        ones_b = k.sb("ones_b", [128, 128], BF16)
        blk_b = k.sb("blk_b", [128, 128], BF16)
        blk256_f = k.sb("blk256_f", [128, 256], BF16)
        blkind_b = k.sb("blkind_b", [128, 2], BF16)
        mask2 = k.sb("mask2", [128, 2, 256], BF16)
        maskQ = k.sb("maskQ", [128, 2, 128], BF16)
        maskI = k.sb("maskI", [128, 2, 128], BF16)
        scm = k.sb("scm", [128, 257], F32)

        MS("pool", ident_b[:], 1.0)
        ASEL(ident_b[:], [[1, 128]], ALU.is_equal, 0, -1)
        MS("pool", ones_b[:], 1.0)
        if NDUM:
            k.dummy = (pdum[:, 0:int(_ENVD.get("KDUMN", "128"))], ident_b[:], ones_b[:, 0:int(_ENVD.get("KDUMN", "128"))], NDUM)
        MS("pool", blk_b[:], 0.0)
        MS("pool", blk_b[0:64, 0:64], 1.0)
        MS("pool", blk_b[64:128, 64:128], 1.0)
        MS("pool", blk256_f[:], 0.0)
        MS("pool", blk256_f[0:64, 0:128], 1.0)
        MS("pool", blk256_f[64:128, 128:256], 1.0)
        MS("pool", blkind_b[:], 0.0)
        MS("pool", blkind_b[0:64, 0:1], 1.0)
        MS("pool", blkind_b[64:128, 1:2], 1.0)
        MS("pool", mask2[:], 1.0)
        MS("pool", maskQ[:], 1.0)
        MS("pool", maskI[:], 1.0)
        ASEL(mask2[:, 0, 0:128], [[1, 128]], ALU.is_ge, -1, -1)
        ASEL(mask2[:, 0, 128:256], [[1, 128]], ALU.is_ge, 0, -1)
        ASEL(mask2[:, 1, 0:128], [[-1, 128]], ALU.is_ge, -1, 1)
        ASEL(mask2[:, 1, 128:256], [[-1, 128]], ALU.is_ge, 0, 1)
        ASEL(maskQ[:, 0, :], [[-1, 128]], ALU.is_ge, -1, 1)
        ASEL(maskQ[:, 1, :], [[1, 128]], ALU.is_ge, -1, -1)
        ASEL(maskI[:, 0, :], [[1, 128]], ALU.is_ge, 0, -1)
        ASEL(maskI[:, 1, :], [[-1, 128]], ALU.is_ge, 0, 1)
        MS("pool", scm[:], 1.0)
        for c_ in (0, 128, 256):
            MS("pool", scm[:, c_:c_ + 1], 0.0)

        epsv = k.sb("epsv", [128, 4], F32)
        MS("pool", epsv[:, 0:1], 1e-6)
        MS("pool", epsv[:, 1:2], 64e-5)
        MS("pool", epsv[:, 2:3], 1e-5)
        MS("pool", epsv[:, 3:4], 0.0)
        CK("A")
        vps = k.sb("vps", [128, NV], F32)
        k.dma(vps[:], vp)
        conds = k.sb("conds", [128, 8, 2], F32)
        k.dma(conds[:], cond)
        omka = k.sb("omka", [128, 4], F32)
        TS("dve", omka[:], vps[:, V_KA:V_KA + 4], -1.0, 1.0, ALU.mult, ALU.add)

        stg = [k.sb("stg%d" % i, [128, 8, 512], F32) for i in range(2)]
        wbf = [k.sb("wbf%d" % i, [128, 8, 512], BF16) for i in range(2)]
        xg = stg[0]
        sqb = wbf[1]

        CK("B")
        csil = k.sb("csil", [128, 8, 2], F32)
        ACT(csil[:], conds[:], AF.Sigmoid)
        TT("dve", csil[:], csil[:], conds[:], ALU.mult)
        mod = k.sb("mod", [128, 48, 2], F32)
        pm = PF()
        for blk in range(12):
            si = STG()
            k.dma(stg[si][:], ada_w[:, blk * 512:(blk + 1) * 512].rearrange("(k p) n -> p k n", p=128))
            for cc in range(4):
                ch = blk * 4 + cc
                for kc in range(8):
                    MM(pm[:, 2 * ch:2 * ch + 2], stg[si][:, kc, cc * 128:(cc + 1) * 128], csil[:, kc, :],
                       start=(kc == 0), stop=(kc == 7))
        for v_ in range(2):
            TT("dve", mod[:, :, v_], pm[:, 0:96].rearrange("p (c v) -> p c v", v=2)[:, :, v_], vps[:, V_ADB:V_ADB + 48], ALU.add)
        CK("C")
        A1 = k.sb("A1", [128, 8, 2], F32)
        A2 = k.sb("A2", [128, 8, 2], F32)
        for v_ in range(2):
            STT("dve", A1[:, :, v_], mod[:, 8:16, v_], 1.0, vps[:, V_N1G:V_N1G + 8], ALU.add, ALU.mult)
            STT("dve", A2[:, :, v_], mod[:, 32:40, v_], 1.0, vps[:, V_N2G:V_N2G + 8], ALU.add, ALU.mult)
        SH1, GT1, SH2, GT2 = 0, 16, 24, 40

        HW = 1152
        hT = k.sb("hT", [128, 8, HW], BF16)
        dhT = k.sb("dhT", [128, 8, 1024], BF16)
        tanhT = k.sb("tanhT", [128, 1024], BF16)
        a1T = k.sb("a1T", [128, 1024], BF16)
        sigT = k.sb("sigT", [128, 1024], BF16)
        gk1T = k.sb("gk1T", [64, 1024], BF16)
        mixtok = k.sb("mixtok", [128, 12, 1024], BF16)
        store = k.sb("store", [128, 16, 384], BF16)
        gam = k.sb("gam", [128, 2, 8], F32)
        vtok = k.sb("vtok", [128, 8, 256], BF16)
        prodb = k.sb("prodb", [128, 8, 128], BF16)
        Tb = k.sb("Tb", [128, 256], BF16)
        Whp = k.sb("Whp", [128, 8, 2, 448], BF16)
        L1 = Whp
        W2b = k.sb("W2b", [128, 2, 128], BF16)
        G2b = k.sb("G2b", [128, 512], BF16)
        GK2b = k.sb("GK2b", [64, 2, 128], BF16)
        arena = k.sb("arena", [128, 8192], F32)
        _ao = [0]

        def carve(n):
            a = arena[:, _ao[0]:_ao[0] + n]
            _ao[0] += n
            return a

        T = {n: carve(256) for n in ("r", "k", "kq", "kk", "sig", "a", "kd", "kd0", "be", "S", "D", "E1", "E2", "E3", "x1", "x2")}
        yacc = carve(2048).rearrange("p (a b) -> p a b", b=256)
        rows = carve(768)
        Tf = carve(256)
        Tmid_r = carve(512).rearrange("p (a b) -> p a b", b=128)
        Tmid_g = carve(512).rearrange("p (a b) -> p a b", b=256)
        assert _ao[0] <= 8192
        x1 = arena[:, 0:6144].rearrange("p (a b) -> p a b", b=768)
        vb = k.sb("vb", [128, 256], BF16)
        AR = [k.sb("AR%d" % i, [128, 2, 2, 128], BF16) for i in range(2)]
        BK = [k.sb("BK%d" % i, [128, 2, 2, 128], BF16) for i in range(2)]
        toks = [k.sb("toks%d" % i, [128, 2, 3, 128], BF16) for i in range(2)]
        XK = [k.sb("XK%d" % i, [128, 2, 2, 256], BF16) for i in range(2)]
        QP = [k.sb("QP%d" % i, [128, 2, 2, 128], BF16) for i in range(6)]
        RR = [k.sb("RR%d" % i, [128, 2, 128], BF16) for i in range(4)]
        ZR = [k.sb("ZR%d" % i, [128, 2, 64], BF16) for i in range(2)]
        ZS = [k.sb("ZS%d" % i, [128, 2, 2, 64], BF16) for i in range(2)]
        sm = k.sb("sm", [128, 16], F32)
        rstd = arena[:, 6144:6656]
        ntmp = arena[:, 6656:7168]

        def load_l1():
            k.dma(rows[:, 640:768], rp[:, R_GNG:R_GNG + 128].partition_broadcast(128))
            si = STG()
            for d in range(2):
                k.dma(stg[si][:, :, d * 64:(d + 1) * 64], w1[d].rearrange("(k p) n -> p k n", p=128))
                k.dma(stg[si][:, :, 128 + d * 64:128 + (d + 1) * 64], a1[d].rearrange("(k p) n -> p k n", p=128))
            k.dma(stg[si][:, :, 256:384], g1.rearrange("(k p) n -> p k n", p=128))
            MS("pool", stg[si][:, :, 384:448], 0.0)
            for d in range(2):
                k.dma(stg[si][:, :, 384 + 32 * d:384 + 32 * d + 16], gk1[d].rearrange("(k p) n -> p k n", p=128))
            for kc in range(8):
                CP("act" if kc % 2 else "dve", L1[:, kc, 0, :], stg[si][:, kc, 0:448])
                for j, vo in enumerate((V_MUW, V_MUA, V_MUG)):
                    TS("dve", L1[:, kc, 1, j * 128:(j + 1) * 128], stg[si][:, kc, j * 128:(j + 1) * 128],
                       vps[:, vo + kc:vo + kc + 1], None, ALU.mult)

        si = STG()
        k.dma(stg[si][:, 0, :], g2)
        CP("pool", G2b[:], stg[si][:, 0, :])

        def mixer(P):
            nt = P["nt"]
            pc = P["pc"]
            mv = P["mv"]
            x0 = P["x0"]
            own = P["own"]
            g0 = P["g0"]
            dirs = P["dirs"]
            MARK(P["name"] + ":norm")
            load_l1()
            for (a_, b_) in P["pads"]:
                MS("pool", hT[:, :, a_:b_], 0.0)
            for (pcol, xcol, n) in P["ngroups"]:
                k.dma(xg[:, :, 0:n], xT[:, :, xcol:xcol + n].rearrange("k p t -> p k t"))
                for kc in range(8):
                    ACT(sqb[:, kc, 0:n], xg[:, kc, 0:n], AF.Square)
                pss = PF()
                for kc in range(8):
                    MM(pss[:, 0:n], ones_b[:], sqb[:, kc, 0:n], start=(kc == 0), stop=(kc == 7))
                RSQ(rstd[:, 0:n], pss[:, 0:n], 1.0 / 1024, epsv[:, 0:1])
                for kc in range(8):
                    e_ = ALT()
                    TT(e_, xg[:, kc, 0:n], xg[:, kc, 0:n], rstd[:, 0:n], ALU.mult)
                    ACT(hT[:, kc, pcol:pcol + n], xg[:, kc, 0:n], AF.Identity,
                        bias=mod[:, SH1 + kc, mv:mv + 1], scale=A1[:, kc, mv:mv + 1])
            CK("M1")
            MARK(P["name"] + ":lora1")
            for (t0g, ntile) in P["groups"]:
                for sc in range(t0g // 2, (t0g + ntile) // 2):
                    t0 = 2 * sc
                    cc = pc[t0]
                    e_ = "dve"
                    acc = (arena[:, 0:2048] if sc % 2 == 0 else arena[:, 2048:4096]).rearrange("p (k t) -> p k t", t=256)
                    hh = hT[:, :, cc:cc + 256]
                    if P["kind"] == "seq":
                        TT(e_, acc, hT[:, :, cc - 1:cc + 255], hT[:, :, cc + 1:cc + 257], ALU.add)
                        sc0 = 0.5
                    else:
                        TT("pool", acc, hT[:, :, cc - 64:cc + 192], hT[:, :, cc + 64:cc + 320], ALU.add)
                        a4 = acc.rearrange("p k (r c) -> p k r c", c=64)
                        h4 = hh.rearrange("p k (r c) -> p k r c", c=64)
                        TT(e_, a4[:, :, :, 1:64], a4[:, :, :, 1:64], h4[:, :, :, 0:63], ALU.add)
                        TT(e_, a4[:, :, :, 0:63], a4[:, :, :, 0:63], h4[:, :, :, 1:64], ALU.add)
                        sc0 = 0.25
                    STT("dve", dhT[:, :, 128 * t0:128 * t0 + 256], acc, sc0, hh, ALU.mult, ALU.subtract)
                t0 = t0g
                n = 128 * ntile
                hs = lambda kc: hT[:, kc, pc[t0]:pc[t0] + n]
                ds = lambda kc: dhT[:, kc, 128 * t0:128 * t0 + n]
                cs = slice(128 * t0, 128 * t0 + n)
                for j in range(3):
                    if j == 2 and not own:
                        continue
                    pp = PF()
                    for kc in range(8):
                        MM(pp[:, 0:n], L1[:, kc, 0, j * 128:(j + 1) * 128], hs(kc), start=(kc == 0), stop=False)
                    for kc in range(8):
                        MM(pp[:, 0:n], L1[:, kc, 1, j * 128:(j + 1) * 128], ds(kc), start=False, stop=(kc == 7))
                    if j == 0:
                        ACT(tanhT[:, cs], pp[:, 0:n], AF.Tanh)
                    elif j == 1:
                        CP("act", a1T[:, cs], pp[:, 0:n])
                    else:
                        ACT(sigT[:, cs], pp[:, 0:n], AF.Sigmoid)
                pp = PF()
                for kc in range(8):
                    MM(pp[0:64, 0:n], L1[:, kc, 0, 384:448], hs(kc), start=(kc == 0), stop=(kc == 7))
                CP("act", gk1T[:, cs], pp[0:64, 0:n])

            CK("M3")

            def init_state(kind, d, width, dram_blocks, mid):
                tf = Tf[:, 0:width]
                if kind == "mid":
                    CP("pool", tf, mid)
                else:
                    MS("pool", tf, 0.0)
                    if kind == "dram":
                        bw = width // 2
                        for e in range(2):
                            k.dma(Tf[64 * e:64 * e + 64, bw * e:bw * (e + 1)], dram_blocks[e])
                CP("act", Tb[:, 0:width], tf)

            wb0 = wbf[0][:].rearrange("p a b -> p (a b)")
            wb1 = wbf[1][:].rearrange("p a b -> p (a b)")

            def carve_job(wb, o):
                xk_ = wb[:, o:o + 1024].rearrange("p (e m c) -> p e m c", e=2, m=2)
                o += 1024
                qp_ = []
                for _q in range(3):
                    qp_.append(wb[:, o:o + 512].rearrange("p (e m c) -> p e m c", e=2, m=2))
                    o += 512
                rb_ = []
                for _q in range(2):
                    rb_.append(wb[:, o:o + 256].rearrange("p (e c) -> p e c", e=2))
                    o += 256
                zr_ = wb[:, o:o + 128].rearrange("p (e c) -> p e c", e=2)
                o += 128
                zs_ = wb[:, o:o + 256].rearrange("p (m e c) -> p m e c", m=2, e=2)
                return dict(xk=xk_, qp=qp_, rb=rb_, zr=zr_, zs=zs_)

            JB = [dict(xk=XK[i][:], qp=[q[:] for q in QP[3 * i:3 * i + 3]], rb=[r_[:] for r_ in RR[2 * i:2 * i + 2]], zr=ZR[i][:], zs=ZS[i][:]) for i in range(2)]
            JB.append(carve_job(wb0, 0))
            JB.append(carve_job(wb1, 256))
            SETS = [[(AR[di][:], BK[di][:], toks[di][:]) for di in range(2)], []]
            for di in range(2):
                ar_b = mixtok[:, 4 + di, 512:1024].rearrange("p (a b c) -> p a b c", a=2, b=2)
                bk_b = mixtok[:, 6 + di, 512:1024].rearrange("p (a b c) -> p a b c", a=2, b=2)
                tk_b = mixtok[:, 8 + 2 * di:10 + 2 * di, 512:896].rearrange("p i (j c) -> p i j c", j=3)
                SETS[1].append((ar_b, bk_b, tk_b))
            SETSF = [SETS[0][0], SETS[0][1], SETS[1][0], SETS[1][1]]

            def wprep_steps(hp):
                si = STG()
                for j in range(3):
                    k.dma(stg[si][:, :, j * 128:(j + 1) * 128],
                          w_in[:, j * 512 + hp * 128:j * 512 + (hp + 1) * 128].rearrange("(k p) n -> p k n", p=128))
                for j in range(3):
                    k.dma(rows[:, j * 128:(j + 1) * 128],
                          rp[:, R_MU + j * 512 + hp * 128:R_MU + j * 512 + (hp + 1) * 128].partition_broadcast(128))
                si2 = STG()
                for d in range(2):
                    k.dma(stg[si2][64 * d:64 * d + 64, 0, 0:128], w2[d][:, hp * 128:(hp + 1) * 128])
                    k.dma(stg[si2][64 * d:64 * d + 64, 0, 128:256], a2[d][:, hp * 128:(hp + 1) * 128])
                yield
                for kc in range(8):
                    CP("act" if kc % 2 else "dve", Whp[:, kc, 0, 0:384], stg[si][:, kc, 0:384])
                    TT("dve", Whp[:, kc, 1, 0:384], stg[si][:, kc, 0:384], rows[:, 0:384], ALU.mult)
                    if kc % 2:
                        yield
                CP("pool", W2b[:], stg[si2][:, 0, 0:256].rearrange("p (a b) -> p a b", b=128))
                yield

            def gla_wprep_steps(gp):
                si = STG()
                k.dma(stg[si][:, :, 0:128], w_in[:, 1536 + gp * 128:1536 + (gp + 1) * 128].rearrange("(k p) n -> p k n", p=128))
                k.dma(stg[si][:, :, 128:256], w_in[:, 1792 + gp * 128:1792 + (gp + 1) * 128].rearrange("(k p) n -> p k n", p=128))
                MS("pool", stg[si][0:64, 0, 256:384], 0.0)
                for d in range(2):
                    k.dma(stg[si][32 * d:32 * d + 16, 0, 256:384], gk2[d][:, gp * 128:(gp + 1) * 128])
                si2 = STG()
                k.dma(stg[si2][:, :, 0:256], w_in[:, 2048 + gp * 256:2048 + (gp + 1) * 256].rearrange("(k p) n -> p k n", p=128))
                k.dma(stg[si2][:, :, 256:512], w_in[:, 2560 + gp * 256:2560 + (gp + 1) * 256].rearrange("(k p) n -> p k n", p=128))
                yield
                for kc in range(8):
                    CP("act" if kc % 2 else "dve", Whp[:, kc, gp, 0:256], stg[si][:, kc, 0:256])
                    CP("dve" if kc % 2 else "act", wbf[gp][:, kc, :], stg[si2][:, kc, :])
                    if kc % 2:
                        yield
                CP("pool", GK2b[:, gp, :], stg[si][0:64, 0, 256:384])
                yield

            for hp in range(4):
                MARK(P["name"] + ":rwkv%d" % hp)
                if hp == 0:
                    for _ in wprep_steps(0):
                        pass
                kkv = vps[:, V_KK + hp:V_KK + hp + 1]
                kav = vps[:, V_KA + hp:V_KA + hp + 1]
                rkv_ = vps[:, V_RK + hp:V_RK + hp + 1]
                CK("M3a")

                def proj_steps(sc):
                    t0 = 2 * sc
                    hs = lambda kc: hT[:, kc, pc[t0]:pc[t0] + 256]
                    ds = lambda kc: dhT[:, kc, 128 * t0:128 * t0 + 256]
                    dst = [T["r"], T["k"], vb[:]]
                    for j in range(3):
                        if j == 0 and not own:
                            continue
                        pp = PF()
                        for kc in range(8):
                            MM(pp[:, 0:256], Whp[:, kc, 0, j * 128:(j + 1) * 128], hs(kc), start=(kc == 0), stop=False)
                        for kc in range(8):
                            MM(pp[:, 0:256], Whp[:, kc, 1, j * 128:(j + 1) * 128], ds(kc), start=False, stop=(kc == 7))
                        CP("act", dst[j], pp[:, 0:256])
                        yield
                    TS("dve", T["kq"], T["k"], kkv, None, ALU.mult)
                    ACT(sqb[:, 0, 0:256], T["kq"], AF.Square)
                    pp = PF()
                    MM(pp[:, 0:256], blk_b[:], sqb[:, 0, 0:256])
                    TS("dve", T["x1"], pp[:, 0:256], 1e-12, None, ALU.max)
                    yield
                    RSQ(T["x1"], T["x1"], 1.0, epsv[:, 3:4])
                    TT("dve", T["kk"], T["kq"], T["x1"], ALU.mult)
                    pt = PB()
                    for i in range(2):
                        TR(pt[:, i * 128:(i + 1) * 128], vb[:, i * 128:(i + 1) * 128], ident_b[:])
                    CP("dve", vtok[:, t0:t0 + 2, 0:128], pt[:, 0:256].rearrange("p (a b) -> p a b", b=128))
                    yield

                def prep_steps(sc, d, par):
                    t0 = 2 * sc
                    cs = slice(128 * t0, 128 * t0 + 256)
                    ar, bk, tk = SETSF[par]
                    pz = PF()
                    MM(pz[:, 0:256], W2b[64 * d:64 * d + 64, 0, :], tanhT[64 * d:64 * d + 64, cs])
                    MM(pz[:, 256:512], W2b[64 * d:64 * d + 64, 1, :], a1T[64 * d:64 * d + 64, cs])
                    ACT(T["sig"], pz[:, 0:256], AF.Sigmoid, bias=vps[:, V_W0 + d * 4 + hp:V_W0 + d * 4 + hp + 1])
                    ACT(T["a"], pz[:, 256:512], AF.Sigmoid, bias=vps[:, V_A0 + d * 4 + hp:V_A0 + d * 4 + hp + 1])
                    yield
                    kd = T["kd0"] if d == 0 else T["kd"]
                    TS("pool", T["x2"], T["a"], kav, omka[:, hp:hp + 1], ALU.mult, ALU.add)
                    TT("pool", kd, T["x2"], T["k"], ALU.mult)
                    TT("pool", T["be"], T["kk"], T["a"], ALU.mult)
                    if d == 0:
                        SCAN(T["S"], scm[:, 0:256], T["sig"])
                    else:
                        SCAN(rev(T["S"]), rev(scm[:, 1:257]), rev(T["sig"]))
                    TT("dve", T["D"], T["S"], T["sig"], ALU.subtract)
                    yield
                    ACT(T["E1"], T["S"], AF.Exp, scale=-CW)
                    ACT(T["E2"], T["S"], AF.Exp, scale=CW)
                    ACT(T["E3"], T["D"], AF.Exp, scale=-CW)
                    yield
                    v3 = lambda t: t.rearrange("p (a b) -> p a b", b=128)
                    STT("dve", ar[:, :, 0, :], v3(T["kk"]), -1.0, v3(T["E3"]), ALU.mult, ALU.mult)
                    TT("dve", ar[:, :, 1, :], v3(T["r"]), v3(T["E1"]), ALU.mult)
                    TT("dve", bk[:, :, 0, :], v3(T["be"]), v3(T["E2"]), ALU.mult)
                    TT("dve", bk[:, :, 1, :], v3(kd), v3(T["E2"]), ALU.mult)
                    gcol = 127 if d == 0 else 0
                    CP("pool", gam[:, d, t0:t0 + 2], v3(T["E1"])[:, :, gcol])
                    yield
                    if d == 1 and 0 in dirs:
                        for i in range(2):
                            if (t0 + i) in own:
                                oi = own.index(t0 + i)
                                TT("pool", T["x2"][:, 0:128], T["kd0"][:, i * 128:(i + 1) * 128], T["kd"][:, i * 128:(i + 1) * 128], ALU.add)
                                STT("pool", prodb[:, oi, :], T["x2"][:, 0:128], rkv_, T["r"][:, i * 128:(i + 1) * 128], ALU.mult, ALU.mult)
                    for i in range(2):
                        pt = PB()
                        TR(pt[:, 0:128], ar[:, i, 0, :], ident_b[:])
                        TR(pt[:, 128:256], bk[:, i, 0, :], ident_b[:])
                        TR(pt[:, 256:384], bk[:, i, 1, :], ident_b[:])
                        CP("act", tk[:, i, :, :], pt[:, 0:384].rearrange("p (a b) -> p a b", b=128))
                        yield

                def chunk_steps(group):
                    jobs = []
                    for gi_, (sc_, d_, si_) in enumerate(group):
                        for i in range(2):
                            jb = JB[2 * gi_ + i]
                            st_ = SETSF[si_]
                            jobs.append(dict(i=i, d=d_, tile=2 * sc_ + i, cd=d_ * 8 + 2 * sc_ + i, ar=st_[0], bk=st_[1], tk=st_[2],
                                             ev=("dve" if (2 * gi_ + i) == 2 * len(group) - 1 else "act"), **jb))
                    for J in jobs:
                        i, d, ar, bk, tk = J["i"], J["d"], J["ar"], J["bk"], J["tk"]
                        J["bE"] = [PF(), PF()]
                        J["bQ"] = [PF(), PF()]
                        for e in range(2):
                            hsl = slice(64 * e, 64 * e + 64)
                            arf = ar[hsl, i, :, :].rearrange("p a b -> p (a b)")
                            MM(J["bE"][e][:, 0:256], bk[hsl, i, 0, :], arf)
                            MM(J["bE"][e][:, 256:512], bk[hsl, i, 1, :], arf)
                            MM(J["bQ"][e][:, 0:128], ar[hsl, i, 0, :], bk[hsl, i, 0, :])
                        xk = J["xk"]
                        for e in range(2):
                            TT("dve", xk[:, e, :, :], J["bE"][e][:, 0:512].rearrange("p (a b) -> p a b", b=256), bcm(mask2[:, d, :], 2), ALU.mult)
                            TT("dve", J["qp"][0][:, e, 0, :], J["bQ"][e][:, 0:128], maskQ[:, d, :], ALU.mult)
                        yield
                    for J in jobs:
                        xk = J["xk"]
                        J["rr"] = J["rb"][0]
                        TT("pool", J["rr"][:], xk[:, :, 0, 0:128], bcm(ident_b[:], 2), ALU.add)
                        J["Pm"] = [xk[:, e, 0, 0:128] for e in range(2)]
                        J["Qm"] = [J["qp"][0][:, e, 0, :] for e in range(2)]
                    for lv in range(1, 7):
                        for J in jobs:
                            pq = PF()
                            J["pq"] = pq
                            for e in range(2):
                                MM(pq[:, e * 256:e * 256 + 128], J["Pm"][e], J["Qm"][e])
                                if lv < 6:
                                    MM(pq[:, e * 256 + 128:e * 256 + 256], J["Qm"][e], J["Pm"][e])
                        for J in jobs:
                            qn = J["qp"][1 + (lv % 2)]
                            pq4 = J["pq"][:, 0:512].rearrange("p (a b c) -> p a b c", b=2, c=128)
                            ee_ = J["ev"]
                            if lv < 6:
                                CP(ee_, qn[:], pq4)
                            else:
                                CP(ee_, qn[:, :, 0, :], pq4[:, :, 0, :])
                            J["Qm"] = [qn[:, e, 0, :] for e in range(2)]
                            J["Pm"] = [qn[:, e, 1, :] for e in range(2)]
                        yield
                        for J in jobs:
                            prr = PF()
                            J["prr"] = prr
                            for e in range(2):
                                MM(prr[:, e * 128:(e + 1) * 128], J["Qm"][e], J["rr"][:, e, :])
                        for J in jobs:
                            rn = J["rb"][lv % 2]
                            TT("dve", rn[:], J["prr"][:, 0:256].rearrange("p (a b) -> p a b", b=128), J["rr"][:], ALU.add)
                            J["rr"] = rn
                        yield
                    for J in jobs:
                        pw = PF()
                        J["pw"] = pw
                        for e in range(2):
                            MM(pw[:, e * 64:(e + 1) * 64], J["xk"][:, e, 1, 0:128], vtok[:, J["tile"], 64 * e:64 * e + 64])
                    for J in jobs:
                        CP(J["ev"], J["zr"][:], J["pw"][:, 0:128].rearrange("p (a b) -> p a b", b=64))
                    yield
                    for J in jobs:
                        pzz = PF()
                        J["pzz"] = pzz
                        for e in range(2):
                            MM(pzz[:, e * 64:(e + 1) * 64], J["rr"][:, e, :], J["tk"][:, J["i"], 0, 64 * e:64 * e + 64])
                            MM(pzz[:, 128 + e * 64:128 + (e + 1) * 64], J["rr"][:, e, :], J["zr"][:, e, :])
                    for J in jobs:
                        CP(J["ev"], J["zs"][:], J["pzz"][:, 0:256].rearrange("p (a b c) -> p a b c", b=2, c=64))
                        J["Atok"] = J["zs"][:, 0, :, :].rearrange("p a b -> p (a b)")
                        J["U0"] = J["zs"][:, 1, :, :].rearrange("p a b -> p (a b)")
                    yield
                    for J in jobs:
                        i, tile, tk = J["i"], J["tile"], J["tk"]
                        pg1 = PF()
                        J["pg1"] = pg1
                        MM(pg1[:, 0:128], J["Atok"], tk[:, i, 1, :], start=True, stop=False)
                        MM(pg1[:, 0:128], ident_b[:], ident_b[:], start=False, stop=True)
                        MM(pg1[:, 128:256], tk[:, i, 1, :], J["U0"], start=True, stop=False)
                        MM(pg1[:, 128:256], tk[:, i, 2, :], vtok[:, tile, 0:128], start=False, stop=True)
                        if tile in own:
                            for e in range(2):
                                MM(pg1[:, 256 + e * 128:256 + (e + 1) * 128], J["Atok"], J["xk"][:, e, 0, 128:256])
                    for J in jobs:
                        i, tile, cd, d, ar = J["i"], J["tile"], J["cd"], J["d"], J["ar"]
                        pg1 = J["pg1"]
                        gsc = gam[:, d, tile:tile + 1]
                        TT("dve", store[:, cd, 0:128], pg1[:, 0:128], blk_b[:], ALU.mult)
                        STT("dve", store[:, cd, 256:384], pg1[:, 128:256], gsc, blk_b[:], ALU.mult, ALU.mult)
                        if tile in own:
                            for e in range(2):
                                hsl = slice(64 * e, 64 * e + 64)
                                TT("dve", store[hsl, cd, 128:256], pg1[hsl, 256 + e * 128:256 + (e + 1) * 128], ar[hsl, i, 1, :], ALU.add)
                    yield
                    for J in jobs:
                        i, tile = J["i"], J["tile"]
                        if tile in own:
                            py = PF()
                            J["py"] = py
                            for e in range(2):
                                MM(py[:, e * 64:(e + 1) * 64], J["xk"][:, e, 0, 128:256], J["zs"][:, 1, e, :], start=True, stop=False)
                                MM(py[:, e * 64:(e + 1) * 64], J["xk"][:, e, 1, 128:256], vtok[:, tile, 64 * e:64 * e + 64], start=False, stop=True)
                    for J in jobs:
                        tile = J["tile"]
                        if tile in own:
                            oi = own.index(tile)
                            if J["d"] == dirs[0]:
                                CP("act", yacc[:, oi, 0:128], J["py"][:, 0:128])
                            else:
                                TT("dve", yacc[:, oi, 0:128], J["py"][:, 0:128], yacc[:, oi, 0:128], ALU.add)
                    yield

                import itertools
                units = [(sc, d) for sc in range(nt // 2) for d in dirs]
                groups = [[(sc, d, (2 * g_ + j_) % 4) for j_, (sc, d) in enumerate(units[2 * g_:2 * g_ + 2])] for g_ in range((len(units) + 1) // 2)]

                def P_of(group):
                    its = []
                    seen = set()
                    for (sc, d, si) in group:
                        if d == dirs[0] and sc not in seen:
                            its.append(proj_steps(sc))
                            seen.add(sc)
                        its.append(prep_steps(sc, d, si))
                    return itertools.chain(*its)

                for _ in P_of(groups[0]):
                    pass
                for gi, group in enumerate(groups):
                    C = chunk_steps(group)
                    if gi + 1 < len(groups):
                        Pn = P_of(groups[gi + 1])
                    else:
                        Pn = wprep_steps(hp + 1) if hp < 3 else iter(())
                    ca = pa_ = True
                    while ca or pa_:
                        if ca:
                            try:
                                next(C)
                            except StopIteration:
                                ca = False
                        if pa_:
                            try:
                                next(Pn)
                            except StopIteration:
                                pa_ = False
                MARK(P["name"] + ":rseq%d" % hp)
                CK("M8")
                for d in dirs:
                    for ci, chain in enumerate(P["chains"]):
                        init_state(P["init"][d], d, 128, [st_r[d, 2 * hp + e] for e in range(2)], Tmid_r[:, hp, :])
                        order = chain if d == 0 else chain[::-1]
                        pend = None
                        for tile in order:
                            cd = d * 8 + tile
                            ptt = PF()
                            MM(ptt[:, 0:128], store[:, cd, 0:128], Tb[:, 0:128])
                            this = None
                            if tile in own:
                                MM(ptt[:, 128:256], store[:, cd, 128:256], Tb[:, 0:128])
                                this = (ptt, own.index(tile))
                            STT("dve", Tf[:, 0:128], ptt[:, 0:128], gam[:, d, tile:tile + 1], store[:, cd, 256:384], ALU.mult, ALU.add)
                            CP("act", Tb[:, 0:128], Tf[:, 0:128])
                            if pend is not None:
                                TT("dve", yacc[:, pend[1], 0:128], pend[0][:, 128:256], yacc[:, pend[1], 0:128], ALU.add)
                            pend = this
                        if pend is not None:
                            TT("dve", yacc[:, pend[1], 0:128], pend[0][:, 128:256], yacc[:, pend[1], 0:128], ALU.add)
                        if P["end"][d] == "out":
                            for e in range(2):
                                k.dma(ns_r[ci, d, 2 * hp + e], Tf[64 * e:64 * e + 64, 64 * e:64 * e + 64], is_output=True)
                        elif P["end"][d] == "mid":
                            CP("pool", Tmid_r[:, hp, :], Tf[:, 0:128])
                MARK(P["name"] + ":rfin%d" % hp)
                k.dma(rows[:, 384:512], rp[:, R_LNG + hp * 128:R_LNG + (hp + 1) * 128].partition_broadcast(128))
                k.dma(rows[:, 512:640], rp[:, R_LNB + hp * 128:R_LNB + (hp + 1) * 128].partition_broadcast(128))
                CK("M9")
                n_ = len(own)
                if n_:
                    assert own == list(range(n_))
                    yv = yacc[:, 0:n_, 0:128]
                    y4 = yv.rearrange("p n (a b) -> p n a b", b=64)
                    big1 = arena[:, 0:n_ * 128].rearrange("p (n c) -> p n c", c=128)
                    big2 = arena[:, 1024:1024 + n_ * 128].rearrange("p (n c) -> p n c", c=128)
                    b14 = big1.rearrange("p n (a b) -> p n a b", b=64)
                    b24 = big2.rearrange("p n (a b) -> p n a b", b=64)
                    st = lambda j: arena[:, 2048 + 16 * j:2048 + 16 * j + 2 * n_]
                    st3 = lambda j: st(j).rearrange("p (n a) -> p n a", a=2)
                    RSUM("dve", st3(0), y4)
                    ACT(big1, yv, AF.Square)
                    RSUM("dve", st3(1), b14)
                    TS("dve", st(2), st(0), 1.0 / 64, None, ALU.mult)
                    TT("dve", st(3), st(2), st(2), ALU.mult)
                    STT("dve", st(4), st(1), 1.0 / 64, st(3), ALU.mult, ALU.subtract)
                    RSQ(st(4), st(4), 1.0, epsv[:, 1:2])
                    TT("dve", b14, y4, bc(st3(2), 64), ALU.subtract)
                    TT("dve", b14, b14, bc(st3(4), 64), ALU.mult)
                    TT("dve", big1, big1, bcm(rows[:, 384:512], n_), ALU.mult)
                    TT("dve", big1, big1, bcm(rows[:, 512:640], n_), ALU.add)
                    pbn = PF()
                    for oi in range(n_):
                        MM(pbn[:, 2 * oi:2 * oi + 2], prodb[:, oi, :], blkind_b[:])
                    CP("act", st(5), pbn[:, 0:2 * n_])
                    TT("dve", b24, vtok[:, 0:n_, 0:128].rearrange("p n (a b) -> p n a b", b=64), bc(st3(5), 64), ALU.mult)
                    TT("dve", big1, big1, big2, ALU.add)
                    for g_ in range(n_ // 4):
                        pgt = PF()
                        for j in range(4):
                            tile = 4 * g_ + j
                            MM(pgt[:, j * 128:(j + 1) * 128], sigT[:, 128 * tile:128 * tile + 128], G2b[:, hp * 128:(hp + 1) * 128])
                        TT("dve", mixtok[:, g0 + 4 * g_:g0 + 4 * g_ + 4, hp * 128:(hp + 1) * 128], big1[:, 4 * g_:4 * g_ + 4, :],
                           pgt[:, 0:512].rearrange("p (n c) -> p n c", c=128), ALU.mult)

            for _ in gla_wprep_steps(0):
                pass
            CK("M10")
            for gp in range(2):
                MARK(P["name"] + ":gla%d" % gp)
                wq = Whp[:, :, gp, :]
                wv = wbf[gp]
                gkw = GK2b[:, gp, :]
                def gproj_steps(sc):
                    t0 = 2 * sc
                    hs = lambda kc: hT[:, kc, pc[t0]:pc[t0] + 256]
                    pq_ = PF()
                    if own:
                        for kc in range(8):
                            MM(pq_[:, 0:256], wq[:, kc, 0:128], hs(kc), start=(kc == 0), stop=(kc == 7))
                    for kc in range(8):
                        MM(pq_[:, 256:512], wq[:, kc, 128:256], hs(kc), start=(kc == 0), stop=(kc == 7))
                    if own:
                        k.op("act", lambda e, o=T["r"], a=pq_[:, 0:256]: e.mul(o, a, 0.125), reads=[pq_[:, 0:256]], writes=[T["r"]])
                    CP("act", T["k"], pq_[:, 256:512])
                    yield
                    for i in range(2):
                        tile = t0 + i
                        pv = PF()
                        for kc in range(8):
                            MM(pv[:, 0:256], hT[:, kc, pc[tile]:pc[tile] + 128], wv[:, kc, 0:256], start=(kc == 0), stop=(kc == 7))
                        CP("act", vtok[:, tile, :], pv[:, 0:256])
                        yield

                def gprep_steps(sc, d, par):
                    t0 = 2 * sc
                    cs = slice(128 * t0, 128 * t0 + 256)
                    ar, bk, tk = G4[par]
                    Tn = GT_[dirs.index(d)]
                    pz = PF()
                    MM(pz[:, 0:256], gkw[32 * d:32 * d + 16, :], gk1T[32 * d:32 * d + 16, cs])
                    ACT(Tn["sig"], pz[:, 0:256], AF.Sigmoid, bias=vps[:, V_GKB + d * 2 + gp:V_GKB + d * 2 + gp + 1])
                    ACT(Tn["a"], Tn["sig"], AF.Ln)
                    yield
                    if d == 0:
                        SCAN(Tn["S"], scm[:, 0:256], Tn["a"])
                    else:
                        SCAN(rev(Tn["S"]), rev(scm[:, 1:257]), rev(Tn["a"]))
                    yield
                    ACT(Tn["E1"], Tn["S"], AF.Exp, scale=1.0 / 16)
                    ACT(Tn["E2"], Tn["S"], AF.Exp, scale=-1.0 / 16)
                    yield
                    v3 = lambda t: t.rearrange("p (a b) -> p a b", b=128)
                    TT("dve", ar[:, :, 0, :], v3(T["r"]), v3(Tn["E1"]), ALU.mult)
                    TT("dve", bk[:, :, 0, :], v3(T["k"]), v3(Tn["E2"]), ALU.mult)
                    gcol = 127 if d == 0 else 0
                    CP("pool", gam[:, d, t0:t0 + 2], v3(Tn["E1"])[:, :, gcol])
                    yield
                    for i in range(2):
                        pt = PB()
                        TR(pt[:, 0:128], bk[:, i, 0, :], ident_b[:])
                        CP("act", tk[:, i, 0, :], pt[:, 0:128])
                        yield

                def gchunk_steps(sc, d, par):
                    t0 = 2 * sc
                    ar, bk, tk = G4[par]
                    gj = [dict(i=i, tile=t0 + i, cd=d * 8 + t0 + i, at=XK[i]) for i in range(2)]
                    for J in gj:
                        ph = PF()
                        J["ph"] = ph
                        MM(ph[:, 0:256], tk[:, J["i"], 0, :], vtok[:, J["tile"], :])
                        if J["tile"] in own:
                            J["pa"] = [PF(), PF()]
                            for e in range(2):
                                hsl = slice(64 * e, 64 * e + 64)
                                MM(J["pa"][e][:, 0:128], bk[hsl, J["i"], 0, :], ar[hsl, J["i"], 0, :])
                        STT("dve", store[:, J["cd"], 128:384], ph[:, 0:256], gam[:, d, J["tile"]:J["tile"] + 1], blk256_f[:], ALU.mult, ALU.mult)
                        if J["tile"] in own:
                            for e in range(2):
                                TT("dve", J["at"][:, e, 0, 0:128], J["pa"][e][:, 0:128], maskI[:, d, :], ALU.mult)
                            CP("pool", store[:, J["cd"], 0:128], ar[:, J["i"], 0, :])
                        yield
                    for J in gj:
                        if J["tile"] in own:
                            phy = PF()
                            J["phy"] = phy
                            for e in range(2):
                                MM(phy[:, e * 128:(e + 1) * 128], J["at"][:, e, 0, 0:128], vtok[:, J["tile"], 128 * e:128 * e + 128])
                    for J in gj:
                        if J["tile"] in own:
                            oi = own.index(J["tile"])
                            if d == dirs[0]:
                                CP("act", yacc[:, oi, :], J["phy"][:, 0:256])
                            else:
                                TT("dve", yacc[:, oi, :], J["phy"][:, 0:256], yacc[:, oi, :], ALU.add)
                    yield

                G4 = [(AR[0][:], BK[0][:], toks[0][:]), (AR[1][:], BK[1][:], toks[1][:]),
                      (QP[0][:], QP[2][:], RR[0][:].rearrange("p i (j c) -> p i j c", j=1)),
                      (QP[1][:], QP[3][:], RR[1][:].rearrange("p i (j c) -> p i j c", j=1))]
                GT_ = [dict(sig=T["sig"], a=T["a"], S=T["S"], E1=T["E1"], E2=T["E2"]),
                       dict(sig=T["kd"], a=T["kd0"], S=T["be"], E1=T["D"], E2=T["E3"])]

                def rrobin(its):
                    its = list(its)
                    while its:
                        nxt = []
                        for it in its:
                            try:
                                next(it)
                                nxt.append(it)
                                yield
                            except StopIteration:
                                pass
                        its = nxt

                def GP_of(sc):
                    return itertools.chain(gproj_steps(sc), rrobin([gprep_steps(sc, d, (2 * sc + di) % 4) for di, d in enumerate(dirs)]))

                def GC_of(sc):
                    return itertools.chain(*[gchunk_steps(sc, d, (2 * sc + di) % 4) for di, d in enumerate(dirs)])

                for _ in GP_of(0):
                    pass
                for sc in range(nt // 2):
                    C = GC_of(sc)
                    if sc + 1 < nt // 2:
                        Pn = GP_of(sc + 1)
                    else:
                        Pn = gla_wprep_steps(1) if gp == 0 else iter(())
                    ca = pa_ = True
                    while ca or pa_:
                        if ca:
                            try:
                                next(C)
                            except StopIteration:
                                ca = False
                        if pa_:
                            try:
                                next(Pn)
                            except StopIteration:
                                pa_ = False
                for d in dirs:
                    for ci, chain in enumerate(P["chains"]):
                        init_state(P["init"][d], d, 256, [st_g[d, 2 * gp + e] for e in range(2)], Tmid_g[:, gp, :])
                        order = chain if d == 0 else chain[::-1]
                        Tfa = [Tf, T["x1"]]
                        Tbs = [vb[:], Tb[:]]
                        cur = 0
                        pend = None
                        for ci_, tile in enumerate(order):
                            cd = d * 8 + tile
                            this = None
                            if tile in own:
                                ptt = PF()
                                MM(ptt[:, 0:256], store[:, cd, 0:128], Tbs[(ci_ - 1) % 2])
                                this = (ptt, own.index(tile))
                            STT("dve", Tfa[1 - cur], Tfa[cur], gam[:, d, tile:tile + 1], store[:, cd, 128:384], ALU.mult, ALU.add)
                            CP("act", Tbs[ci_ % 2], Tfa[1 - cur])
                            cur ^= 1
                            if pend is not None:
                                TT("dve", yacc[:, pend[1], :], pend[0][:, 0:256], yacc[:, pend[1], :], ALU.add)
                            pend = this
                        if pend is not None:
                            TT("dve", yacc[:, pend[1], :], pend[0][:, 0:256], yacc[:, pend[1], :], ALU.add)
                        Tfin = Tfa[cur]
                        if P["end"][d] == "out":
                            for e in range(2):
                                k.dma(ns_g[ci, d, 2 * gp + e], Tfin[64 * e:64 * e + 64, 128 * e:128 * e + 128], is_output=True)
                        elif P["end"][d] == "mid":
                            CP("pool", Tmid_g[:, gp, :], Tfin)
                n_ = len(own)
                if n_:
                    gr = rows[:, 640:768]
                    gbc = bass.AP(gr.tensor, gr.offset, [gr.ap[0], (0, n_), (0, 2), (1, 128)])
                    ov = yacc[:, 0:n_, :]
                    o4 = ov.rearrange("p n (a b) -> p n a b", b=128)
                    big1 = arena[:, 0:n_ * 256].rearrange("p (n c) -> p n c", c=256)
                    big2 = arena[:, 2048:2048 + n_ * 256].rearrange("p (n c) -> p n c", c=256)
                    b14 = big1.rearrange("p n (a b) -> p n a b", b=128)
                    for pr_ in range(n_ // 2):
                        pgg = PF()
                        for j in range(2):
                            tile = 2 * pr_ + j
                            for kc in range(8):
                                MM(pgg[:, j * 256:(j + 1) * 256], hT[:, kc, pc[tile]:pc[tile] + 128], wv[:, kc, 256:512], start=(kc == 0), stop=(kc == 7))
                        pv2 = pgg[:, 0:512].rearrange("p (n c) -> p n c", c=256)
                        ACT(big2[:, 2 * pr_:2 * pr_ + 2, :], pv2, AF.Sigmoid)
                        TT("dve", big2[:, 2 * pr_:2 * pr_ + 2, :], big2[:, 2 * pr_:2 * pr_ + 2, :], pv2, ALU.mult)
                    ACT(big1, ov, AF.Square)
                    ms = sm[:, 0:2 * n_]
                    ms3 = ms.rearrange("p (n a) -> p n a", a=2)
                    RSUM("dve", ms3, b14)
                    RSQ(ms, ms, 1.0 / 128, epsv[:, 2:3])
                    TT("dve", b14, o4, bc(ms3, 128), ALU.mult)
                    TT("dve", b14, b14, gbc, ALU.mult)
                    TT("dve", mixtok[:, g0:g0 + n_, 512 + gp * 256:512 + (gp + 1) * 256], big1, big2, ALU.mult)

        PP = dict(name="PP", nt=4, pc=[64, 192, 384, 512], mv=0, x0=0, own=[0, 1, 2, 3], g0=0, kind="seq", dirs=[0, 1],
                  pads=[(0, 64), (320, 384), (640, 704)], groups=[(0, 2), (2, 2)],
                  ngroups=[(64, 0, 256), (384, 256, 256)], chains=[[0, 1], [2, 3]],
                  init={0: "zero", 1: "zero"}, end={0: "out", 1: "out"})
        PSO = dict(name="PSO", nt=8, pc=[64 + 128 * i for i in range(8)], mv=1, x0=1536, own=[], g0=0, kind="grid", dirs=[1],
                   pads=[(1088, 1152)], groups=[(0, 4), (4, 4)],
                   ngroups=[(0, 1472, 64), (64, 1536, 512), (576, 2048, 512)], chains=[list(range(8))],
                   init={1: "dram"}, end={1: "mid"})
        PSW = dict(name="PSW", nt=8, pc=[64 + 128 * i for i in range(8)], mv=1, x0=512, own=list(range(8)), g0=4, kind="grid", dirs=[0, 1],
                   pads=[(0, 64)], groups=[(0, 4), (4, 4)],
                   ngroups=[(64, 512, 512), (576, 1024, 512), (1088, 1536, 64)], chains=[list(range(8))],
                   init={0: "dram", 1: "mid"}, end={0: None, 1: None})
        stage = int(_ENVD.get("KSTAGE", "9"))
        if stage >= 1:
            mixer(PP)
        if stage >= 2:
            mixer(PSO)
        if stage >= 3:
            mixer(PSW)

        if dbg:
            dbg_out["mixtok"] = (mixtok, [128, 12, 1024], BF16)

        mixT = dhT
        h2T = hT
        hid = store[:].rearrange("p a b -> p (a b)")[:, 0:3072].rearrange("p (a b) -> p a b", b=768)

        def rms_stats(c0, n):
            for kc in range(8):
                ACT(sqb[:, kc, 0:n], x1[:, kc, c0:c0 + n], AF.Square)
            pss = PF()
            for kc in range(8):
                MM(pss[:, 0:n], ones_b[:], sqb[:, kc, 0:n], start=(kc == 0), stop=(kc == 7))
            RSQ(rstd[:, 0:n], pss[:, 0:n], 1.0 / 1024, epsv[:, 0:1])

        def tr_steps(half_):
            for gi in range(6):
                go = 6 * half_ + gi
                for hh in range(2):
                    pt = PB()
                    for j in range(4):
                        kc = hh * 4 + j
                        TR(pt[:, j * 128:(j + 1) * 128], mixtok[:, go, kc * 128:(kc + 1) * 128], ident_b[:])
                    CP(ALT("dve", "act"), mixT[:, hh * 4:hh * 4 + 4, gi * 128:(gi + 1) * 128],
                       pt[:, 0:512].rearrange("p (a b) -> p a b", b=128))
                    yield

        for half in range(2 if stage >= 4 else 0):
            MARK("post%d" % half)
            tb = 768 * half
            grp = [(0, 512, 0 if half == 0 else 1), (512, 256, 1)]
            if half == 0:
                for _ in tr_steps(0):
                    pass
                tr_next = None
            else:
                for _ in tr_next:
                    pass
            k.dma(x1[:, :, 0:768], xT[:, :, tb:tb + 768].rearrange("k p t -> p k t"))
            for cb in range(2):
                si = STG()
                k.dma(stg[si][:], w_out[:, cb * 512:(cb + 1) * 512].rearrange("(k p) n -> p k n", p=128))
                for kc in range(8):
                    CP(ALT("dve", "act"), wbf[si][:, kc, :], stg[si][:, kc, :])
                for cc in range(4):
                    oc = cb * 4 + cc
                    for (c0, n, mv) in grp:
                        pp = PF()
                        for kc in range(8):
                            MM(pp[:, 0:n], wbf[si][:, kc, cc * 128:(cc + 1) * 128], mixT[:, kc, c0:c0 + n],
                               start=(kc == 0), stop=(kc == 7))
                        xs_ = x1[:, oc, c0:c0 + n]
                        STT("dve", xs_, pp[:, 0:n], mod[:, GT1 + oc, mv:mv + 1], xs_, ALU.mult, ALU.add)
            for (c0, n, mv) in grp:
                rms_stats(c0, n)
                for kc in range(8):
                    TT("dve", ntmp[:, 0:n], x1[:, kc, c0:c0 + n], rstd[:, 0:n], ALU.mult)
                    ACT(h2T[:, kc, c0:c0 + n], ntmp[:, 0:n], AF.Identity,
                        bias=mod[:, SH2 + kc, mv:mv + 1], scale=A2[:, kc, mv:mv + 1])
            MARK("mlp%d" % half)
            if half == 0:
                tr_next = tr_steps(1)
            for hb in range(8):
                if half == 0:
                    for _r in range(2):
                        try:
                            next(tr_next)
                        except StopIteration:
                            pass
                si = STG()
                k.dma(stg[si][:], m1[:, hb * 512:(hb + 1) * 512].rearrange("(k p) n -> p k n", p=128))
                for kc in range(8):
                    CP(ALT("dve", "act"), wbf[si][:, kc, :], stg[si][:, kc, :])
                for cc in range(4):
                    for (c0, n, mv) in grp:
                        pp = PF()
                        for kc in range(8):
                            MM(pp[:, 0:n], wbf[si][:, kc, cc * 128:(cc + 1) * 128], h2T[:, kc, c0:c0 + n],
                               start=(kc == 0), stop=(kc == 7))
                        ACT(ntmp[:, 0:n], pp[:, 0:n], AF.Relu)
                        TT("dve", hid[:, cc, c0:c0 + n], ntmp[:, 0:n], ntmp[:, 0:n], ALU.mult)
                si2 = STG()
                s2v = stg[si2][:].rearrange("p a b -> p (a b)").rearrange("p (a b) -> p a b", b=1024)
                w2v = wbf[si2][:].rearrange("p a b -> p (a b)").rearrange("p (a b) -> p a b", b=1024)
                k.dma(s2v, m2[hb * 512:(hb + 1) * 512, :].rearrange("(k p) n -> p k n", p=128))
                for kc in range(4):
                    CP(ALT("dve", "act"), w2v[:, kc, :], s2v[:, kc, :])
                for oc in range(8):
                    for (c0, n, mv) in grp:
                        pp = PF()
                        for kc in range(4):
                            MM(pp[:, 0:n], w2v[:, kc, oc * 128:(oc + 1) * 128], hid[:, kc, c0:c0 + n],
                               start=(kc == 0), stop=(kc == 3))
                        xs_ = x1[:, oc, c0:c0 + n]
                        STT("dve", xs_, pp[:, 0:n], mod[:, GT2 + oc, mv:mv + 1], xs_, ALU.mult, ALU.add)
            for (c0, n, mv) in grp:
                rms_stats(c0, n)
                for kc in range(8):
                    xs_ = x1[:, kc, c0:c0 + n]
                    STT(ALT(), xs_, xs_, vps[:, V_FNG + kc:V_FNG + kc + 1], rstd[:, 0:n], ALU.mult, ALU.mult)
            k.dma(yT[:, :, tb:tb + 768].rearrange("k p t -> p k t"), x1[:, :, 0:768], is_output=True)

    try:
        body()
    except _Stop:
        pass
    if dbg:
        for name, (t, shp, dt) in dbg_out.items():
            o = nc.dram_tensor("dbg_" + name, shp, dt, kind="ExternalOutput").ap()
            k.dma(o, t[:], is_output=True)
    MARK("end")
    globals()["_LASTK"] = k
    stats = k.emit()
    return nc, stats


def _lay_kc(v):
    return np.ascontiguousarray(v.reshape(8, 128).T)


def _prep_core(c, I):
    f = c % 2
    b = c // 2
    fl = (lambda a: a[::-1]) if f else (lambda a: a)
    xs = [fl(I["x_prompt"][2 * c]), fl(I["x_prompt"][2 * c + 1]), fl(I["x_sample"][b])]
    x = np.concatenate(xs, axis=0)
    xT = np.ascontiguousarray(x.T).reshape(8, 128, 2560)
    cond = np.stack([_lay_kc(I["c_ctx"]), _lay_kc(I["c"][b])], axis=-1)
    dsel = [1, 0] if f else [0, 1]
    vp = np.zeros((128, NV), np.float32)
    vp[:, V_N1G:V_N1G + 8] = _lay_kc(I["norm1_g"][0])
    vp[:, V_N2G:V_N2G + 8] = _lay_kc(I["norm2_g"][0])
    vp[:, V_FNG:V_FNG + 8] = _lay_kc(I["final_norm_g"])
    vp[:, V_MUW:V_MUW + 8] = _lay_kc(I["rwkv_mu_wag"][0, 0])
    vp[:, V_MUA:V_MUA + 8] = _lay_kc(I["rwkv_mu_wag"][0, 1])
    vp[:, V_MUG:V_MUG + 8] = _lay_kc(I["rwkv_mu_wag"][0, 2])
    for d in range(2):
        vp[:, V_W0 + 4 * d:V_W0 + 4 * d + 4] = I["rwkv_w0"][0, dsel[d]].reshape(4, 128).T
        vp[:, V_A0 + 4 * d:V_A0 + 4 * d + 4] = I["rwkv_a0"][0, dsel[d]].reshape(4, 128).T
        vp[:, V_GKB + 2 * d:V_GKB + 2 * d + 2] = I["gla_gk_b"][0, dsel[d]].reshape(2, 128).T
    vp[:, V_KK:V_KK + 4] = I["rwkv_k_k"][0].reshape(4, 128).T
    vp[:, V_KA:V_KA + 4] = I["rwkv_k_a"][0].reshape(4, 128).T
    vp[:, V_RK:V_RK + 4] = I["rwkv_r_k"][0].reshape(512).reshape(4, 128).T
    vp[:, V_ADB:V_ADB + 48] = I["ada_b"][0].reshape(48, 128).T
    rp = np.zeros((1, NR), np.float32)
    rp[0, R_MU:R_MU + 1536] = I["rwkv_mu_rkv"][0]
    rp[0, R_LNG:R_LNG + 512] = I["rwkv_lnx_g"][0]
    rp[0, R_LNB:R_LNB + 512] = I["rwkv_lnx_b"][0]
    rp[0, R_GNG:R_GNG + 512] = np.tile(I["gla_norm_g"][0], 4)
    sr = [I["state_rwkv_fwd"][b, 0], I["state_rwkv_bwd"][b, 0]]
    sg = [I["state_gla_fwd"][b, 0], I["state_gla_bwd"][b, 0]]
    st_r = np.stack([np.swapaxes(sr[dsel[d]], -1, -2) for d in range(2)])
    st_g = np.stack([sg[dsel[d]] for d in range(2)])
    A = np.ascontiguousarray
    return {
        "xT": A(xT), "cond": A(cond.astype(np.float32)), "ada_w": A(I["ada_w"][0]), "vp": vp, "rp": rp,
        "w_in": A(I["w_in"][0]),
        "w1": A(I["rwkv_w1"][0][dsel]), "w2": A(I["rwkv_w2"][0][dsel]),
        "a1": A(I["rwkv_a1"][0][dsel]), "a2": A(I["rwkv_a2"][0][dsel]),
        "g1": A(I["rwkv_g1"][0]), "g2": A(I["rwkv_g2"][0]),
        "gk1": A(I["gla_gk1"][0][dsel]), "gk2": A(I["gla_gk2"][0][dsel]),
        "w_out": A(I["w_out"][0]), "m1": A(I["mlp_w1"][0]), "m2": A(I["mlp_w2"][0]),
        "st_r": A(st_r), "st_g": A(st_g),
    }


_CACHE = {}


def kernel(**inputs):
    I = {k_: np.asarray(v) for k_, v in inputs.items()}
    if "nc" not in _CACHE:
        _CACHE["nc"] = build()[0]
    nc = _CACHE["nc"]
    in_maps = [_prep_core(c, I) for c in range(8)]
    res = run_bass_kernel_spmd(nc, in_maps, core_ids=list(range(8)))
    y_prompt = np.zeros((16, 256, 1024), np.float32)
    y_sample = np.zeros((4, 2048, 1024), np.float32)
    nrf = np.zeros((16, 1, 8, 64, 64), np.float32)
    nrb = np.zeros((16, 1, 8, 64, 64), np.float32)
    ngf = np.zeros((16, 1, 4, 64, 128), np.float32)
    ngb = np.zeros((16, 1, 4, 64, 128), np.float32)
    for c in range(8):
        r = res.results[c]
        f = c % 2
        b = c // 2
        y = np.asarray(r["yT"]).reshape(1024, 1536).T
        fl = (lambda a: a[::-1]) if f else (lambda a: a)
        y_prompt[2 * c] = fl(y[0:256])
        y_prompt[2 * c + 1] = fl(y[256:512])
        ys = y[512:1536]
        if f:
            y_sample[b, 1024:2048] = ys[::-1]
        else:
            y_sample[b, 0:1024] = ys
        nsr = np.asarray(r["ns_r"])
        nsg = np.asarray(r["ns_g"])
        for s in range(2):
            for d in range(2):
                gd = d ^ f
                tgt_r = nrf if gd == 0 else nrb
                tgt_g = ngf if gd == 0 else ngb
                tgt_r[2 * c + s, 0] = np.swapaxes(nsr[s, d], -1, -2)
                tgt_g[2 * c + s, 0] = nsg[s, d]
    return (y_prompt, y_sample, nrf, nrb, ngf, ngb)
```
